# Optimizing a Trainium2 kernel written in Bass

```python
import functools
import jax, jax.numpy as jnp
from jax import lax
import numpy as np

D_MODEL = 2048
BATCH = 2
SEQ = 4096
DEPTH = 1
DEC_BATCH = 8
DEC_SEQ = 4
PAST_LEN = 16384
PAGE_SIZE = 128

HEAD_DIM = 64
D_CONV = D_MODEL // 2
N_HEADS = (D_MODEL // 2) // HEAD_DIM
D_ATTN = N_HEADS * HEAD_DIM
D_MIX = D_CONV + D_ATTN
CONV_A_WIDTH = 31
DIL_PATTERNS = ((128, 1), (512, 4), (2048, 16))
WIN_MAX = max(w for w, _ in DIL_PATTERNS)
ROPE_THETA = 10000.0
D_FF = 5632
FFN_CONV_WIDTH = 3
EPS = 1e-6
NEG_INF = -1e30

kernel_name = 'hymba_conformer_dilated_convffn_step'


def rms_norm(x, g):
    xf = x.astype(jnp.float32)
    y = xf * lax.rsqrt(jnp.mean(xf * xf, axis=-1, keepdims=True) + EPS) * g.astype(jnp.float32)
    return y.astype(x.dtype)


def layer_norm(x, g, b):
    xf = x.astype(jnp.float32)
    xc = xf - jnp.mean(xf, axis=-1, keepdims=True)
    y = xc * lax.rsqrt(jnp.mean(xc * xc, axis=-1, keepdims=True) + EPS) * g.astype(jnp.float32) + b.astype(jnp.float32)
    return y.astype(x.dtype)


def rope(x, pos):
    half = HEAD_DIM // 2
    inv = ROPE_THETA ** (-jnp.arange(half, dtype=jnp.float32) / half)
    ang = pos.astype(jnp.float32)[:, None] * inv[None, :]
    cos = jnp.cos(ang)[None, :, None, :]
    sin = jnp.sin(ang)[None, :, None, :]
    xf = x.astype(jnp.float32)
    x1, x2 = xf[..., :half], xf[..., half:]
    return jnp.concatenate([x1 * cos - x2 * sin, x2 * cos + x1 * sin], axis=-1).astype(x.dtype)


def causal_dwconv(prev, x, w, b):
    xx = jnp.concatenate([prev.astype(x.dtype), x], axis=1)
    y = lax.conv_general_dilated(xx, w[:, None, :].astype(x.dtype), window_strides=(1,), padding='VALID',
                                 dimension_numbers=('NWC', 'WIO', 'NWC'), feature_group_count=x.shape[-1])
    return y + b.astype(x.dtype), xx[:, -(w.shape[0] - 1):]


def dilated_band_attention(q, k, v, window, dil):
    B, S, H, Dh = q.shape
    blk = window // dil
    span = blk * dil
    Sp = -(-S // span) * span
    nb = Sp // span

    def to_blocks(a):
        a = jnp.pad(a.astype(jnp.float32), ((0, 0), (0, Sp - S), (0, 0), (0, 0)))
        return a.reshape(B, nb, blk, dil, H, Dh)

    def with_prev(a):
        prev = jnp.pad(a[:, :-1], ((0, 0), (1, 0), (0, 0), (0, 0), (0, 0), (0, 0)))
        return jnp.concatenate([prev, a], axis=2)

    qb = to_blocks(q)
    kc = with_prev(to_blocks(k))
    vc = with_prev(to_blocks(v))
    s = jnp.einsum('bnqrhd,bnkrhd->bnrhqk', qb, kc) * (Dh ** -0.5)
    qi = jnp.arange(blk)[:, None]
    ki = jnp.arange(2 * blk)[None, :]
    band = (ki >= qi) & (ki <= qi + blk)
    has_prev = (jnp.arange(nb) > 0)[:, None, None] | (ki >= blk)[None]
    mask = band[None] & has_prev
    s = jnp.where(mask[None, :, None, None], s, NEG_INF)
    lse = jax.nn.logsumexp(s, axis=-1)
    p = jnp.exp(s - lse[..., None])
    o = jnp.einsum('bnrhqk,bnkrhd->bnqrhd', p, vc).reshape(B, Sp, H, Dh)[:, :S]
    lse = lse.transpose(0, 1, 4, 2, 3).reshape(B, Sp, H)[:, :S]
    return o, lse


def merge_by_denominator(outs, lses):
    w = jax.nn.softmax(jnp.stack(lses), axis=0)
    return jnp.einsum('pblh,pblhd->blhd', w, jnp.stack(outs))


def prompt_attend(q, k, v):
    outs, lses = [], []
    for window, dil in DIL_PATTERNS:
        o, lse = dilated_band_attention(q, k, v, window, dil)
        outs.append(o)
        lses.append(lse)
    return merge_by_denominator(outs, lses).astype(v.dtype)


def sample_attend(q, k, v, k_buf, v_buf):
    kc = jnp.concatenate([k_buf, k.astype(k_buf.dtype)], axis=1).astype(jnp.float32)
    vc = jnp.concatenate([v_buf, v.astype(v_buf.dtype)], axis=1).astype(jnp.float32)
    wb = k_buf.shape[1]
    T = q.shape[1]
    qf = q.astype(jnp.float32)
    outs, lses = [], []
    for window, dil in DIL_PATTERNS:
        nk = window // dil + 1
        idx = wb + jnp.arange(T)[:, None] - dil * jnp.arange(nk)[None, :]
        valid = idx >= 0
        idx = jnp.maximum(idx, 0)
        kg = jnp.take(kc, idx, axis=1)
        vg = jnp.take(vc, idx, axis=1)
        s = jnp.einsum('bthd,btjhd->bthj', qf, kg) * (HEAD_DIM ** -0.5)
        s = jnp.where(valid[None, :, None, :], s, NEG_INF)
        lse = jax.nn.logsumexp(s, axis=-1)
        p = jnp.exp(s - lse[..., None])
        outs.append(jnp.einsum('bthj,btjhd->bthd', p, vg))
        lses.append(lse)
    return merge_by_denominator(outs, lses).astype(v.dtype)


def trunk_layer(x, c, pos, a_prev, f_prev, attend, p):
    mod = (jax.nn.silu(c) @ p['w_ada'] + p['b_ada'])[:, None, :]
    sh1, sc1, g1, sh2, sc2, g2 = jnp.split(mod, 6, axis=-1)
    B, L = x.shape[0], x.shape[1]

    h = rms_norm(x, p['norm_mix_g']) * (1 + sc1) + sh1
    proj = h @ p['w_in']
    a_val, a_gate, q, k, v = jnp.split(
        proj, [D_CONV, 2 * D_CONV, 2 * D_CONV + D_ATTN, 2 * D_CONV + 2 * D_ATTN], axis=-1)

    g = a_val * jax.nn.sigmoid(a_gate)
    gc, a_tail = causal_dwconv(a_prev, g, p['conv_a_w'], p['conv_a_b'])
    a_out = jax.nn.silu(layer_norm(gc, p['ln_a_g'], p['ln_a_b']))

    q = rope(rms_norm(q.reshape(B, L, N_HEADS, HEAD_DIM), p['q_norm_g']), pos)
    k = rope(rms_norm(k.reshape(B, L, N_HEADS, HEAD_DIM), p['k_norm_g']), pos)
    v = v.reshape(B, L, N_HEADS, HEAD_DIM)
    o = attend(q, k, v).reshape(B, L, D_ATTN)

    x = x + g1 * (jnp.concatenate([a_out, o], axis=-1) @ p['w_out'])

    h2 = rms_norm(x, p['norm_ffn_g']) * (1 + sc2) + sh2
    u, f_tail = causal_dwconv(f_prev, h2 @ p['w_up'], p['ffn_conv_w'], p['ffn_conv_b'])
    gate, val = jnp.split(u, 2, axis=-1)
    x = x + g2 * ((jax.nn.silu(gate) * val) @ p['w_down'])
    return x, k, v, a_tail, f_tail


def setup_inputs(seed: int = 0) -> dict:
    key = jax.random.key(seed)
    ks = jax.random.split(key, 26)

    def nrm(k, shape, s):
        return jax.random.normal(k, shape, jnp.float32) * s

    wb = min(WIN_MAX, PAST_LEN)
    L = DEPTH
    return {
        'x_prompt': nrm(ks[0], (BATCH, SEQ, D_MODEL), 1.0),
        'x_sample': nrm(ks[1], (DEC_BATCH, DEC_SEQ, D_MODEL), 1.0),
        'cache_win_k': nrm(ks[2], (L, DEC_BATCH, wb, N_HEADS, HEAD_DIM), 1.0),
        'cache_win_v': nrm(ks[3], (L, DEC_BATCH, wb, N_HEADS, HEAD_DIM), 1.0),
        'state_conv_a': nrm(ks[4], (L, DEC_BATCH, CONV_A_WIDTH - 1, D_CONV), 0.5),
        'state_ffn_conv': nrm(ks[5], (L, DEC_BATCH, FFN_CONV_WIDTH - 1, 2 * D_FF), 0.5),
        'c_prompt': nrm(ks[6], (BATCH, D_MODEL), 1.0),
        'c_sample': nrm(ks[7], (DEC_BATCH, D_MODEL), 1.0),
        'norm_mix_g': 1.0 + nrm(ks[8], (L, D_MODEL), 0.02),
        'norm_ffn_g': 1.0 + nrm(ks[9], (L, D_MODEL), 0.02),
        'w_ada': nrm(ks[10], (L, D_MODEL, 6 * D_MODEL), 0.5 * D_MODEL ** -0.5),
        'b_ada': nrm(ks[11], (L, 6 * D_MODEL), 0.02),
        'w_in': nrm(ks[12], (L, D_MODEL, 2 * D_CONV + 3 * D_ATTN), D_MODEL ** -0.5),
        'conv_a_w': nrm(ks[13], (L, CONV_A_WIDTH, D_CONV), CONV_A_WIDTH ** -0.5),
        'conv_a_b': nrm(ks[14], (L, D_CONV), 0.02),
        'ln_a_g': 1.0 + nrm(ks[15], (L, D_CONV), 0.02),
        'ln_a_b': nrm(ks[16], (L, D_CONV), 0.02),
        'q_norm_g': 1.0 + nrm(ks[17], (L, HEAD_DIM), 0.02),
        'k_norm_g': 1.0 + nrm(ks[18], (L, HEAD_DIM), 0.02),
        'w_out': nrm(ks[19], (L, D_MIX, D_MODEL), D_MIX ** -0.5),
        'w_up': nrm(ks[20], (L, D_MODEL, 2 * D_FF), D_MODEL ** -0.5),
        'ffn_conv_w': nrm(ks[21], (L, FFN_CONV_WIDTH, 2 * D_FF), FFN_CONV_WIDTH ** -0.5),
        'ffn_conv_b': nrm(ks[22], (L, 2 * D_FF), 0.02),
        'w_down': nrm(ks[23], (L, D_FF, D_MODEL), D_FF ** -0.5),
    }


def reference(x_prompt, x_sample, cache_win_k, cache_win_v, state_conv_a, state_ffn_conv,
              c_prompt, c_sample, norm_mix_g, norm_ffn_g, w_ada, b_ada, w_in, conv_a_w, conv_a_b,
              ln_a_g, ln_a_b, q_norm_g, k_norm_g, w_out, w_up, ffn_conv_w, ffn_conv_b, w_down):
    B, S = x_prompt.shape[0], x_prompt.shape[1]
    Bd, T = x_sample.shape[0], x_sample.shape[1]
    pos_p = jnp.arange(S)
    pos_s = PAST_LEN + jnp.arange(T)
    wp = min(WIN_MAX, S)
    xp, xs = x_prompt, x_sample
    kp_l, vp_l, ap_l, fp_l = [], [], [], []
    ks_l, vs_l, as_l, fs_l = [], [], [], []
    for l in range(DEPTH):
        p = dict(norm_mix_g=norm_mix_g[l], norm_ffn_g=norm_ffn_g[l], w_ada=w_ada[l], b_ada=b_ada[l],
                 w_in=w_in[l], conv_a_w=conv_a_w[l], conv_a_b=conv_a_b[l], ln_a_g=ln_a_g[l],
                 ln_a_b=ln_a_b[l], q_norm_g=q_norm_g[l], k_norm_g=k_norm_g[l], w_out=w_out[l],
                 w_up=w_up[l], ffn_conv_w=ffn_conv_w[l], ffn_conv_b=ffn_conv_b[l], w_down=w_down[l])
        a0 = jnp.zeros((B, CONV_A_WIDTH - 1, D_CONV), xp.dtype)
        f0 = jnp.zeros((B, FFN_CONV_WIDTH - 1, 2 * D_FF), xp.dtype)
        xp, kp, vp, ap, fp = trunk_layer(xp, c_prompt, pos_p, a0, f0, prompt_attend, p)
        kp_l.append(kp[:, S - wp:])
        vp_l.append(vp[:, S - wp:])
        ap_l.append(ap)
        fp_l.append(fp)
        kb, vb = cache_win_k[l], cache_win_v[l]
        wb = kb.shape[1]
        attend_s = functools.partial(sample_attend, k_buf=kb, v_buf=vb)
        xs, kn, vn, an, fn = trunk_layer(xs, c_sample, pos_s, state_conv_a[l], state_ffn_conv[l], attend_s, p)
        ks_l.append(jnp.concatenate([kb, kn.astype(kb.dtype)], axis=1)[:, T:T + wb])
        vs_l.append(jnp.concatenate([vb, vn.astype(vb.dtype)], axis=1)[:, T:T + wb])
        as_l.append(an)
        fs_l.append(fn)
    return (xp, xs, jnp.stack(kp_l), jnp.stack(vp_l), jnp.stack(ap_l), jnp.stack(fp_l),
            jnp.stack(ks_l), jnp.stack(vs_l), jnp.stack(as_l), jnp.stack(fs_l))
```

```python
import numpy as np
import ml_dtypes
from contextlib import ExitStack
import concourse.bass as bass
import concourse.mybir as mybir
from concourse.bass_utils import run_bass_kernel_spmd

F32 = mybir.dt.float32
BF16 = mybir.dt.bfloat16
AF = mybir.ActivationFunctionType
ALU = mybir.AluOpType
AX = mybir.AxisListType
ENGS = ("sp", "act", "dve", "pool", "pe")
ND = 44
DPOOL = {"sp": (0, 28), "pool": (28, 36), "act": (36, 44)}
EPS = 1e-6

CAW, CAB, LNG, LNB, FCW, FCB, QG, KG, VAL, NHALF, VROW, NCST = 0, 248, 256, 264, 272, 536, 624, 688, 752, 778, 780, 782


class Sched:
    def __init__(self):
        self.ops = {e: [] for e in ENGS}
        self.cnt = {e: 0 for e in ENGS}
        self.seen = {e: {} for e in ENGS}
        self.lastw = {}
        self.readers = {}
        self.dma_uses = [0] * ND
        self.dma_rr = {}

    def _need(self, eng, tok, waits):
        if tok is None:
            return
        k, v = tok
        if k == eng and eng == "pe":
            return
        if self.seen[eng].get(k, 0) >= v:
            return
        if waits.get(k, 0) < v:
            waits[k] = v

    def op(self, eng, fn, reads=(), writes=(), dma=False):
        excl = [r for r in reads if isinstance(r, str) and len(r) == 3 and r[:2] in ("ps", "pt")]
        if excl:
            reads = [r for r in reads if r not in excl]
            writes = list(writes) + excl
        waits = {}
        for r in reads:
            self._need(eng, self.lastw.get(r), waits)
        for w in writes:
            self._need(eng, self.lastw.get(w), waits)
            for k, v in self.readers.get(w, {}).items():
                self._need(eng, (k, v), waits)
        if dma:
            lo, hi = DPOOL[eng]
            j = self.dma_rr.get(eng, lo)
            self.dma_rr[eng] = lo + (j + 1 - lo) % (hi - lo)
            if self.dma_uses[j] > 0:
                self._need(eng, (("d", j), 16 * self.dma_uses[j]), waits)
            self.dma_uses[j] += 1
            tok = (("d", j), 16 * self.dma_uses[j])
        else:
            self.cnt[eng] += 1
            tok = (eng, self.cnt[eng])
        for k, v in waits.items():
            self.seen[eng][k] = v
        self.ops[eng].append((list(waits.items()), fn, tok, dma))
        for r in reads:
            d = self.readers.setdefault(r, {})
            if d.get(tok[0], 0) < tok[1]:
                d[tok[0]] = tok[1]
        for w in writes:
            self.lastw[w] = tok
            self.readers[w] = {}
        return tok

    def finish(self):
        waits = {}
        for j in range(ND):
            if self.dma_uses[j] > 0:
                self._need("sp", (("d", j), 16 * self.dma_uses[j]), waits)
        for e in ENGS:
            if e != "sp" and self.cnt[e] > 0:
                self._need("sp", (e, self.cnt[e]), waits)
        for k, v in waits.items():
            self.seen["sp"][k] = v
        self.ops["sp"].append((list(waits.items()), None, None, False))

    def emit(self, nc, esem, dsem):
        def semof(k):
            return esem[k] if isinstance(k, str) else dsem[k[1]]

        def run(e, name):
            for waits, fn, tok, dma in self.ops[name]:
                for k, v in waits:
                    e.wait_ge(semof(k), v)
                if fn is None:
                    continue
                ins = fn(e)
                ins.then_inc(semof(tok[0]), 16 if dma else 1)
            self.ops[name] = []

        with nc.Block() as block:
            @block.sync
            def _(e):
                run(e, "sp")

            @block.scalar
            def _(e):
                run(e, "act")

            @block.vector
            def _(e):
                run(e, "dve")

            @block.gpsimd
            def _(e):
                run(e, "pool")

            @block.tensor
            def _(e):
                run(e, "pe")


def build_program():
    try:
        return _build()
    except _Stop as ex:
        return ex.nc


def _build():
    nc = bass.Bass("TRN2", target_bir_lowering=False)
    S = Sched()

    def din(name, shape, dt=F32):
        return nc.dram_tensor(name, list(shape), dt, kind="ExternalInput").ap()

    def dout(name, shape):
        return nc.dram_tensor(name, list(shape), F32, kind="ExternalOutput").ap()

    def dscr(name, shape, dt):
        return nc.dram_tensor(name, list(shape), dt, kind="Internal").ap()

    xh = din("xh", [2048, 2048]); xm = din("xm", [1152, 2048]); xs = din("xs", [4, 2048])
    cvec = din("cvec", [2, 2048]); gn = din("gn", [2, 2048]); bada = din("bada", [2, 12288])
    cst_d = din("cst", [128, NCST]); ropec_d = din("ropec", [128, 26, 32]); ropes_d = din("ropes", [128, 26, 32])
    band_d = din("band", [128, 256], BF16); msk12_d = din("msk12", [128, 96], BF16); wm_d = din("wm", [4, 32], BF16)
    identb_d = din("identb", [128, 128], BF16); identf_d = din("identf", [128, 128]); sel_d = din("sel", [2, 256])
    ck = din("ck", [2048, 1024]); cv = din("cv", [2048, 1024]); sta = din("sta", [30, 1024]); stf = din("stf", [2, 11264])
    w_ada = din("w_ada", [2048, 12288]); w_in = din("w_in", [2048, 5120]); w_out = din("w_out", [2048, 2048])
    w_up = din("w_up", [2048, 11264]); w_down = din("w_down", [5632, 2048])

    yp = dout("yp", [1024, 2048]); yso = dout("yso", [4, 2048]); kp = dout("kp", [1024, 1024]); vp = dout("vp", [1024, 1024])
    cap = dout("cap", [30, 1024]); fcp = dout("fcp", [2, 11264]); kso = dout("kso", [2048, 1024]); vso = dout("vso", [2048, 1024])
    cas = dout("cas", [30, 1024]); fcs = dout("fcs", [2, 11264])

    KT_s = dscr("KT_s", [8, 128, 3200], BF16)
    V_s = dscr("V_s", [3200, 1536], BF16)
    GS = dscr("GS", [4, 128, 2048], F32)
    XM_s = dscr("XM_s", [1284, 2048], F32)

    def chk(n):
        if STOP_AFTER is not None and n == STOP_AFTER:
            S.finish()
            S.emit(nc, esem, dsem)
            ex = _Stop()
            ex.nc = nc
            raise ex

    with ExitStack() as st0:
        def T(stk, name, shape, dt):
            return stk.enter_context(nc.sbuf_tensor("sb_" + name, list(shape), dt))

        esem = {e: st0.enter_context(nc.semaphore("s_" + e)) for e in ENGS}
        dsem = [st0.enter_context(nc.semaphore("d%d" % j)) for j in range(ND)]
        ps = [st0.enter_context(nc.psum_tensor("ps%d" % i, [128, 512], F32)) for i in range(6)]
        pt = [st0.enter_context(nc.psum_tensor("pt%d" % i, [128, 1024], BF16)) for i in range(2)]

        identb = T(st0, "identb", [128, 128], BF16); identf = T(st0, "identf", [128, 128], F32)
        cst = T(st0, "cst", [128, NCST], F32); band = T(st0, "band", [128, 256], BF16)
        msk12 = T(st0, "msk12", [128, 96], BF16); wm = T(st0, "wm", [4, 32], BF16)
        sel = T(st0, "sel", [2, 256], F32); onesf = T(st0, "onesf", [128, 128], F32); ones64 = T(st0, "ones64", [128, 64], BF16)
        modT = T(st0, "modT", [128, 96, 2], F32); gnT = T(st0, "gnT", [128, 16, 2], F32)
        A1T = T(st0, "A1T", [128, 16, 2], F32); A2T = T(st0, "A2T", [128, 16, 2], F32)
        ss = T(st0, "ss", [128, 8], F32); rstd = T(st0, "rstd", [128, 8], F32)

        def dma(eng, out, in_, reads=(), writes=()):
            S.op(eng, lambda e: e.dma_start(out=out, in_=in_), reads=reads, writes=writes, dma=True)

        def mm(out, lhsT, rhs, start, stop, reads, writes, skip=False):
            if skip:
                S.op("pe", lambda e: e.matmul(out=out, lhsT=lhsT, rhs=rhs, start=start, stop=stop, skip_group_check=True), reads=reads, writes=writes)
            else:
                S.op("pe", lambda e: e.matmul(out=out, lhsT=lhsT, rhs=rhs, start=start, stop=stop), reads=reads, writes=writes)

        def tr(out, in_, ident, reads, writes):
            S.op("pe", lambda e: e.transpose(out=out, in_=in_, identity=ident), reads=reads, writes=writes)

        def act(out, in_, func, reads, writes, scale=None, bias=None, accum_out=None):
            kw = {}
            if accum_out is not None:
                kw["accum_out"] = accum_out
            if scale is not None:
                kw["scale"] = scale
            if bias is not None:
                kw["bias"] = bias
            S.op("act", lambda e: e.activation(out=out, in_=in_, func=func, **kw), reads=reads, writes=writes)

        def tt(eng, out, in0, in1, op, reads, writes):
            S.op(eng, lambda e: e.tensor_tensor(out=out, in0=in0, in1=in1, op=op), reads=reads, writes=writes)

        def ts(eng, out, in0, s1, s2, op0, op1, reads, writes):
            if s2 is None:
                S.op(eng, lambda e: e.tensor_scalar(out=out, in0=in0, scalar1=s1, scalar2=None, op0=op0), reads=reads, writes=writes)
            else:
                S.op(eng, lambda e: e.tensor_scalar(out=out, in0=in0, scalar1=s1, scalar2=s2, op0=op0, op1=op1), reads=reads, writes=writes)

        def stt(out, in0, scalar, in1, op0, op1, reads, writes):
            S.op("dve", lambda e: e.scalar_tensor_tensor(out=out, in0=in0, scalar=scalar, in1=in1, op0=op0, op1=op1), reads=reads, writes=writes)

        def cp(eng, out, in_, reads, writes):
            if eng == "act":
                act(out, in_, AF.Copy, reads, writes)
            else:
                S.op(eng, lambda e: e.tensor_copy(out=out, in_=in_), reads=reads, writes=writes)

        def memset(eng, ap, val, writes):
            S.op(eng, lambda e: e.memset(ap, val), writes=writes)

        def rsq(out, in_, P, n, reads, writes):
            tt("pool", out, in_, cst[0:P, NHALF:NHALF + 1].to_broadcast([P, n]), ALU.pow, list(reads) + ["cst"], writes)

        for t_, d_, k_ in ((identb, identb_d, "identb"), (identf, identf_d, "identf"), (cst, cst_d, "cst"), (band, band_d, "band"),
                           (msk12, msk12_d, "msk12"), (wm, wm_d, "wm"), (sel, sel_d, "sel")):
            dma("sp", t_[:], d_, writes=[k_])
        memset("dve", onesf[:], 1.0, ["onesf"])
        memset("dve", ones64[:], 1.0, ["ones64"])

        sT = T(st0, "sT", [128, 16, 2], BF16)

        def ada_stages(cb, wb, tmp, gtl, badp, pmm, psel, ptr_, tr_base, cnt, wst):
            w = wb[cb % 2]; wk = "wb%d" % (cb % 2)
            pb = ps[pmm[cb % 2]]; pk = "ps%d" % pmm[cb % 2]
            tp = tmp[cb % 2]; tk = "tmp%d" % (cb % 2)
            bp = badp[cb % 4]; bk = "badp%d" % (cb % 4)

            def L():
                for q4 in range(4):
                    sb_ = wst[q4 % 2]; sk_ = "wst%d" % (q4 % 2)
                    dma("sp", sb_[:, :, :], w_ada[512 * q4:512 * q4 + 512, cb * 512:(cb + 1) * 512].rearrange("(k p) n -> p k n", p=128), writes=[sk_])
                    cp("act", w[:, 4 * q4:4 * q4 + 4, :], sb_[:, :, :], [sk_], [wk])
                dma("sp", bp[:, :], bada[:, cb * 512:(cb + 1) * 512], writes=[bk])

            def M():
                for k in range(16):
                    mm(pb[0:2, :], sT[:, k, :], w[:, k, :], k == 0, k == 15, ["sT", wk], [pk])

            def E():
                tt("dve", tp[:], pb[0:2, :], bp[:, :], ALU.add, [pk, bk], [tk])
                for j in range(4):
                    c0 = (cb * 4 + j) * 2 - tr_base
                    tr(ps[ptr_][:, c0:c0 + 2], tp[0:2, j * 128:(j + 1) * 128], identf[0:2, 0:2], [tk, "identf"], ["ps%d" % ptr_])
                gi_ = cb // 4
                if gi_ in (2, 5):
                    for grp in range(2):
                        mm(ps[psel][:], sel[0:2, grp * 128:(grp + 1) * 128], tp[0:2, :], True, True, ["sel", tk], ["ps%d" % psel])
                        g = gtl[cnt[0] % 2]; gk = "gtl%d" % (cnt[0] % 2); cnt[0] += 1
                        cp("act", g[:], ps[psel][:], ["ps%d" % psel], [gk])
                        which = (0 if gi_ == 2 else 2) + grp
                        cc = (cb % 4) * 512
                        dma("sp", GS[which, :, cc:cc + 512], g[:], reads=[gk], writes=["GS"])

            return L, M, E

        with ExitStack() as st:
            cvt = T(st, "cvt", [2, 2048], F32); cvb = T(st, "cvb", [2, 2048], BF16)
            gnt = T(st, "gnt", [2, 2048], F32)
            wb = [T(st, "wb%d" % i, [128, 16, 512], BF16) for i in range(2)]
            tmp = [T(st, "tmp%d" % i, [2, 512], F32) for i in range(2)]
            gtl = [T(st, "gtl%d" % i, [128, 512], F32) for i in range(2)]
            badp = [T(st, "badp%d" % i, [2, 512], F32) for i in range(4)]
            dma("sp", cvt[:], cvec, writes=["cvt"]); dma("sp", gnt[:], gn, writes=["gnt"])
            act(cvb[:], cvt[:], AF.Silu, ["cvt"], ["cvb"])
            for k in range(16):
                tr(pt[0][:, 2 * k:2 * k + 2], cvb[0:2, k * 128:(k + 1) * 128], identb[0:2, 0:2], ["cvb", "identb"], ["pt0"])
            cp("dve", sT[:, :, :], pt[0][:, 0:32].rearrange("p (k t) -> p k t", t=2), ["pt0"], ["sT"])
            for k in range(16):
                tr(ps[3][:, 192 + 2 * k:194 + 2 * k], gnt[0:2, k * 128:(k + 1) * 128], identf[0:2, 0:2], ["gnt", "identf"], ["ps3"])
            cnt0 = [0]
            wst0 = [T(st, "wst%d" % i, [128, 4, 512], F32) for i in range(2)]
            stg = [ada_stages(cb, wb, tmp, gtl, badp, (0, 1), 2, 3, 0, cnt0, wst0) for cb in range(8)]
            stg[0][0]()
            for cb in range(8):
                if cb + 1 < 8:
                    stg[cb + 1][0]()
                stg[cb][1]()
                stg[cb][2]()
            cp("dve", modT[:, 0:32, :], ps[3][:, 0:64].rearrange("p (k t) -> p k t", t=2), ["ps3"], ["modT"])
            cp("dve", gnT[:, :, :], ps[3][:, 192:224].rearrange("p (k t) -> p k t", t=2), ["ps3"], ["gnT"])
            for grp in range(2):
                ts("dve", A1T[:, :, grp], modT[:, 16:32, grp], 1.0, None, ALU.add, None, ["modT"], ["A1T"])
                tt("dve", A1T[:, :, grp], A1T[:, :, grp], gnT[:, :, 0], ALU.mult, ["A1T", "gnT"], ["A1T"])
            S.emit(nc, esem, dsem)
            chk(0)

        def pipe_steps(items, lags, rev=True):
            n = len(items); L = max(lags)
            steps = []
            for t_ in range(n + L):
                def step(t_=t_):
                    for st_i, lg in (reversed(list(enumerate(lags))) if rev else list(enumerate(lags))):
                        ii = t_ - lg
                        if 0 <= ii < n and items[ii][st_i] is not None:
                            items[ii][st_i]()
                steps.append(step)
            return steps

        def pipeline(items, lags):
            for st_ in pipe_steps(items, lags):
                st_()

        def run_merged(stepsA, stepsB, off):
            for t_ in range(max(len(stepsA), len(stepsB) + off)):
                if t_ < len(stepsA):
                    stepsA[t_]()
                if 0 <= t_ - off < len(stepsB):
                    stepsB[t_ - off]()

        def norm_loop(stk_tiles, specs, AT, sh_lo, hT, defer=False):
            xts, sqts, xbs = stk_tiles

            def mk(i, sp_):
                b2 = i % 2
                xt = xts[b2]; sqt = sqts[b2]; xb = xbs[b2]
                xk = "xt%d" % (b2 if xts[0] is not xts[1] else 0); sk = "sqt%d" % b2; bk = "xb%d" % b2
                P = sp_["P"]; grp = sp_["grp"]; col0 = sp_["col0"]; c0 = sp_.get("c0", 0); n = sp_.get("n", P); hkey = sp_["hkey"]
                ssc = ss[0:P, b2:b2 + 1]; rsc = rstd[0:P, b2:b2 + 1]; ssk = "ss%d" % b2; rsk = "rstd%d" % b2

                def n0a():
                    dma("sp", xt[0:P, :], sp_["src"], reads=sp_.get("src_reads", ()), writes=[xk])

                def n0():
                    act(sqt[0:P, :], xt[0:P, :], AF.Square, [xk], [sk, ssk], accum_out=ssc)
                    ts("dve", ssc, ssc, 1.0 / 2048, EPS, ALU.mult, ALU.add, [ssk], [ssk])
                    rsq(rsc, ssc, P, 1, [ssk], [rsk])
                    act(xb[0:P, :], xt[0:P, :], AF.Identity, [xk, rsk], [bk], scale=rsc)

                def n1():
                    for k in range(16):
                        b = k // 8
                        tr(pt[b][:, (k % 8) * 128:(k % 8) * 128 + P], xb[0:P, k * 128:(k + 1) * 128], identb[0:P, 0:P], [bk, "identb"], ["pt%d" % b])

                def n2():
                    for b in range(2):
                        src_ = pt[b][:, 0:1024].rearrange("p (k t) -> p k t", k=8)[:, :, c0:c0 + n]
                        dst = hT[:, 8 * b:8 * b + 8, col0:col0 + n]
                        tt("dve", dst, src_, AT[:, 8 * b:8 * b + 8, grp:grp + 1].to_broadcast([128, 8, n]), ALU.mult, ["pt%d" % b, "A1T", "A2T"], [hkey])
                        tt("pool", dst, dst, modT[:, sh_lo + 8 * b:sh_lo + 8 * b + 8, grp:grp + 1].to_broadcast([128, 8, n]), ALU.add, [hkey, "modT"], [hkey])

                return [n0a, n0, n1, n2]

            steps_ = pipe_steps([mk(i, sp_) for i, sp_ in enumerate(specs)], [0, 1, 2, 3])
            if defer:
                return steps_
            for st_ in steps_:
                st_()

        def load_w(w, wk, src_rows_fn, ncolsets):
            for (d0, n, s0) in ncolsets:
                for q4 in range(4):
                    dma("pool", w[:, 4 * q4:4 * q4 + 4, d0:d0 + n], src_rows_fn(512 * q4, 512 * q4 + 512, s0, s0 + n).rearrange("(k p) n -> p k n", p=128), writes=[wk])

        def phase1(tiles, blocks, ntok, pname):
            with ExitStack() as st:
                hT = T(st, "hT" + pname, [128, 16, ntok], BF16)
                ropec = T(st, "ropec" + pname, [128, 26, 32], F32); ropes = T(st, "ropes" + pname, [128, 26, 32], F32)
                dma("sp", ropec[:], ropec_d, writes=["rope"]); dma("sp", ropes[:], ropes_d, writes=["rope2"])
                if pname == "a":
                    xts = [T(st, "xt%d" % i_ + pname, [128, 2048], F32) for i_ in range(2)]
                else:
                    xts = [T(st, "xt0" + pname, [128, 2048], F32)] * 2
                sqts = [T(st, "sqt%d" % i_ + pname, [128, 2048], BF16) for i_ in range(2)]; xbs = [T(st, "xb%d" % i_ + pname, [128, 2048], BF16) for i_ in range(2)]
                wb = [T(st, "wb%d" % i + pname, [128, 16, 512], BF16) for i in range(2)]
                sq5_l = [T(st, "sq5%d" % i + pname, [128, 512], F32) for i in range(2)]; qn_l = [T(st, "qn%d" % i + pname, [128, 512], F32) for i in range(2)]
                qr = [T(st, "qr%d" % i + pname, [128, 512], F32) for i in range(3)]
                t1_l = [T(st, "t1%d" % i + pname, [128, 256], F32) for i in range(2)]; t2_l = [T(st, "t2%d" % i + pname, [128, 256], F32) for i in range(2)]
                t3_l = [T(st, "t3%d" % i + pname, [128, 256], F32) for i in range(2)]; t4_l = [T(st, "t4%d" % i + pname, [128, 256], F32) for i in range(2)]
                kb_l = [T(st, "kb%d" % i + pname, [128, 512], BF16) for i in range(2)]
                ssq = T(st, "ssq" + pname, [128, 16], F32); rsq_ = T(st, "rsq_" + pname, [128, 24], F32)
                ktp = [T(st, "ktp%d" % i + pname, [128, 4, 128], BF16) for i in range(2)]
                vt = [T(st, "vt%d" % i + pname, [128, 512], F32) for i in range(2)]
                vb = [T(st, "vb%d" % i + pname, [128, 768], BF16) for i in range(2)]
                vv = T(st, "vv" + pname, [128, 64], BF16)
                sg_l = [T(st, "sg%d" % i + pname, [128, 256], F32) for i in range(2)]
                gt = [T(st, "gt%d" % i + pname, [128, 256], F32) for i in range(3)]
                gb_l = [T(st, "gb%d" % i + pname, [128, 256], BF16) for i in range(2)]
                nsteps = norm_loop((xts, sqts, xbs), [dict(src=tl["src"], P=tl["P"], grp=(1 if tl["kind"] == "samp" else 0), col0=tl["col0"], hkey=("hT", i)) for i, tl in enumerate(tiles)], A1T, 0, hT, defer=True)
                nb = 0

                wst = [T(st, "wsi%d" % i_ + pname, [128, 1, 512], F32) for i_ in range(2)]

                def piece_dma(bi_, pc):
                    kind_, j_ = blocks[bi_]
                    sb_ = wst[pc % 2]; sk_ = "wsi%d" % (pc % 2)
                    if kind_ == "glu":
                        colsets = [(0, 256, 256 * j_), (256, 256, 1024 + 256 * j_)]
                    else:
                        colsets = [(0, 512, {"q": 2048, "k": 3072, "v": 4096}[kind_] + 512 * j_)]
                    for (d0, n_, s0_) in colsets:
                        dma("sp", sb_[:, 0, d0:d0 + n_], w_in[128 * pc:128 * pc + 128, s0_:s0_ + n_], writes=[sk_])

                def piece_cast(bi_, pc):
                    cp("act", wb[bi_ % 2][:, pc, :], wst[pc % 2][:, 0, :], ["wsi%d" % (pc % 2)], ["wbuf%d" % (bi_ % 2)])

                def issue_load(bi_):
                    for pc in range(16):
                        piece_dma(bi_, pc)
                        piece_cast(bi_, pc)

                def mk_item(bi, blk, i, tl, nb, first, pos):
                    kind, j = blk
                    w = wb[bi % 2]; wk = "wbuf%d" % (bi % 2)
                    P = tl["P"]; t = tl["t"]; col0 = tl["col0"]
                    pb = ps[nb % 4]; pk = "ps%d" % (nb % 4); tb = nb % 2
                    sq5 = sq5_l[tb]; qn = qn_l[tb]; t1 = t1_l[tb]; t2 = t2_l[tb]; t3 = t3_l[tb]; t4 = t4_l[tb]; kb = kb_l[tb]; sg = sg_l[tb]; gb = gb_l[tb]
                    TB = str(tb)
                    ptb = ps[4 + tb][:, :].bitcast(BF16); ptk = "ps%d" % (4 + tb)
                    vcol = cst[0:P, VAL + t:VAL + t + 1]
                    g = gt[nb % 3]; gk = "gt%d" % (nb % 3)
                    o = qr[nb % 3]; ok_ = "qr%d" % (nb % 3)
                    r3 = nb % 3; RK = "rsq_%d" % r3
                    rs8 = rsq_[0:P, 8 * r3:8 * r3 + 8]

                    def stA():
                        if bi + 1 < len(blocks):
                            if 1 <= pos <= 8:
                                piece_cast(bi + 1, 2 * pos - 2)
                                piece_cast(bi + 1, 2 * pos - 1)
                            if pos < 8:
                                piece_dma(bi + 1, 2 * pos)
                                piece_dma(bi + 1, 2 * pos + 1)
                        for k in range(16):
                            mm(pb[0:P, :], hT[:, k, col0:col0 + P], w[:, k, :], k == 0, k == 15, [("hT", i), wk], [pk])

                    def stB():
                        if kind == "glu":
                            act(sg[0:P, :], pb[0:P, 256:512], AF.Sigmoid, [pk], ["sg" + TB])
                            stt(g[0:P, :], pb[0:P, 0:256], vcol, sg[0:P, :], ALU.mult, ALU.mult, [pk, "sg" + TB, "cst"], [gk])
                            if tl["kind"] == "samp":
                                dma("sp", cas[26:30, 256 * j:256 * j + 256], g[0:4, :], reads=[gk])
                            elif tl.get("own") == 7:
                                dma("sp", cap[:, 256 * j:256 * j + 256], g[98:128, :], reads=[gk])
                        elif kind in ("q", "k"):
                            gcol = QG if kind == "q" else KG
                            q3 = qn[0:P, :].rearrange("p (h d) -> p h d", h=8)
                            tt("dve", q3, pb[0:P, :].rearrange("p (h d) -> p h d", h=8), cst[0:P, gcol:gcol + 64].unsqueeze(1).to_broadcast([P, 8, 64]), ALU.mult, [pk, "cst"], ["qn" + TB])
                            act(sq5[0:P, :], pb[0:P, :], AF.Square, [pk], ["sq5" + TB])
                            S.op("dve", lambda e: e.tensor_reduce(out=ssq[0:P, 8 * tb:8 * tb + 8], in_=sq5[0:P, :].rearrange("p (h d) -> p h d", h=8), axis=AX.X, op=ALU.add), reads=["sq5" + TB], writes=["ssq" + TB])
                            ts("dve", ssq[0:P, 8 * tb:8 * tb + 8], ssq[0:P, 8 * tb:8 * tb + 8], 1.0 / 64, EPS, ALU.mult, ALU.add, ["ssq" + TB], ["ssq" + TB])
                            rsq(rs8, ssq[0:P, 8 * tb:8 * tb + 8], P, 8, ["ssq" + TB], [RK])
                            o3 = o[0:P, :].rearrange("p (h d) -> p h d", h=8)
                            C = ropec[0:P, t, :].unsqueeze(1).to_broadcast([P, 8, 32]); Sn = ropes[0:P, t, :].unsqueeze(1).to_broadcast([P, 8, 32])
                            x1 = q3[:, :, 0:32]; x2 = q3[:, :, 32:64]
                            v1 = t1[0:P, :].rearrange("p (h d) -> p h d", h=8); v2 = t2[0:P, :].rearrange("p (h d) -> p h d", h=8)
                            v3 = t3[0:P, :].rearrange("p (h d) -> p h d", h=8); v4 = t4[0:P, :].rearrange("p (h d) -> p h d", h=8)
                            tt("dve", v1, x1, C, ALU.mult, ["qn" + TB, "rope"], ["t1" + TB])
                            tt("dve", v2, x2, Sn, ALU.mult, ["qn" + TB, "rope2"], ["t2" + TB])
                            tt("dve", o3[:, :, 0:32], v1, v2, ALU.subtract, ["t1" + TB, "t2" + TB], [ok_])
                            tt("pool", v3, x2, C, ALU.mult, ["qn" + TB, "rope"], ["t3" + TB])
                            tt("pool", v4, x1, Sn, ALU.mult, ["qn" + TB, "rope2"], ["t4" + TB])
                            tt("pool", o3[:, :, 32:64], v3, v4, ALU.add, ["t3" + TB, "t4" + TB], [ok_])
                        else:
                            if tl["kind"] == "samp":
                                v_ = vt[tb]; vk_ = "vt%d" % tb
                                cp("act", v_[0:4, :], pb[0:4, :], [pk], [vk_])
                                dma("sp", vso[2044:2048, 512 * j:512 * j + 512], v_[0:4, :], reads=[vk_])
                                cp("dve", Vsm[0:4, 512 * j:512 * j + 512], pb[0:4, :], [pk], ["Vsm"])
                            else:
                                if "own" in tl:
                                    v_ = vt[tb]; vk_ = "vt%d" % tb
                                    cp("act", v_[:, :], pb[:, :], [pk], [vk_])
                                    dma("sp", vp[tl["own"] * 128:(tl["own"] + 1) * 128, 512 * j:512 * j + 512], v_[:, :], reads=[vk_])
                                b_ = vb[tb]; bk_ = "vb%d" % tb
                                b4 = b_[:, :].rearrange("p (h s c) -> p h s c", h=4, s=3, c=64)
                                ts("dve", b4[:, :, 0:3:2, :], pb[:, :].rearrange("p (h s c) -> p h s c", h=4, s=2, c=64), vcol, None, ALU.mult, None, [pk, "cst"], [bk_])
                                cp("pool", b4[:, :, 1, :], vcol.unsqueeze(2).to_broadcast([128, 4, 64]), ["cst"], [bk_])
                                dma("sp", V_s[t * 128:(t + 1) * 128, 768 * j:768 * j + 768], b_[:, :], reads=[bk_], writes=["V_s"])

                    def stC1():
                        if kind == "glu":
                            cp("act", gb[0:P, :], g[0:P, :], [gk], ["gb" + TB])
                            for i2 in range(2):
                                tr(ptb[:, i2 * 128:i2 * 128 + P], gb[0:P, i2 * 128:(i2 + 1) * 128], identb[0:P, 0:P], ["gb" + TB, "identb"], [ptk])
                        else:
                            o3 = o[0:P, :].rearrange("p (h d) -> p h d", h=8)
                            tt("dve", o3, o3, rs8.unsqueeze(2).to_broadcast([P, 8, 64]), ALU.mult, [ok_, RK], [ok_])
                            if kind == "k":
                                if tl["kind"] == "samp":
                                    dma("sp", kso[2044:2048, 512 * j:512 * j + 512], o[0:4, :], reads=[ok_])
                                elif "own" in tl:
                                    dma("sp", kp[tl["own"] * 128:(tl["own"] + 1) * 128, 512 * j:512 * j + 512], o[:, :], reads=[ok_])
                            cp("act", kb[0:P, :], o[0:P, :], [ok_], ["kb" + TB])
                            for i2 in range(4):
                                tr(ptb[:, i2 * 128:i2 * 128 + P], kb[0:P, i2 * 128:(i2 + 1) * 128], identb[0:P, 0:P], ["kb" + TB, "identb"], [ptk])

                    def stC2():
                        if kind == "glu":
                            cc = (30 + col0) if tl["kind"] != "samp" else 1212
                            cp("dve", cinT[:, 2 * j:2 * j + 2, cc:cc + P], ptb[:, 0:256].rearrange("p (c t) -> p c t", c=2)[:, :, 0:P], [ptk], ["cinT"])
                        else:
                            src3 = ptb[:, 0:512].rearrange("p (c t) -> p c t", c=4)[:, :, 0:P]
                            if kind == "q":
                                cp("dve", QT[:, 4 * j:4 * j + 4, col0:col0 + P], src3, [ptk], ["QT"])
                            elif tl["kind"] == "samp":
                                cp("dve", KTs[:, 4 * j:4 * j + 4, 0:4], src3, [ptk], ["KTs"])
                            else:
                                kt_ = ktp[tb]; kk = "ktp%d" % tb
                                cp("dve", kt_[:, :, :], src3, [ptk], [kk])
                                dma("sp", KT_s[4 * j:4 * j + 4, :, t * 128:(t + 1) * 128].rearrange("c p n -> p c n"), kt_[:, :, :], reads=[kk], writes=["KT_s"])

                    return [stA, stB, stC1 if kind != "v" else None, stC2 if kind != "v" else None]

                issue_load(0)
                if pname == "a":
                    for q8 in range(4):
                        dma("pool", kso[511 * q8:511 * q8 + 511, :], ck[4 + 511 * q8:4 + 511 * q8 + 511, :])
                        dma("pool", vso[511 * q8:511 * q8 + 511, :], cv[4 + 511 * q8:4 + 511 * q8 + 511, :])
                items = []
                for bi, blk in enumerate(blocks):
                    first = True
                    pos = 0
                    for i, tl in enumerate(tiles):
                        if blk[0] in ("glu", "q") and tl["kind"] == "halo":
                            continue
                        items.append(mk_item(bi, blk, i, tl, len(items), first, pos))
                        first = False
                        pos += 1
                run_merged(nsteps, pipe_steps(items, [0, 1, 3, 4], rev=False), 4)
                S.emit(nc, esem, dsem)
                chk(1 if pname == "a" else 15)

        halo_tiles = [dict(kind="halo", src=xh[t * 128:(t + 1) * 128, :], P=128, t=t, col0=t * 128) for t in range(16)]
        phase1(halo_tiles, [("k", 0), ("k", 1), ("v", 0), ("v", 1)], 2048, "a")
        with ExitStack() as stA:
            QT = T(stA, "QT", [128, 8, 1156], BF16)
            aT = T(stA, "aT", [128, 8, 1156], BF16)
            KTs = T(stA, "KTs", [128, 8, 4], BF16)
            Vsm = T(stA, "Vsm", [4, 1024], BF16)
            with ExitStack() as stB:
                cinT = T(stB, "cinT", [128, 8, 1216], BF16)
                memset("pool", cinT[:, :, 0:30], 0.0, ["cinT"])

                main_tiles = []
                for m in range(9):
                    d = dict(kind="main", src=xm[m * 128:(m + 1) * 128, :], P=128, t=16 + m, col0=m * 128)
                    if m >= 1:
                        d["own"] = m - 1
                    main_tiles.append(d)
                main_tiles.append(dict(kind="samp", src=xs, P=4, t=25, col0=1152))
                blocks_b = [("glu", 0), ("glu", 1), ("glu", 2), ("glu", 3), ("q", 0), ("q", 1), ("k", 0), ("k", 1), ("v", 0), ("v", 1)]
                import os as _os
                if _os.environ.get("DBG_KINDS"):
                    blocks_b = [b_ for b_ in blocks_b if b_[0] in _os.environ["DBG_KINDS"].split(",")]
                if _os.environ.get("DBG_NOSAMP"):
                    main_tiles = main_tiles[:-1]
                phase1(main_tiles, blocks_b, 1156, "b")

                with ExitStack() as st:
                    stt_ = T(st, "stt_", [30, 1024], F32)
                    gcs = T(st, "gcs", [128, 8, 1186], F32)
                    sqc = [T(st, "sqc%d" % i, [128, 512], F32) for i in range(2)]
                    mean = T(st, "mean", [128, 512], F32); var = T(st, "var", [128, 512], F32); m2 = T(st, "m2", [128, 512], F32)
                    tln = [T(st, "tln%d" % i, [128, 512], F32) for i in range(2)]
                    dma("sp", stt_[:, :], sta, writes=["stt_"])
                    for ch in range(8):
                        tr(ps[0][:, ch * 32:ch * 32 + 30], stt_[0:30, ch * 128:(ch + 1) * 128], identf[0:30, 0:30], ["stt_", "identf"], ["ps0"])
                    cp("dve", cinT[:, :, 1182:1212], ps[0][:, 0:256].rearrange("p (c t) -> p c t", c=8)[:, :, 0:30], ["ps0"], ["cinT"])
                    wb2 = [T(st, "wc%d" % i, [128, 16, 512], BF16) for i in range(2)]
                    tmp2 = [T(st, "tmc%d" % i, [2, 512], F32) for i in range(2)]
                    gtl2 = [T(st, "gtc%d" % i, [128, 512], F32) for i in range(2)]
                    badp2 = [T(st, "bdc%d" % i, [2, 512], F32) for i in range(4)]
                    cnt2 = [0]
                    wst2 = [T(st, "wsc%d" % i, [128, 4, 512], F32) for i in range(2)]
                    stg2 = [ada_stages(cb, wb2, tmp2, gtl2, badp2, (2, 3), 4, 5, 64, cnt2, wst2) for cb in range(8, 24)]
                    stg2[0][0](); stg2[1][0]()
                    NPE = 21
                    dg = [T(st, "dg%d" % i, [128, NPE, 128], BF16) for i in range(2)]
                    for ch in range(8):
                        acc = gcs[:, ch, :]; ak = ("gcs", ch)
                        d_ = dg[ch % 2]; dk = "dg%d" % (ch % 2)
                        tt("pool", d_[:, :, :], identb[:, :].unsqueeze(1).to_broadcast([128, NPE, 128]),
                           cst[:, CAW + ch * 31:CAW + ch * 31 + NPE].unsqueeze(2).to_broadcast([128, NPE, 128]), ALU.mult, ["identb", "cst"], [dk])
                        pbs = []
                        cbanks = [(ps[0], "ps0"), (ps[1], "ps1"), (pt[0][:, :].bitcast(F32), "pt0")]
                        for gix, (a0, n) in enumerate(((0, 512), (512, 512), (1024, 162))):
                            pb, pk = cbanks[gix]
                            for k in range(NPE):
                                mm(pb[:, 0:n], d_[:, k, :], cinT[:, ch, k + a0:k + a0 + n], k == 0, k == NPE - 1, [dk, "cinT"], [pk])
                            pbs.append((pb, pk, a0, n))
                        ts("dve", acc, cinT[:, ch, NPE:NPE + 1186], cst[:, CAW + ch * 31 + NPE:CAW + ch * 31 + NPE + 1], cst[:, CAB + ch:CAB + ch + 1], ALU.mult, ALU.add, ["cinT", "cst"], [ak])
                        for k in range(NPE + 1, 31):
                            stt(acc, cinT[:, ch, k:k + 1186], cst[:, CAW + ch * 31 + k:CAW + ch * 31 + k + 1], acc, ALU.mult, ALU.add, ["cinT", "cst", ak], [ak])
                        for (pb, pk, a0, n) in pbs:
                            tt("dve", gcs[:, ch, a0:a0 + n], pb[:, 0:n], gcs[:, ch, a0:a0 + n], ALU.add, [pk, ak], [ak])
                        for i_ in (2 * ch, 2 * ch + 1):
                            stg2[i_][1]()
                            if i_ >= 1:
                                stg2[i_ - 1][2]()
                            if i_ + 2 < 16:
                                stg2[i_ + 2][0]()
                    stg2[15][2]()
                    cp("dve", modT[:, 32:96, :], ps[5][:, 0:128].rearrange("p (k t) -> p k t", t=2), ["ps5"], ["modT"])
                    for grp in range(2):
                        ts("dve", A2T[:, :, grp], modT[:, 64:80, grp], 1.0, None, ALU.add, None, ["modT"], ["A2T"])
                        tt("dve", A2T[:, :, grp], A2T[:, :, grp], gnT[:, :, 1], ALU.mult, ["A2T", "gnT"], ["A2T"])
                    for (a0, n, m0) in ((0, 512, 0), (512, 512, 512), (1024, 128, 1024), (1182, 4, 1152)):
                        for ch in range(8):
                            mm(ps[0][:, 0:n], onesf[:, :], gcs[:, ch, a0:a0 + n], ch == 0, ch == 7, ["onesf", ("gcs", ch)], ["ps0"])
                        for ch in range(8):
                            sq_ = sqc[ch % 2]; sk_ = "sqc%d" % (ch % 2)
                            act(sq_[:, 0:n], gcs[:, ch, a0:a0 + n], AF.Square, [("gcs", ch)], [sk_])
                            mm(ps[1][:, 0:n], onesf[:, :], sq_[:, 0:n], ch == 0, ch == 7, ["onesf", sk_], ["ps1"])
                        ts("dve", mean[:, 0:n], ps[0][:, 0:n], 1.0 / 1024, None, ALU.mult, None, ["ps0"], ["mean"])
                        tt("dve", m2[:, 0:n], mean[:, 0:n], mean[:, 0:n], ALU.mult, ["mean"], ["m2"])
                        stt(var[:, 0:n], ps[1][:, 0:n], 1.0 / 1024, m2[:, 0:n], ALU.mult, ALU.subtract, ["ps1", "m2"], ["var"])
                        ts("dve", var[:, 0:n], var[:, 0:n], EPS, None, ALU.add, None, ["var"], ["var"])
                        act(var[:, 0:n], var[:, 0:n], AF.Sqrt, ["var"], ["var"])
                        S.op("dve", lambda e, n=n: e.reciprocal(out=var[:, 0:n], in_=var[:, 0:n]), reads=["var"], writes=["var"])
                        for ch in range(8):
                            tl_ = tln[ch % 2]; tk_ = "tln%d" % (ch % 2)
                            tt("dve", tl_[:, 0:n], gcs[:, ch, a0:a0 + n], mean[:, 0:n], ALU.subtract, [("gcs", ch), "mean"], [tk_])
                            tt("pool", tl_[:, 0:n], tl_[:, 0:n], var[:, 0:n], ALU.mult, [tk_, "var"], [tk_])
                            act(aT[:, ch, m0:m0 + n], tl_[:, 0:n], AF.Silu, [tk_, "cst"], ["mixT"], scale=cst[:, LNG + ch:LNG + ch + 1], bias=cst[:, LNB + ch:LNB + ch + 1])
                    S.emit(nc, esem, dsem)
                    chk(2)

            oT = T(stA, "oT", [128, 8, 1156], BF16)
            with ExitStack() as st:
                KTh = [T(st, "KTh%d" % i, [128, 3200], BF16) for i in range(2)]
                V1 = [T(st, "V1_%d" % i, [128, 10, 192], BF16) for i in range(2)]
                V4 = [T(st, "V4_%d" % i, [128, 4, 4, 192], BF16) for i in range(2)]
                V16 = [T(st, "V16_%d" % i, [128, 2, 16, 192], BF16) for i in range(2)]
                QZ = [T(st, "QZ%d" % i, [128, 2, 1152], BF16) for i in range(2)]
                pT = [T(st, "pT%d" % i, [128, 512], BF16) for i in range(4)]
                rD = T(st, "rD", [128, 512], F32)
                Kc = T(st, "Kc", [128, 9, 1024], BF16); Vc = T(st, "Vc", [128, 9, 1024], BF16)
                KcT = T(st, "KcT", [128, 8, 9, 128], BF16)
                pN = T(st, "pN", [4, 64], BF16)

                def load_hp(hp_):
                    bs = hp_ % 2; key = "hp%d" % bs; c0 = hp_ * 192
                    dma("sp", KTh[bs][:, :], KT_s[hp_, :, :], reads=["KT_s"], writes=[key])
                    dma("sp", V1[bs][:, :, :], V_s[15 * 128:25 * 128, c0:c0 + 192].rearrange("(s p) f -> p s f", p=128), reads=["V_s"], writes=[key])
                    for G in range(3, 6):
                        dma("sp", V4[bs][:, G - 3, :, :], V_s[512 * G:512 * G + 512, c0:c0 + 192].rearrange("(m r) f -> m r f", r=4), reads=["V_s"], writes=[key])
                    dma("sp", V4[bs][0:32, 3, :, :], V_s[3072:3200, c0:c0 + 192].rearrange("(m r) f -> m r f", r=4), reads=["V_s"], writes=[key])
                    dma("sp", V16[bs][:, 0, :, :], V_s[0:2048, c0:c0 + 192].rearrange("(m r) f -> m r f", r=16), reads=["V_s"], writes=[key])
                    dma("sp", V16[bs][0:72, 1, :, :], V_s[2048:3200, c0:c0 + 192].rearrange("(m r) f -> m r f", r=16), reads=["V_s"], writes=[key])
                    cp("pool", QZ[bs][0:64, 0, :], QT[0:64, hp_, 0:1152], ["QT"], ["qz%d" % bs])
                    cp("pool", QZ[bs][64:128, 1, :], QT[64:128, hp_, 0:1152], ["QT"], ["qz%d" % bs])

                for bs in range(2):
                    memset("pool", QZ[bs][64:128, 0, :], 0.0, ["qz%d" % bs])
                    memset("pool", QZ[bs][0:64, 1, :], 0.0, ["qz%d" % bs])
                for (dst, srcd, key) in ((Kc, ck, "Kc"), (Vc, cv, "Vc")):
                    dma("pool", dst[:, 0, :], srcd[1920:2048, :], writes=[key])
                    dma("pool", dst[:, 1:5, :], srcd[1536:2048, :].rearrange("(m t) f -> m t f", t=4), writes=[key])
                    dma("pool", dst[:, 5:9, :], srcd.rearrange("(m t) f -> m t f", t=16)[:, 0:4, :], writes=[key])
                load_hp(0)
                units = []
                for hp in range(8):
                    bsel = hp % 2
                    kth = KTh[bsel]; v1 = V1[bsel]; v4 = V4[bsel]; v16 = V16[bsel]; hk = "hp%d" % bsel; qz = QZ[bsel]; qk_ = "qz%d" % bsel
                    for gi in range(3):
                        G = 4 + gi
                        nqt = 512 if gi < 2 else 128
                        qb = 512 * gi
                        batches = []
                        ntq = nqt // 128
                        for kind in ("prev", "cur"):
                            tl_ = []
                            for qt in range(ntq):
                                Tq = 16 + 4 * gi + qt
                                Tk = Tq - 1 if kind == "prev" else Tq
                                tl_.append(dict(kc=slice(Tk * 128, Tk * 128 + 128), qc=slice(qb + 128 * qt, qb + 128 * qt + 128), oc=slice(128 * qt, 128 * qt + 128), V=v1[:, Tk - 15, :]))
                            batches.append(dict(nk=128, nq=128, mask=band[:, 128:256] if kind == "prev" else band[:, 0:128], tiles=tl_))
                        nq2 = nqt // 4
                        for kind in ("prev", "cur"):
                            tl_ = []
                            nk = 128 if (kind == "prev" or gi < 2) else 32
                            for r in range(4):
                                Gk = G - 1 if kind == "prev" else G
                                tl_.append(dict(kc=slice(512 * Gk + r, 512 * Gk + 4 * nk, 4), qc=slice(qb + r, qb + nqt, 4), oc=slice(r, nqt, 4), V=v4[0:nk, Gk - 3, r, :]))
                            mk_ = band[0:nk, 128:128 + nq2] if kind == "prev" else band[0:nk, 0:nq2]
                            batches.append(dict(nk=nk, nq=nq2, mask=mk_, tiles=tl_))
                        nq3 = nqt // 16
                        m0 = 32 * gi
                        for kind in ("prev", "cur"):
                            tl_ = []
                            nk = 128 if kind == "prev" else (m0 + nq3)
                            for r in range(16):
                                if kind == "prev":
                                    kc = slice(r, 2048, 16); H = 0
                                else:
                                    kc = slice(2048 + r, 2048 + 16 * nk, 16); H = 1
                                tl_.append(dict(kc=kc, qc=slice(qb + r, qb + nqt, 16), oc=slice(r, nqt, 16), V=v16[0:nk, H, r, :]))
                            mk_ = band[0:nk, 128 + m0:128 + m0 + nq3] if kind == "prev" else band[0:nk, m0:m0 + nq3]
                            batches.append(dict(nk=nk, nq=nq3, mask=mk_, tiles=tl_))
                        ulist = []
                        for bt in batches:
                            per = max(1, 256 // bt["nq"])
                            for c_ in range(0, len(bt["tiles"]), per):
                                ulist.append(dict(nk=bt["nk"], nq=bt["nq"], mask=bt["mask"], tiles=bt["tiles"][c_:c_ + per]))
                        for ui, un in enumerate(ulist):
                            units.append(dict(un=un, first=(ui == 0), last=(ui == len(ulist) - 1), hp=hp, gi=gi, nqt=nqt, qb=qb, kth=kth, hk=hk, qz=qz, qk_=qk_,
                                              pre=(hp + 1 if (gi == 0 and ui == 5 and hp + 1 < 8) else None)))

                def mk3(u, d):
                    un = d["un"]; nk = un["nk"]; nq = un["nq"]; tiles_ = un["tiles"]; ntl = len(tiles_); tot = 2 * ntl * nq
                    sb = ps[u % 4]; sk = "ps%d" % (u % 4); p_ = pT[u % 4]; pk_ = "pT%d" % (u % 4)
                    kth = d["kth"]; hk = d["hk"]; qz = d["qz"]; qk_ = d["qk_"]; hp = d["hp"]; nqt = d["nqt"]; qb = d["qb"]

                    def s0():
                        if d["pre"] is not None:
                            load_hp(d["pre"])
                        for ti, tl in enumerate(tiles_):
                            mm(sb[0:nk, 2 * ti * nq:2 * (ti + 1) * nq].rearrange("p (h q) -> p h q", h=2), kth[:, tl["kc"]], qz[:, :, tl["qc"]], True, True, [hk, qk_], [sk])

                    def s1():
                        act(p_[0:nk, 0:tot], sb[0:nk, 0:tot], AF.Exp, [sk], [pk_], scale=0.125)
                        p3 = p_[0:nk, 0:tot].rearrange("p (t q) -> p t q", q=nq)
                        tt("dve" if u % 2 == 0 else "pool", p3, p3, un["mask"].unsqueeze(1).to_broadcast([nk, 2 * ntl, nq]), ALU.mult, [pk_, "band"], [pk_])

                    def s2():
                        for ti, tl in enumerate(tiles_):
                            st_ = bool(d["first"] and ti == 0)
                            mm(ps[4][:, tl["oc"]], tl["V"][:, 0:128], p_[0:nk, 2 * ti * nq:(2 * ti + 1) * nq], st_, False, [hk, pk_], ["ps4"], skip=True)
                            mm(ps[5][:, tl["oc"]], tl["V"][:, 64:192], p_[0:nk, (2 * ti + 1) * nq:(2 * ti + 2) * nq], st_, False, [hk, pk_], ["ps5"], skip=True)
                        if d["last"]:
                            S.op("dve", lambda e: e.reciprocal(out=rD[64:128, 0:nqt], in_=ps[4][64:128, 0:nqt]), reads=["ps4"], writes=["rDa"])
                            tt("dve", oT[0:64, hp, qb:qb + nqt], ps[4][0:64, 0:nqt], rD[64:128, 0:nqt], ALU.mult, ["ps4", "rDa"], ["mixT"])
                            S.op("dve", lambda e: e.reciprocal(out=rD[0:64, 0:nqt], in_=ps[5][0:64, 0:nqt]), reads=["ps5"], writes=["rDb"])
                            tt("dve", oT[64:128, hp, qb:qb + nqt], ps[5][64:128, 0:nqt], rD[0:64, 0:nqt], ALU.mult, ["ps5", "rDb"], ["mixT"])

                    return [s0, s1, s2]

                pipeline([mk3(u, d) for u, d in enumerate(units)], [0, 2, 4])
                S.emit(nc, esem, dsem)
                chk(3)

                ntr = 0
                for ch in range(8):
                    for half in range(2):
                        pb = pt[ntr % 2]; pk = "pt%d" % (ntr % 2); ntr += 1
                        n_ = 5 if half == 0 else 4
                        for s_ in range(n_):
                            sl = half * 5 + s_
                            tr(pb[:, s_ * 128:(s_ + 1) * 128], Kc[:, sl, ch * 128:(ch + 1) * 128], identb[:, :], ["Kc", "identb"], [pk])
                        cp("dve" if ntr % 2 else "act", KcT[:, ch, half * 5:half * 5 + n_, :], pb[:, 0:n_ * 128].rearrange("p (s k) -> p s k", s=n_), [pk], ["KcT"])
                memset("dve", ps[4][:, :], 0.0, ["ps4"])
                memset("dve", ps[5][:, :], 0.0, ["ps5"])
                for hh in range(2):
                    hr = slice(64 * hh, 64 * hh + 64)
                    sb = ps[2 * hh]; sk = "ps%d" % (2 * hh); sn = ps[2 * hh + 1]; snk = "ps%d" % (2 * hh + 1)
                    p_ = pT[2 * hh]; pk_ = "pT%d" % (2 * hh)
                    for hp in range(8):
                        qs = QT[hr, hp, 1152:1156]
                        mm(sb[:, 12 * hp:12 * hp + 4], KcT[hr, hp, 0, :], qs, True, True, ["KcT", "QT"], [sk])
                        for t in range(4):
                            mm(sb[:, 12 * hp + 4 + t:12 * hp + 5 + t], KcT[hr, hp, 1 + t, :], QT[hr, hp, 1152 + t:1153 + t], True, True, ["KcT", "QT"], [sk])
                            mm(sb[:, 12 * hp + 8 + t:12 * hp + 9 + t], KcT[hr, hp, 5 + t, :], QT[hr, hp, 1152 + t:1153 + t], True, True, ["KcT", "QT"], [sk])
                        mm(sn[0:4, 4 * hp:4 * hp + 4], KTs[hr, hp, 0:4], qs, True, True, ["KTs", "QT"], [snk])
                    act(p_[:, 0:96], sb[:, 0:96], AF.Exp, [sk], [pk_], scale=0.125)
                    tt("dve", p_[:, 0:96], p_[:, 0:96], msk12[:, :], ALU.mult, [pk_, "msk12"], [pk_])
                    act(pN[0:4, 32 * hh:32 * hh + 32], sn[0:4, 0:32], AF.Exp, [snk], ["pN"], scale=0.125)
                    tt("dve", pN[0:4, 32 * hh:32 * hh + 32], pN[0:4, 32 * hh:32 * hh + 32], wm[0:4, :], ALU.mult, ["pN", "wm"], ["pN"])
                    for hp in range(8):
                        vc0 = hp * 128 + 64 * hh
                        oc = slice(4 * hp, 4 * hp + 4)
                        mm(ps[4][hr, oc], Vc[:, 0, vc0:vc0 + 64], p_[:, 12 * hp:12 * hp + 4], False, False, ["Vc", pk_], ["ps4"], skip=True)
                        mm(ps[5][hr, oc], ones64[:, :], p_[:, 12 * hp:12 * hp + 4], False, False, ["ones64", pk_], ["ps5"], skip=True)
                        for t in range(4):
                            for (sl, cc) in ((1 + t, 12 * hp + 4 + t), (5 + t, 12 * hp + 8 + t)):
                                mm(ps[4][hr, 4 * hp + t:4 * hp + t + 1], Vc[:, sl, vc0:vc0 + 64], p_[:, cc:cc + 1], False, False, ["Vc", pk_], ["ps4"], skip=True)
                                mm(ps[5][hr, 4 * hp + t:4 * hp + t + 1], ones64[:, :], p_[:, cc:cc + 1], False, False, ["ones64", pk_], ["ps5"], skip=True)
                        pn = pN[0:4, 32 * hh + 4 * hp:32 * hh + 4 * hp + 4]
                        mm(ps[4][hr, oc], Vsm[0:4, vc0:vc0 + 64], pn, False, False, ["Vsm", "pN"], ["ps4"], skip=True)
                        mm(ps[5][hr, oc], ones64[0:4, :], pn, False, False, ["ones64", "pN"], ["ps5"], skip=True)
                S.op("dve", lambda e: e.reciprocal(out=rD[:, 0:32], in_=ps[5][:, 0:32]), reads=["ps5"], writes=["rD"])
                tt("dve", oT[:, :, 1152:1156], ps[4][:, 0:32].rearrange("p (h t) -> p h t", h=8), rD[:, 0:32].rearrange("p (h t) -> p h t", h=8), ALU.mult, ["ps4", "rD"], ["mixT"])
                S.emit(nc, esem, dsem)
                chk(4)

            with ExitStack() as st:
                wb = [T(st, "wo%d" % i, [128, 16, 512], BF16) for i in range(2)]
                G1 = [T(st, "G1_%d" % i, [128, 2048], F32) for i in range(2)]
                xin = [T(st, "xin%d" % i, [128, 512], F32) for i in range(3)]
                tq = [T(st, "tq%d" % i, [128, 512], F32) for i in range(2)]
                dma("sp", G1[0][:, :], GS[0, :, :], reads=["GS"], writes=["G1"])
                dma("sp", G1[1][:, :], GS[1, :, :], reads=["GS"], writes=["G1"])
                tiles4 = [(xm[m * 128:(m + 1) * 128, :], 128, m * 128, m * 128, 0) for m in range(9)] + [(xs, 4, 1152, 1280, 1)]
                nb = 0
                def ld4(cb_):
                    load_w(wb[cb_ % 2], "wo%d" % (cb_ % 2), lambda r0, r1, c0_, c1_: w_out[r0:r1, c0_:c1_], [(0, 512, 512 * cb_)])
                ld4(0)
                seq4 = [(cb_, tl_) for cb_ in range(4) for tl_ in tiles4]

                def ldx(n_):
                    cb_, (src_, P_, _c, _r, _g) = seq4[n_]
                    dma("sp", xin[n_ % 3][0:P_, :], src_[:, 512 * cb_:512 * cb_ + 512], writes=["xin%d" % (n_ % 3)])
                ldx(0); ldx(1)
                for cb in range(4):
                    w = wb[cb % 2]; wk = "wo%d" % (cb % 2)
                    if cb + 1 < 4:
                        ld4(cb + 1)
                    for (src, P, col0, row0, grp) in tiles4:
                        pb = ps[nb % 4]; pk = "ps%d" % (nb % 4)
                        xi = xin[nb % 3]; xk = "xin%d" % (nb % 3); tq_ = tq[nb % 2]; tk_ = "tq%d" % (nb % 2)
                        if nb + 2 < len(seq4):
                            ldx(nb + 2)
                        nb += 1
                        for k in range(16):
                            mm(pb[0:P, :], (aT if k < 8 else oT)[:, k % 8, col0:col0 + P], w[:, k, :], k == 0, k == 15, ["mixT", wk], [pk])
                        tt("dve", tq_[0:P, :], pb[0:P, :], G1[grp][0:P, 512 * cb:512 * cb + 512], ALU.mult, [pk, "G1"], [tk_])
                        tt("pool", tq_[0:P, :], tq_[0:P, :], xi[0:P, :], ALU.add, [tk_, xk], [tk_])
                        dma("pool", XM_s[row0:row0 + P, 512 * cb:512 * cb + 512], tq_[0:P, :], reads=[tk_], writes=["XM_s"])
                S.emit(nc, esem, dsem)
                chk(5)

        with ExitStack() as stF:
            actT = T(stF, "actT", [128, 44, 1030], BF16)
            with ExitStack() as stH:
                h2T = T(stH, "h2T", [128, 16, 1030], BF16)
                with ExitStack() as st:
                    xts = [T(st, "xu%d" % i, [128, 2048], F32) for i in range(2)]
                    sqts = [T(st, "squ%d" % i, [128, 2048], BF16) for i in range(2)]; xbs = [T(st, "xbu%d" % i, [128, 2048], BF16) for i in range(2)]
                    specs = [dict(src=XM_s[0:128, :], P=128, grp=0, col0=0, hkey="h2T", c0=126, n=2, src_reads=["XM_s"])]
                    for m in range(1, 9):
                        specs.append(dict(src=XM_s[m * 128:(m + 1) * 128, :], P=128, grp=0, col0=2 + 128 * (m - 1), hkey="h2T", src_reads=["XM_s"]))
                    specs.append(dict(src=XM_s[1280:1284, :], P=4, grp=1, col0=1026, hkey="h2T", src_reads=["XM_s"]))
                    norm_loop((xts, sqts, xbs), specs, A2T, 48, h2T)
                    S.emit(nc, esem, dsem)
                    chk(6)
                with ExitStack() as st:
                    wb = [T(st, "wu%d" % i, [128, 16, 512], BF16) for i in range(2)]
                    upg = [T(st, "upg%d" % i, [128, 1032], F32) for i in range(2)]
                    upv = [T(st, "upv%d" % i, [128, 1032], F32) for i in range(2)]
                    ug_l = [T(st, "ug%d" % i, [128, 1030], F32) for i in range(2)]; uv_l = [T(st, "uv%d" % i, [128, 1030], F32) for i in range(2)]
                    upsel = T(st, "upsel", [128, 88, 4], F32)
                    stft = T(st, "stft", [2, 512], F32); stfT = T(st, "stfT", [128, 88, 2], F32)
                    fco = [T(st, "fco%d" % i, [4, 512], F32) for i in range(1)]
                    for c4 in range(0, 88, 4):
                        dma("sp", stft[:, :], stf[:, c4 * 128:(c4 + 4) * 128], writes=["stft"])
                        for ch in range(c4, c4 + 4):
                            tr(ps[0][:, ch * 2:ch * 2 + 2], stft[0:2, (ch - c4) * 128:(ch - c4 + 1) * 128], identf[0:2, 0:2], ["stft", "identf"], ["ps0"])
                    cp("dve", stfT[:, :, :], ps[0][:, 0:176].rearrange("p (c t) -> p c t", t=2), ["ps0"], ["stfT"])
                    dma("sp", cas[0:26, :], sta[4:30, :])
                    groups = ((0, 512), (512, 512), (1024, 6))

                    wus = [T(st, "wus%d" % i, [128, 2, 512], F32) for i in range(2)]
                    nld5 = [0]

                    def ld5_dma(jb_, k2):
                        sb_ = wus[k2 % 2]; sk_ = "wus%d" % (k2 % 2)
                        for (d0, s0_) in ((0, 256 * jb_), (256, 5632 + 256 * jb_)):
                            dma("sp", sb_[:, :, d0:d0 + 256], w_up[256 * k2:256 * k2 + 256, s0_:s0_ + 256].rearrange("(k p) n -> p k n", p=128), writes=[sk_])

                    def ld5_cast(jb_, k2):
                        sb_ = wus[k2 % 2]; sk_ = "wus%d" % (k2 % 2)
                        cp("act", wb[jb_ % 2][:, 2 * k2:2 * k2 + 2, :], sb_[:, :, :], [sk_], ["wu%d" % (jb_ % 2)])

                    def ld5(jb_):
                        for k2 in range(8):
                            ld5_dma(jb_, k2)
                            ld5_cast(jb_, k2)

                    def mk5(u, jb, sub, isv):
                        w = wb[jb % 2]; wk = "wu%d" % (jb % 2)
                        up = (upv if isv else upg)[sub]; uk = ("upv" if isv else "upg") + str(sub)
                        ch = (44 if isv else 0) + 2 * jb + sub
                        wc0 = isv * 256 + sub * 128
                        banks = [(3 * u + g_) % 6 for g_ in range(3)]
                        ucv = uv_l[sub] if isv else ug_l[sub]; k_ = ("uv%d" if isv else "ug%d") % sub

                        q_ = 2 * sub + isv

                        def s0():
                            if jb + 1 < 22:
                                ld5_dma(jb + 1, 2 * q_)
                                ld5_dma(jb + 1, 2 * q_ + 1)
                            for gi_, (g0, gn_) in enumerate(groups):
                                pb = ps[banks[gi_]]; pk = "ps%d" % banks[gi_]
                                for k in range(16):
                                    mm(pb[:, 0:gn_], w[:, k, wc0:wc0 + 128], h2T[:, k, g0:g0 + gn_], k == 0, k == 15, [wk, "h2T"], [pk])

                        def s1():
                            if jb + 1 < 22:
                                ld5_cast(jb + 1, 2 * q_)
                                ld5_cast(jb + 1, 2 * q_ + 1)
                            for gi_, (g0, gn_) in enumerate(groups):
                                pb = ps[banks[gi_]]; pk = "ps%d" % banks[gi_]
                                if gn_ == 512:
                                    cp("act", up[:, g0:g0 + 512], pb[:, 0:512], [pk], [uk])
                                else:
                                    cp("act", up[:, 1024:1026], pb[:, 0:2], [pk], [uk])
                                    cp("act", up[:, 1028:1032], pb[:, 2:6], [pk], [uk])
                            tt("pool", up[:, 0:2], up[:, 0:2], cst[:, VROW:VROW + 2], ALU.mult, [uk, "cst"], [uk])
                            cp("pool", up[:, 1026:1028], stfT[:, ch, :], ["stfT"], [uk])
                            cp("pool", upsel[:, ch, :].rearrange("p (a b) -> p a b", a=2), up[:, 1024:1032].rearrange("p (a b) -> p a b", b=2)[:, 0:4:3, :], [uk], ["upsel"])
                            fw = FCW + ch * 3
                            ts("dve", ucv[:, :], up[:, 0:1030], cst[:, fw:fw + 1], cst[:, FCB + ch:FCB + ch + 1], ALU.mult, ALU.add, [uk, "cst"], [k_])
                            stt(ucv[:, :], up[:, 1:1031], cst[:, fw + 1:fw + 2], ucv[:, :], ALU.mult, ALU.add, [uk, "cst", k_], [k_])
                            stt(ucv[:, :], up[:, 2:1032], cst[:, fw + 2:fw + 3], ucv[:, :], ALU.mult, ALU.add, [uk, "cst", k_], [k_])

                        def s2():
                            act(ug_l[sub][:, :], ug_l[sub][:, :], AF.Silu, ["ug%d" % sub], ["ug%d" % sub])
                            tt("pool", actT[:, 2 * jb + sub, :], ug_l[sub][:, :], uv_l[sub][:, :], ALU.mult, ["ug%d" % sub, "uv%d" % sub], ["actT"])

                        return [s0, s1, s2 if isv else None]

                    ld5(0)
                    items5 = []
                    for jb in range(22):
                        for sub in range(2):
                            for isv in range(2):
                                items5.append(mk5(len(items5), jb, sub, isv))
                    pipeline(items5, [0, 1, 2])
                    for r4 in range(22):
                        pb = ps[4 + r4 % 2]; pk = "ps%d" % (4 + r4 % 2)
                        for i4 in range(4):
                            ch = r4 * 4 + i4
                            tr(pb[0:4, i4 * 128:(i4 + 1) * 128], upsel[:, ch, :], identf[:, :], ["upsel", "identf"], [pk])
                        f_ = fco[0]; fk = "fco0"
                        cp("act", f_[:, :], pb[0:4, :], [pk], [fk])
                        dma("sp", fcp[:, r4 * 512:(r4 + 1) * 512], f_[0:2, :], reads=[fk])
                        dma("sp", fcs[:, r4 * 512:(r4 + 1) * 512], f_[2:4, :], reads=[fk])
                    S.emit(nc, esem, dsem)
                    chk(7)
            with ExitStack() as st:
                wd = [T(st, "wd%d" % i, [128, 44, 256], BF16) for i in range(2)]
                G2 = [T(st, "G2_%d" % i, [128, 2048], F32) for i in range(2)]
                xin = [T(st, "xmi%d" % i, [128, 256], F32) for i in range(2)]
                tq = [T(st, "ty%d" % i, [128, 256], F32) for i in range(2)]
                dma("sp", G2[0][:, :], GS[2, :, :], reads=["GS"], writes=["G2"])
                dma("sp", G2[1][:, :], GS[3, :, :], reads=["GS"], writes=["G2"])
                tiles6 = [(128, 128 * m, 128 * (m + 1), yp[m * 128:(m + 1) * 128, :], 0) for m in range(8)] + [(4, 1026, 1280, yso, 1)]
                nb = 0
                wds = [T(st, "wds%d" % i, [128, 11, 256], F32) for i in range(2)]
                nld6 = [0]

                def ld6(cb_):
                    for q4 in range(4):
                        sb_ = wds[nld6[0] % 2]; sk_ = "wds%d" % (nld6[0] % 2); nld6[0] += 1
                        dma("act", sb_[:, :, :], w_down[1408 * q4:1408 * q4 + 1408, 256 * cb_:256 * cb_ + 256].rearrange("(k p) n -> p k n", p=128), writes=[sk_])
                        cp("act", wd[cb_ % 2][:, 11 * q4:11 * q4 + 11, :], sb_[:, :, :], [sk_], ["wd%d" % (cb_ % 2)])
                ld6(0)
                for cb in range(8):
                    w = wd[cb % 2]; wk = "wd%d" % (cb % 2)
                    if cb + 1 < 8:
                        ld6(cb + 1)
                    for (P, acol, xrow, ydst, grp) in tiles6:
                        pb = ps[nb % 4]; pk = "ps%d" % (nb % 4)
                        xi = xin[nb % 2]; xk = "xmi%d" % (nb % 2); tq_ = tq[nb % 2]; tk_ = "ty%d" % (nb % 2); nb += 1
                        dma("sp", xi[0:P, :], XM_s[xrow:xrow + P, 256 * cb:256 * cb + 256], reads=["XM_s"], writes=[xk])
                        for k in range(44):
                            mm(pb[0:P, 0:256], actT[:, k, acol:acol + P], w[:, k, :], k == 0, k == 43, ["actT", wk], [pk])
                        tt("dve", tq_[0:P, :], pb[0:P, 0:256], G2[grp][0:P, 256 * cb:256 * cb + 256], ALU.mult, [pk, "G2"], [tk_])
                        tt("pool", tq_[0:P, :], tq_[0:P, :], xi[0:P, :], ALU.add, [tk_, xk], [tk_])
                        dma("sp", ydst[:, 256 * cb:256 * cb + 256], tq_[0:P, :], reads=[tk_])
                S.finish()
                S.emit(nc, esem, dsem)
    return nc


_NC = None
STOP_AFTER = None
DEBUG_CORES = None


class _Stop(Exception):
    pass


def _host_consts():
    half = 32
    inv = (np.float32(10000.0) ** (-(np.arange(half, dtype=np.float32) / np.float32(half)))).astype(np.float32)
    band = np.zeros((128, 256), np.float32)
    ki = np.arange(128)[:, None]; c = np.arange(256)[None, :]
    band[(c >= ki) & (c <= ki + 128)] = 1.0
    m12 = np.ones((128, 12), np.float32)
    m12[:, 0:4] = (np.arange(128)[:, None] >= np.arange(4)[None, :]).astype(np.float32)
    msk12 = np.tile(m12, (1, 8))
    tq = np.arange(4)
    wmat = (tq[:, None] <= tq[None, :]).astype(np.float32) + 2.0 * (tq[:, None] == tq[None, :]).astype(np.float32)
    wm = np.tile(wmat, (1, 8))
    sel = np.zeros((2, 256), np.float32); sel[0, 0:128] = 1.0; sel[1, 128:256] = 1.0
    return inv, band.astype(ml_dtypes.bfloat16), msk12.astype(ml_dtypes.bfloat16), wm.astype(ml_dtypes.bfloat16), sel


def kernel(x_prompt, x_sample, cache_win_k, cache_win_v, state_conv_a, state_ffn_conv, c_prompt, c_sample,
           norm_mix_g, norm_ffn_g, w_ada, b_ada, w_in, conv_a_w, conv_a_b, ln_a_g, ln_a_b, q_norm_g, k_norm_g,
           w_out, w_up, ffn_conv_w, ffn_conv_b, w_down):
    global _NC
    f = lambda a: np.ascontiguousarray(np.asarray(a, dtype=np.float32))
    x_prompt = f(x_prompt); x_sample = f(x_sample)
    inv, band, msk12, wm, sel = _host_consts()
    if _NC is None:
        _NC = build_program()
    nc = _NC
    caw = f(conv_a_w)[0]; fcw = f(ffn_conv_w)[0]
    shared = dict(
        gn=np.stack([f(norm_mix_g)[0], f(norm_ffn_g)[0]]), bada=np.stack([f(b_ada)[0], f(b_ada)[0]]),
        band=band, msk12=msk12, wm=wm, sel=sel,
        identb=np.eye(128, dtype=np.float32).astype(ml_dtypes.bfloat16), identf=np.eye(128, dtype=np.float32),
        w_ada=f(w_ada)[0], w_in=f(w_in)[0], w_out=f(w_out)[0], w_up=f(w_up)[0], w_down=f(w_down)[0])
    cst0 = np.zeros((128, NCST), np.float32)
    cst0[:, CAW:CAW + 248] = caw.T.reshape(8, 128, 31).transpose(1, 0, 2).reshape(128, 248)
    cst0[:, CAB:CAB + 8] = f(conv_a_b)[0].reshape(8, 128).T
    cst0[:, LNG:LNG + 8] = f(ln_a_g)[0].reshape(8, 128).T
    cst0[:, LNB:LNB + 8] = f(ln_a_b)[0].reshape(8, 128).T
    cst0[:, FCW:FCW + 264] = fcw.T.reshape(88, 128, 3).transpose(1, 0, 2).reshape(128, 264)
    cst0[:, FCB:FCB + 88] = f(ffn_conv_b)[0].reshape(88, 128).T
    cst0[:, QG:QG + 64] = f(q_norm_g)[0][None, :]
    cst0[:, KG:KG + 64] = f(k_norm_g)[0][None, :]
    cst0[:, NHALF] = -0.5
    in_maps = []
    for i in range(8):
        b, c = i // 4, i % 4
        s = 1024 * c
        P0 = s - 128 - 2048
        pos = P0 + np.arange(3200)
        xall = np.zeros((3200, 2048), np.float32)
        ok = pos >= 0
        xall[ok] = x_prompt[b, pos[ok]]
        cst = cst0.copy()
        val = np.ones((26, 128), np.float32)
        val[0:25] = np.where(ok.reshape(25, 128), np.float32(1.0), np.float32(1e-30))
        cst[:, VAL:VAL + 26] = val.T
        cst[:, VROW:VROW + 2] = 1.0 if c > 0 else 0.0
        posf = np.concatenate([np.maximum(pos, 0), 16384 + np.arange(4), np.zeros(124, np.int64)]).astype(np.float32)
        ang = posf[:, None] * inv[None, :]
        ropec = np.cos(ang).astype(np.float32).reshape(26, 128, 32).transpose(1, 0, 2)
        ropes = np.sin(ang).astype(np.float32).reshape(26, 128, 32).transpose(1, 0, 2)
        m = dict(shared)
        m.update(xh=np.ascontiguousarray(xall[0:2048]), xm=np.ascontiguousarray(xall[2048:3200]), xs=x_sample[i],
                 cvec=np.stack([f(c_prompt)[b], f(c_sample)[i]]), cst=cst, ropec=np.ascontiguousarray(ropec), ropes=np.ascontiguousarray(ropes),
                 ck=f(cache_win_k)[0, i].reshape(2048, 1024), cv=f(cache_win_v)[0, i].reshape(2048, 1024),
                 sta=f(state_conv_a)[0, i], stf=f(state_ffn_conv)[0, i])
        in_maps.append(m)
    if DEBUG_CORES is not None:
        res = run_bass_kernel_spmd(nc, [in_maps[c] for c in DEBUG_CORES], core_ids=list(range(len(DEBUG_CORES))))
        return res.results, in_maps
    res = run_bass_kernel_spmd(nc, in_maps, core_ids=list(range(8)))
    R = res.results
    y_p = np.zeros((2, 4096, 2048), np.float32); y_s = np.zeros((8, 4, 2048), np.float32)
    wkp = np.zeros((1, 2, 2048, 16, 64), np.float32); wvp = np.zeros_like(wkp)
    cap = np.zeros((1, 2, 30, 1024), np.float32); fcp = np.zeros((1, 2, 2, 11264), np.float32)
    wks = np.zeros((1, 8, 2048, 16, 64), np.float32); wvs = np.zeros_like(wks)
    cas = np.zeros((1, 8, 30, 1024), np.float32); fcs = np.zeros((1, 8, 2, 11264), np.float32)
    for i in range(8):
        b, c = i // 4, i % 4
        r = R[i]
        y_p[b, 1024 * c:1024 * c + 1024] = r["yp"]
        y_s[i] = r["yso"]
        if c >= 2:
            wkp[0, b, 1024 * (c - 2):1024 * (c - 1)] = r["kp"].reshape(1024, 16, 64)
            wvp[0, b, 1024 * (c - 2):1024 * (c - 1)] = r["vp"].reshape(1024, 16, 64)
        if c == 3:
            cap[0, b] = r["cap"]; fcp[0, b] = r["fcp"]
        wks[0, i] = r["kso"].reshape(2048, 16, 64); wvs[0, i] = r["vso"].reshape(2048, 16, 64)
        cas[0, i] = r["cas"]; fcs[0, i] = r["fcs"]
    return (y_p, y_s, wkp, wvp, cap, fcp, wks, wvs, cas, fcs)
```

```python
import numpy as np
import ml_dtypes
from contextlib import ExitStack
import concourse.bass as bass
import concourse.mybir as mybir
from concourse.bass_utils import run_bass_kernel_spmd

F32 = mybir.dt.float32
BF16 = mybir.dt.bfloat16
AF = mybir.ActivationFunctionType
ALU = mybir.AluOpType
AX = mybir.AxisListType
ENGS = ("sp", "act", "dve", "pool", "pe")
ND = 44
DPOOL = {"sp": (0, 28), "pool": (28, 36), "act": (36, 44)}
EPS = 1e-6

CAW, CAB, LNG, LNB, FCW, FCB, QG, KG, VAL, NHALF, VROW, NCST = 0, 248, 256, 264, 272, 536, 624, 688, 752, 778, 780, 782


class Sched:
    def __init__(self):
        self.ops = {e: [] for e in ENGS}
        self.cnt = {e: 0 for e in ENGS}
        self.seen = {e: {} for e in ENGS}
        self.lastw = {}
        self.readers = {}
        self.dma_uses = [0] * ND
        self.dma_rr = {}

    def _need(self, eng, tok, waits):
        if tok is None:
            return
        k, v = tok
        if k == eng and eng == "pe":
            return
        if self.seen[eng].get(k, 0) >= v:
            return
        if waits.get(k, 0) < v:
            waits[k] = v

    def op(self, eng, fn, reads=(), writes=(), dma=False):
        excl = [r for r in reads if isinstance(r, str) and len(r) == 3 and r[:2] in ("ps", "pt")]
        if excl:
            reads = [r for r in reads if r not in excl]
            writes = list(writes) + excl
        waits = {}
        for r in reads:
            self._need(eng, self.lastw.get(r), waits)
        for w in writes:
            self._need(eng, self.lastw.get(w), waits)
            for k, v in self.readers.get(w, {}).items():
                self._need(eng, (k, v), waits)
        if dma:
            lo, hi = DPOOL[eng]
            j = self.dma_rr.get(eng, lo)
            self.dma_rr[eng] = lo + (j + 1 - lo) % (hi - lo)
            if self.dma_uses[j] > 0:
                self._need(eng, (("d", j), 16 * self.dma_uses[j]), waits)
            self.dma_uses[j] += 1
            tok = (("d", j), 16 * self.dma_uses[j])
        else:
            self.cnt[eng] += 1
            tok = (eng, self.cnt[eng])
        for k, v in waits.items():
            self.seen[eng][k] = v
        self.ops[eng].append((list(waits.items()), fn, tok, dma))
        for r in reads:
            d = self.readers.setdefault(r, {})
            if d.get(tok[0], 0) < tok[1]:
                d[tok[0]] = tok[1]
        for w in writes:
            self.lastw[w] = tok
            self.readers[w] = {}
        return tok

    def finish(self):
        waits = {}
        for j in range(ND):
            if self.dma_uses[j] > 0:
                self._need("sp", (("d", j), 16 * self.dma_uses[j]), waits)
        for e in ENGS:
            if e != "sp" and self.cnt[e] > 0:
                self._need("sp", (e, self.cnt[e]), waits)
        for k, v in waits.items():
            self.seen["sp"][k] = v
        self.ops["sp"].append((list(waits.items()), None, None, False))

    def emit(self, nc, esem, dsem):
        def semof(k):
            return esem[k] if isinstance(k, str) else dsem[k[1]]

        def run(e, name):
            for waits, fn, tok, dma in self.ops[name]:
                for k, v in waits:
                    e.wait_ge(semof(k), v)
                if fn is None:
                    continue
                ins = fn(e)
                ins.then_inc(semof(tok[0]), 16 if dma else 1)
            self.ops[name] = []

        with nc.Block() as block:
            @block.sync
            def _(e):
                run(e, "sp")

            @block.scalar
            def _(e):
                run(e, "act")

            @block.vector
            def _(e):
                run(e, "dve")

            @block.gpsimd
            def _(e):
                run(e, "pool")

            @block.tensor
            def _(e):
                run(e, "pe")


def build_program():
    try:
        return _build()
    except _Stop as ex:
        return ex.nc


def _build():
    nc = bass.Bass("TRN2", target_bir_lowering=False)
    S = Sched()

    def din(name, shape, dt=F32):
        return nc.dram_tensor(name, list(shape), dt, kind="ExternalInput").ap()

    def dout(name, shape):
        return nc.dram_tensor(name, list(shape), F32, kind="ExternalOutput").ap()

    def dscr(name, shape, dt):
        return nc.dram_tensor(name, list(shape), dt, kind="Internal").ap()

    xh = din("xh", [2048, 2048]); xm = din("xm", [1152, 2048]); xs = din("xs", [4, 2048])
    cvec = din("cvec", [2, 2048]); gn = din("gn", [2, 2048]); bada = din("bada", [2, 12288])
    cst_d = din("cst", [128, NCST]); ropec_d = din("ropec", [128, 26, 32]); ropes_d = din("ropes", [128, 26, 32])
    band_d = din("band", [128, 256], BF16); msk12_d = din("msk12", [128, 96], BF16); wm_d = din("wm", [4, 32], BF16)
    identb_d = din("identb", [128, 128], BF16); identf_d = din("identf", [128, 128]); sel_d = din("sel", [2, 256])
    ck = din("ck", [2048, 1024]); cv = din("cv", [2048, 1024]); sta = din("sta", [30, 1024]); stf = din("stf", [2, 11264])
    w_ada = din("w_ada", [2048, 12288]); w_in = din("w_in", [2048, 5120]); w_out = din("w_out", [2048, 2048])
    w_up = din("w_up", [2048, 11264]); w_down = din("w_down", [5632, 2048])

    yp = dout("yp", [1024, 2048]); yso = dout("yso", [4, 2048]); kp = dout("kp", [1024, 1024]); vp = dout("vp", [1024, 1024])
    cap = dout("cap", [30, 1024]); fcp = dout("fcp", [2, 11264]); kso = dout("kso", [2048, 1024]); vso = dout("vso", [2048, 1024])
    cas = dout("cas", [30, 1024]); fcs = dout("fcs", [2, 11264])

    KT_s = dscr("KT_s", [8, 128, 3200], BF16)
    V_s = dscr("V_s", [3200, 1536], BF16)
    GS = dscr("GS", [4, 128, 2048], F32)
    XM_s = dscr("XM_s", [1284, 2048], F32)

    def chk(n):
        if STOP_AFTER is not None and n == STOP_AFTER:
            S.finish()
            S.emit(nc, esem, dsem)
            ex = _Stop()
            ex.nc = nc
            raise ex

    with ExitStack() as st0:
        def T(stk, name, shape, dt):
            return stk.enter_context(nc.sbuf_tensor("sb_" + name, list(shape), dt))

        esem = {e: st0.enter_context(nc.semaphore("s_" + e)) for e in ENGS}
        dsem = [st0.enter_context(nc.semaphore("d%d" % j)) for j in range(ND)]
        ps = [st0.enter_context(nc.psum_tensor("ps%d" % i, [128, 512], F32)) for i in range(6)]
        pt = [st0.enter_context(nc.psum_tensor("pt%d" % i, [128, 1024], BF16)) for i in range(2)]

        identb = T(st0, "identb", [128, 128], BF16); identf = T(st0, "identf", [128, 128], F32)
        cst = T(st0, "cst", [128, NCST], F32); band = T(st0, "band", [128, 256], BF16)
        msk12 = T(st0, "msk12", [128, 96], BF16); wm = T(st0, "wm", [4, 32], BF16)
        sel = T(st0, "sel", [2, 256], F32); onesf = T(st0, "onesf", [128, 128], F32); ones64 = T(st0, "ones64", [128, 64], BF16)
        modT = T(st0, "modT", [128, 96, 2], F32); gnT = T(st0, "gnT", [128, 16, 2], F32)
        A1T = T(st0, "A1T", [128, 16, 2], F32); A2T = T(st0, "A2T", [128, 16, 2], F32)
        ss = T(st0, "ss", [128, 8], F32); rstd = T(st0, "rstd", [128, 8], F32)

        def dma(eng, out, in_, reads=(), writes=()):
            S.op(eng, lambda e: e.dma_start(out=out, in_=in_), reads=reads, writes=writes, dma=True)

        def mm(out, lhsT, rhs, start, stop, reads, writes, skip=False):
            if skip:
                S.op("pe", lambda e: e.matmul(out=out, lhsT=lhsT, rhs=rhs, start=start, stop=stop, skip_group_check=True), reads=reads, writes=writes)
            else:
                S.op("pe", lambda e: e.matmul(out=out, lhsT=lhsT, rhs=rhs, start=start, stop=stop), reads=reads, writes=writes)

        def tr(out, in_, ident, reads, writes):
            S.op("pe", lambda e: e.transpose(out=out, in_=in_, identity=ident), reads=reads, writes=writes)

        def act(out, in_, func, reads, writes, scale=None, bias=None, accum_out=None):
            kw = {}
            if accum_out is not None:
                kw["accum_out"] = accum_out
            if scale is not None:
                kw["scale"] = scale
            if bias is not None:
                kw["bias"] = bias
            S.op("act", lambda e: e.activation(out=out, in_=in_, func=func, **kw), reads=reads, writes=writes)

        def tt(eng, out, in0, in1, op, reads, writes):
            S.op(eng, lambda e: e.tensor_tensor(out=out, in0=in0, in1=in1, op=op), reads=reads, writes=writes)

        def ts(eng, out, in0, s1, s2, op0, op1, reads, writes):
            if s2 is None:
                S.op(eng, lambda e: e.tensor_scalar(out=out, in0=in0, scalar1=s1, scalar2=None, op0=op0), reads=reads, writes=writes)
            else:
                S.op(eng, lambda e: e.tensor_scalar(out=out, in0=in0, scalar1=s1, scalar2=s2, op0=op0, op1=op1), reads=reads, writes=writes)

        def stt(out, in0, scalar, in1, op0, op1, reads, writes):
            S.op("dve", lambda e: e.scalar_tensor_tensor(out=out, in0=in0, scalar=scalar, in1=in1, op0=op0, op1=op1), reads=reads, writes=writes)

        def cp(eng, out, in_, reads, writes):
            if eng == "act":
                act(out, in_, AF.Copy, reads, writes)
            else:
                S.op(eng, lambda e: e.tensor_copy(out=out, in_=in_), reads=reads, writes=writes)

        def memset(eng, ap, val, writes):
            S.op(eng, lambda e: e.memset(ap, val), writes=writes)

        def rsq(out, in_, P, n, reads, writes):
            tt("pool", out, in_, cst[0:P, NHALF:NHALF + 1].to_broadcast([P, n]), ALU.pow, list(reads) + ["cst"], writes)

        for t_, d_, k_ in ((identb, identb_d, "identb"), (identf, identf_d, "identf"), (cst, cst_d, "cst"), (band, band_d, "band"),
                           (msk12, msk12_d, "msk12"), (wm, wm_d, "wm"), (sel, sel_d, "sel")):
            dma("sp", t_[:], d_, writes=[k_])
        memset("dve", onesf[:], 1.0, ["onesf"])
        memset("dve", ones64[:], 1.0, ["ones64"])

        sT = T(st0, "sT", [128, 16, 2], BF16)

        def ada_stages(cb, wb, tmp, gtl, badp, pmm, psel, ptr_, tr_base, cnt, wst):
            w = wb[cb % 2]; wk = "wb%d" % (cb % 2)
            pb = ps[pmm[cb % 2]]; pk = "ps%d" % pmm[cb % 2]
            tp = tmp[cb % 2]; tk = "tmp%d" % (cb % 2)
            bp = badp[cb % 4]; bk = "badp%d" % (cb % 4)

            def L():
                for q4 in range(4):
                    sb_ = wst[q4 % 2]; sk_ = "wst%d" % (q4 % 2)
                    dma("sp", sb_[:, :, :], w_ada[512 * q4:512 * q4 + 512, cb * 512:(cb + 1) * 512].rearrange("(k p) n -> p k n", p=128), writes=[sk_])
                    cp("act", w[:, 4 * q4:4 * q4 + 4, :], sb_[:, :, :], [sk_], [wk])
                dma("sp", bp[:, :], bada[:, cb * 512:(cb + 1) * 512], writes=[bk])

            def M():
                for k in range(16):
                    mm(pb[0:2, :], sT[:, k, :], w[:, k, :], k == 0, k == 15, ["sT", wk], [pk])

            def E():
                tt("dve", tp[:], pb[0:2, :], bp[:, :], ALU.add, [pk, bk], [tk])
                for j in range(4):
                    c0 = (cb * 4 + j) * 2 - tr_base
                    tr(ps[ptr_][:, c0:c0 + 2], tp[0:2, j * 128:(j + 1) * 128], identf[0:2, 0:2], [tk, "identf"], ["ps%d" % ptr_])
                gi_ = cb // 4
                if gi_ in (2, 5):
                    for grp in range(2):
                        mm(ps[psel][:], sel[0:2, grp * 128:(grp + 1) * 128], tp[0:2, :], True, True, ["sel", tk], ["ps%d" % psel])
                        g = gtl[cnt[0] % 2]; gk = "gtl%d" % (cnt[0] % 2); cnt[0] += 1
                        cp("act", g[:], ps[psel][:], ["ps%d" % psel], [gk])
                        which = (0 if gi_ == 2 else 2) + grp
                        cc = (cb % 4) * 512
                        dma("sp", GS[which, :, cc:cc + 512], g[:], reads=[gk], writes=["GS"])

            return L, M, E

        with ExitStack() as st:
            cvt = T(st, "cvt", [2, 2048], F32); cvb = T(st, "cvb", [2, 2048], BF16)
            gnt = T(st, "gnt", [2, 2048], F32)
            wb = [T(st, "wb%d" % i, [128, 16, 512], BF16) for i in range(2)]
            tmp = [T(st, "tmp%d" % i, [2, 512], F32) for i in range(2)]
            gtl = [T(st, "gtl%d" % i, [128, 512], F32) for i in range(2)]
            badp = [T(st, "badp%d" % i, [2, 512], F32) for i in range(4)]
            dma("sp", cvt[:], cvec, writes=["cvt"]); dma("sp", gnt[:], gn, writes=["gnt"])
            act(cvb[:], cvt[:], AF.Silu, ["cvt"], ["cvb"])
            for k in range(16):
                tr(pt[0][:, 2 * k:2 * k + 2], cvb[0:2, k * 128:(k + 1) * 128], identb[0:2, 0:2], ["cvb", "identb"], ["pt0"])
            cp("dve", sT[:, :, :], pt[0][:, 0:32].rearrange("p (k t) -> p k t", t=2), ["pt0"], ["sT"])
            for k in range(16):
                tr(ps[3][:, 192 + 2 * k:194 + 2 * k], gnt[0:2, k * 128:(k + 1) * 128], identf[0:2, 0:2], ["gnt", "identf"], ["ps3"])
            cnt0 = [0]
            wst0 = [T(st, "wst%d" % i, [128, 4, 512], F32) for i in range(2)]
            stg = [ada_stages(cb, wb, tmp, gtl, badp, (0, 1), 2, 3, 0, cnt0, wst0) for cb in range(8)]
            stg[0][0]()
            for cb in range(8):
                if cb + 1 < 8:
                    stg[cb + 1][0]()
                stg[cb][1]()
                stg[cb][2]()
            cp("dve", modT[:, 0:32, :], ps[3][:, 0:64].rearrange("p (k t) -> p k t", t=2), ["ps3"], ["modT"])
            cp("dve", gnT[:, :, :], ps[3][:, 192:224].rearrange("p (k t) -> p k t", t=2), ["ps3"], ["gnT"])
            for grp in range(2):
                ts("dve", A1T[:, :, grp], modT[:, 16:32, grp], 1.0, None, ALU.add, None, ["modT"], ["A1T"])
                tt("dve", A1T[:, :, grp], A1T[:, :, grp], gnT[:, :, 0], ALU.mult, ["A1T", "gnT"], ["A1T"])
            S.emit(nc, esem, dsem)
            chk(0)

        def pipe_steps(items, lags, rev=True):
            n = len(items); L = max(lags)
            steps = []
            for t_ in range(n + L):
                def step(t_=t_):
                    for st_i, lg in (reversed(list(enumerate(lags))) if rev else list(enumerate(lags))):
                        ii = t_ - lg
                        if 0 <= ii < n and items[ii][st_i] is not None:
                            items[ii][st_i]()
                steps.append(step)
            return steps

        def pipeline(items, lags):
            for st_ in pipe_steps(items, lags):
                st_()

        def run_merged(stepsA, stepsB, off):
            for t_ in range(max(len(stepsA), len(stepsB) + off)):
                if t_ < len(stepsA):
                    stepsA[t_]()
                if 0 <= t_ - off < len(stepsB):
                    stepsB[t_ - off]()

        def norm_loop(stk_tiles, specs, AT, sh_lo, hT, defer=False):
            xts, sqts, xbs = stk_tiles

            def mk(i, sp_):
                b2 = i % 2
                xt = xts[b2]; sqt = sqts[b2]; xb = xbs[b2]
                xk = "xt%d" % (b2 if xts[0] is not xts[1] else 0); sk = "sqt%d" % b2; bk = "xb%d" % b2
                P = sp_["P"]; grp = sp_["grp"]; col0 = sp_["col0"]; c0 = sp_.get("c0", 0); n = sp_.get("n", P); hkey = sp_["hkey"]
                ssc = ss[0:P, b2:b2 + 1]; rsc = rstd[0:P, b2:b2 + 1]; ssk = "ss%d" % b2; rsk = "rstd%d" % b2

                def n0():
                    dma("sp", xt[0:P, :], sp_["src"], reads=sp_.get("src_reads", ()), writes=[xk])
                    act(sqt[0:P, :], xt[0:P, :], AF.Square, [xk], [sk, ssk], accum_out=ssc)
                    ts("dve", ssc, ssc, 1.0 / 2048, EPS, ALU.mult, ALU.add, [ssk], [ssk])
                    rsq(rsc, ssc, P, 1, [ssk], [rsk])
                    act(xb[0:P, :], xt[0:P, :], AF.Identity, [xk, rsk], [bk], scale=rsc)

                def n1():
                    for k in range(16):
                        b = k // 8
                        tr(pt[b][:, (k % 8) * 128:(k % 8) * 128 + P], xb[0:P, k * 128:(k + 1) * 128], identb[0:P, 0:P], [bk, "identb"], ["pt%d" % b])

                def n2():
                    for b in range(2):
                        src_ = pt[b][:, 0:1024].rearrange("p (k t) -> p k t", k=8)[:, :, c0:c0 + n]
                        dst = hT[:, 8 * b:8 * b + 8, col0:col0 + n]
                        tt("dve", dst, src_, AT[:, 8 * b:8 * b + 8, grp:grp + 1].to_broadcast([128, 8, n]), ALU.mult, ["pt%d" % b, "A1T", "A2T"], [hkey])
                        tt("pool", dst, dst, modT[:, sh_lo + 8 * b:sh_lo + 8 * b + 8, grp:grp + 1].to_broadcast([128, 8, n]), ALU.add, [hkey, "modT"], [hkey])

                return [n0, n1, n2]

            steps_ = pipe_steps([mk(i, sp_) for i, sp_ in enumerate(specs)], [0, 1, 2])
            if defer:
                return steps_
            for st_ in steps_:
                st_()

        def load_w(w, wk, src_rows_fn, ncolsets):
            for (d0, n, s0) in ncolsets:
                for q4 in range(4):
                    dma("pool", w[:, 4 * q4:4 * q4 + 4, d0:d0 + n], src_rows_fn(512 * q4, 512 * q4 + 512, s0, s0 + n).rearrange("(k p) n -> p k n", p=128), writes=[wk])

        def phase1(tiles, blocks, ntok, pname):
            with ExitStack() as st:
                hT = T(st, "hT" + pname, [128, 16, ntok], BF16)
                ropec = T(st, "ropec" + pname, [128, 26, 32], F32); ropes = T(st, "ropes" + pname, [128, 26, 32], F32)
                dma("sp", ropec[:], ropec_d, writes=["rope"]); dma("sp", ropes[:], ropes_d, writes=["rope2"])
                if pname == "a":
                    xts = [T(st, "xt%d" % i_ + pname, [128, 2048], F32) for i_ in range(2)]
                else:
                    xts = [T(st, "xt0" + pname, [128, 2048], F32)] * 2
                sqts = [T(st, "sqt%d" % i_ + pname, [128, 2048], BF16) for i_ in range(2)]; xbs = [T(st, "xb%d" % i_ + pname, [128, 2048], BF16) for i_ in range(2)]
                wb = [T(st, "wb%d" % i + pname, [128, 16, 512], BF16) for i in range(2)]
                sq5_l = [T(st, "sq5%d" % i + pname, [128, 512], F32) for i in range(2)]; qn_l = [T(st, "qn%d" % i + pname, [128, 512], F32) for i in range(2)]
                qr = [T(st, "qr%d" % i + pname, [128, 512], F32) for i in range(3)]
                t1_l = [T(st, "t1%d" % i + pname, [128, 256], F32) for i in range(2)]; t2_l = [T(st, "t2%d" % i + pname, [128, 256], F32) for i in range(2)]
                t3_l = [T(st, "t3%d" % i + pname, [128, 256], F32) for i in range(2)]; t4_l = [T(st, "t4%d" % i + pname, [128, 256], F32) for i in range(2)]
                kb_l = [T(st, "kb%d" % i + pname, [128, 512], BF16) for i in range(2)]
                ssq = T(st, "ssq" + pname, [128, 16], F32); rsq_ = T(st, "rsq_" + pname, [128, 24], F32)
                ktp = [T(st, "ktp%d" % i + pname, [128, 4, 128], BF16) for i in range(2)]
                vt = [T(st, "vt%d" % i + pname, [128, 512], F32) for i in range(2)]
                vb = [T(st, "vb%d" % i + pname, [128, 768], BF16) for i in range(2)]
                vv = T(st, "vv" + pname, [128, 64], BF16)
                sg_l = [T(st, "sg%d" % i + pname, [128, 256], F32) for i in range(2)]
                gt = [T(st, "gt%d" % i + pname, [128, 256], F32) for i in range(3)]
                gb_l = [T(st, "gb%d" % i + pname, [128, 256], BF16) for i in range(2)]
                nsteps = norm_loop((xts, sqts, xbs), [dict(src=tl["src"], P=tl["P"], grp=(1 if tl["kind"] == "samp" else 0), col0=tl["col0"], hkey=("hT", i)) for i, tl in enumerate(tiles)], A1T, 0, hT, defer=True)
                nb = 0

                wst = [T(st, "wsi%d" % i_ + pname, [128, 1, 512], F32) for i_ in range(2)]

                def piece_dma(bi_, pc):
                    kind_, j_ = blocks[bi_]
                    sb_ = wst[pc % 2]; sk_ = "wsi%d" % (pc % 2)
                    if kind_ == "glu":
                        colsets = [(0, 256, 256 * j_), (256, 256, 1024 + 256 * j_)]
                    else:
                        colsets = [(0, 512, {"q": 2048, "k": 3072, "v": 4096}[kind_] + 512 * j_)]
                    for (d0, n_, s0_) in colsets:
                        dma("sp", sb_[:, 0, d0:d0 + n_], w_in[128 * pc:128 * pc + 128, s0_:s0_ + n_], writes=[sk_])

                def piece_cast(bi_, pc):
                    cp("act", wb[bi_ % 2][:, pc, :], wst[pc % 2][:, 0, :], ["wsi%d" % (pc % 2)], ["wbuf%d" % (bi_ % 2)])

                def issue_load(bi_):
                    for pc in range(16):
                        piece_dma(bi_, pc)
                        piece_cast(bi_, pc)

                def mk_item(bi, blk, i, tl, nb, first, pos):
                    kind, j = blk
                    w = wb[bi % 2]; wk = "wbuf%d" % (bi % 2)
                    P = tl["P"]; t = tl["t"]; col0 = tl["col0"]
                    pb = ps[nb % 4]; pk = "ps%d" % (nb % 4); tb = nb % 2
                    sq5 = sq5_l[tb]; qn = qn_l[tb]; t1 = t1_l[tb]; t2 = t2_l[tb]; t3 = t3_l[tb]; t4 = t4_l[tb]; kb = kb_l[tb]; sg = sg_l[tb]; gb = gb_l[tb]
                    TB = str(tb)
                    ptb = ps[4 + tb][:, :].bitcast(BF16); ptk = "ps%d" % (4 + tb)
                    vcol = cst[0:P, VAL + t:VAL + t + 1]
                    g = gt[nb % 3]; gk = "gt%d" % (nb % 3)
                    o = qr[nb % 3]; ok_ = "qr%d" % (nb % 3)
                    r3 = nb % 3; RK = "rsq_%d" % r3
                    rs8 = rsq_[0:P, 8 * r3:8 * r3 + 8]

                    def stA():
                        if pname == "a" and bi in (1, 2) and pos in (0, 12):
                            q8 = (bi - 1) * 2 + (0 if pos == 0 else 1)
                            dma("pool", kso[511 * q8:511 * q8 + 511, :], ck[4 + 511 * q8:4 + 511 * q8 + 511, :])
                            dma("pool", vso[511 * q8:511 * q8 + 511, :], cv[4 + 511 * q8:4 + 511 * q8 + 511, :])
                        if bi + 1 < len(blocks):
                            if 1 <= pos <= 8:
                                piece_cast(bi + 1, 2 * pos - 2)
                                piece_cast(bi + 1, 2 * pos - 1)
                            if pos < 8:
                                piece_dma(bi + 1, 2 * pos)
                                piece_dma(bi + 1, 2 * pos + 1)
                        for k in range(16):
                            mm(pb[0:P, :], hT[:, k, col0:col0 + P], w[:, k, :], k == 0, k == 15, [("hT", i), wk], [pk])

                    def stB():
                        if kind == "glu":
                            act(sg[0:P, :], pb[0:P, 256:512], AF.Sigmoid, [pk], ["sg" + TB])
                            stt(g[0:P, :], pb[0:P, 0:256], vcol, sg[0:P, :], ALU.mult, ALU.mult, [pk, "sg" + TB, "cst"], [gk])
                            if tl["kind"] == "samp":
                                dma("sp", cas[26:30, 256 * j:256 * j + 256], g[0:4, :], reads=[gk])
                            elif tl.get("own") == 7:
                                dma("sp", cap[:, 256 * j:256 * j + 256], g[98:128, :], reads=[gk])
                        elif kind in ("q", "k"):
                            gcol = QG if kind == "q" else KG
                            q3 = qn[0:P, :].rearrange("p (h d) -> p h d", h=8)
                            tt("dve", q3, pb[0:P, :].rearrange("p (h d) -> p h d", h=8), cst[0:P, gcol:gcol + 64].unsqueeze(1).to_broadcast([P, 8, 64]), ALU.mult, [pk, "cst"], ["qn" + TB])
                            act(sq5[0:P, :], pb[0:P, :], AF.Square, [pk], ["sq5" + TB])
                            S.op("dve", lambda e: e.tensor_reduce(out=ssq[0:P, 8 * tb:8 * tb + 8], in_=sq5[0:P, :].rearrange("p (h d) -> p h d", h=8), axis=AX.X, op=ALU.add), reads=["sq5" + TB], writes=["ssq" + TB])
                            ts("dve", ssq[0:P, 8 * tb:8 * tb + 8], ssq[0:P, 8 * tb:8 * tb + 8], 1.0 / 64, EPS, ALU.mult, ALU.add, ["ssq" + TB], ["ssq" + TB])
                            rsq(rs8, ssq[0:P, 8 * tb:8 * tb + 8], P, 8, ["ssq" + TB], [RK])
                            o3 = o[0:P, :].rearrange("p (h d) -> p h d", h=8)
                            C = ropec[0:P, t, :].unsqueeze(1).to_broadcast([P, 8, 32]); Sn = ropes[0:P, t, :].unsqueeze(1).to_broadcast([P, 8, 32])
                            x1 = q3[:, :, 0:32]; x2 = q3[:, :, 32:64]
                            v1 = t1[0:P, :].rearrange("p (h d) -> p h d", h=8); v2 = t2[0:P, :].rearrange("p (h d) -> p h d", h=8)
                            v3 = t3[0:P, :].rearrange("p (h d) -> p h d", h=8); v4 = t4[0:P, :].rearrange("p (h d) -> p h d", h=8)
                            tt("dve", v1, x1, C, ALU.mult, ["qn" + TB, "rope"], ["t1" + TB])
                            tt("dve", v2, x2, Sn, ALU.mult, ["qn" + TB, "rope2"], ["t2" + TB])
                            tt("dve", o3[:, :, 0:32], v1, v2, ALU.subtract, ["t1" + TB, "t2" + TB], [ok_])
                            tt("pool", v3, x2, C, ALU.mult, ["qn" + TB, "rope"], ["t3" + TB])
                            tt("pool", v4, x1, Sn, ALU.mult, ["qn" + TB, "rope2"], ["t4" + TB])
                            tt("pool", o3[:, :, 32:64], v3, v4, ALU.add, ["t3" + TB, "t4" + TB], [ok_])
                        else:
                            if tl["kind"] == "samp":
                                v_ = vt[tb]; vk_ = "vt%d" % tb
                                cp("act", v_[0:4, :], pb[0:4, :], [pk], [vk_])
                                dma("sp", vso[2044:2048, 512 * j:512 * j + 512], v_[0:4, :], reads=[vk_])
                                cp("dve", Vsm[0:4, 512 * j:512 * j + 512], pb[0:4, :], [pk], ["Vsm"])
                            else:
                                if "own" in tl:
                                    v_ = vt[tb]; vk_ = "vt%d" % tb
                                    cp("act", v_[:, :], pb[:, :], [pk], [vk_])
                                    dma("sp", vp[tl["own"] * 128:(tl["own"] + 1) * 128, 512 * j:512 * j + 512], v_[:, :], reads=[vk_])
                                b_ = vb[tb]; bk_ = "vb%d" % tb
                                b4 = b_[:, :].rearrange("p (h s c) -> p h s c", h=4, s=3, c=64)
                                ts("dve", b4[:, :, 0:3:2, :], pb[:, :].rearrange("p (h s c) -> p h s c", h=4, s=2, c=64), vcol, None, ALU.mult, None, [pk, "cst"], [bk_])
                                cp("pool", b4[:, :, 1, :], vcol.unsqueeze(2).to_broadcast([128, 4, 64]), ["cst"], [bk_])
                                dma("sp", V_s[t * 128:(t + 1) * 128, 768 * j:768 * j + 768], b_[:, :], reads=[bk_], writes=["V_s"])

                    def stC1():
                        if kind == "glu":
                            cp("act", gb[0:P, :], g[0:P, :], [gk], ["gb" + TB])
                            for i2 in range(2):
                                tr(ptb[:, i2 * 128:i2 * 128 + P], gb[0:P, i2 * 128:(i2 + 1) * 128], identb[0:P, 0:P], ["gb" + TB, "identb"], [ptk])
                        else:
                            o3 = o[0:P, :].rearrange("p (h d) -> p h d", h=8)
                            tt("dve", o3, o3, rs8.unsqueeze(2).to_broadcast([P, 8, 64]), ALU.mult, [ok_, RK], [ok_])
                            if kind == "k":
                                if tl["kind"] == "samp":
                                    dma("sp", kso[2044:2048, 512 * j:512 * j + 512], o[0:4, :], reads=[ok_])
                                elif "own" in tl:
                                    dma("sp", kp[tl["own"] * 128:(tl["own"] + 1) * 128, 512 * j:512 * j + 512], o[:, :], reads=[ok_])
                            cp("act", kb[0:P, :], o[0:P, :], [ok_], ["kb" + TB])
                            for i2 in range(4):
                                tr(ptb[:, i2 * 128:i2 * 128 + P], kb[0:P, i2 * 128:(i2 + 1) * 128], identb[0:P, 0:P], ["kb" + TB, "identb"], [ptk])

                    def stC2():
                        if kind == "glu":
                            cc = (30 + col0) if tl["kind"] != "samp" else 1212
                            cp("dve", cinT[:, 2 * j:2 * j + 2, cc:cc + P], ptb[:, 0:256].rearrange("p (c t) -> p c t", c=2)[:, :, 0:P], [ptk], ["cinT"])
                        else:
                            src3 = ptb[:, 0:512].rearrange("p (c t) -> p c t", c=4)[:, :, 0:P]
                            if kind == "q":
                                cp("dve", QT[:, 4 * j:4 * j + 4, col0:col0 + P], src3, [ptk], ["QT"])
                            elif tl["kind"] == "samp":
                                cp("dve", KTs[:, 4 * j:4 * j + 4, 0:4], src3, [ptk], ["KTs"])
                            else:
                                kt_ = ktp[tb]; kk = "ktp%d" % tb
                                cp("dve", kt_[:, :, :], src3, [ptk], [kk])
                                dma("sp", KT_s[4 * j:4 * j + 4, :, t * 128:(t + 1) * 128].rearrange("c p n -> p c n"), kt_[:, :, :], reads=[kk], writes=["KT_s"])

                    return [stA, stB, stC1 if kind != "v" else None, stC2 if kind != "v" else None]

                issue_load(0)
                items = []
                for bi, blk in enumerate(blocks):
                    first = True
                    pos = 0
                    for i, tl in enumerate(tiles):
                        if blk[0] in ("glu", "q") and tl["kind"] == "halo":
                            continue
                        items.append(mk_item(bi, blk, i, tl, len(items), first, pos))
                        first = False
                        pos += 1
                run_merged(nsteps, pipe_steps(items, [0, 1, 3, 4], rev=False), 3)
                S.emit(nc, esem, dsem)
                chk(1 if pname == "a" else 15)

        halo_tiles = [dict(kind="halo", src=xh[t * 128:(t + 1) * 128, :], P=128, t=t, col0=t * 128) for t in range(16)]
        phase1(halo_tiles, [("k", 0), ("k", 1), ("v", 0), ("v", 1)], 2048, "a")
        with ExitStack() as stA:
            QT = T(stA, "QT", [128, 8, 1156], BF16)
            aT = T(stA, "aT", [128, 8, 1156], BF16)
            KTs = T(stA, "KTs", [128, 8, 4], BF16)
            Vsm = T(stA, "Vsm", [4, 1024], BF16)
            with ExitStack() as stB:
                cinT = T(stB, "cinT", [128, 8, 1216], BF16)
                memset("pool", cinT[:, :, 0:30], 0.0, ["cinT"])

                main_tiles = []
                for m in range(9):
                    d = dict(kind="main", src=xm[m * 128:(m + 1) * 128, :], P=128, t=16 + m, col0=m * 128)
                    if m >= 1:
                        d["own"] = m - 1
                    main_tiles.append(d)
                main_tiles.append(dict(kind="samp", src=xs, P=4, t=25, col0=1152))
                blocks_b = [("glu", 0), ("glu", 1), ("glu", 2), ("glu", 3), ("q", 0), ("q", 1), ("k", 0), ("k", 1), ("v", 0), ("v", 1)]
                import os as _os
                if _os.environ.get("DBG_KINDS"):
                    blocks_b = [b_ for b_ in blocks_b if b_[0] in _os.environ["DBG_KINDS"].split(",")]
                if _os.environ.get("DBG_NOSAMP"):
                    main_tiles = main_tiles[:-1]
                phase1(main_tiles, blocks_b, 1156, "b")

                with ExitStack() as st:
                    stt_ = T(st, "stt_", [30, 1024], F32)
                    gcs = T(st, "gcs", [128, 8, 1186], F32)
                    sqc = [T(st, "sqc%d" % i, [128, 512], F32) for i in range(2)]
                    mean = T(st, "mean", [128, 512], F32); var = T(st, "var", [128, 512], F32); m2 = T(st, "m2", [128, 512], F32)
                    tln = [T(st, "tln%d" % i, [128, 512], F32) for i in range(2)]
                    dma("sp", stt_[:, :], sta, writes=["stt_"])
                    for ch in range(8):
                        tr(ps[0][:, ch * 32:ch * 32 + 30], stt_[0:30, ch * 128:(ch + 1) * 128], identf[0:30, 0:30], ["stt_", "identf"], ["ps0"])
                    cp("dve", cinT[:, :, 1182:1212], ps[0][:, 0:256].rearrange("p (c t) -> p c t", c=8)[:, :, 0:30], ["ps0"], ["cinT"])
                    wb2 = [T(st, "wc%d" % i, [128, 16, 512], BF16) for i in range(2)]
                    tmp2 = [T(st, "tmc%d" % i, [2, 512], F32) for i in range(2)]
                    gtl2 = [T(st, "gtc%d" % i, [128, 512], F32) for i in range(2)]
                    badp2 = [T(st, "bdc%d" % i, [2, 512], F32) for i in range(4)]
                    cnt2 = [0]
                    wst2 = [T(st, "wsc%d" % i, [128, 4, 512], F32) for i in range(2)]
                    stg2 = [ada_stages(cb, wb2, tmp2, gtl2, badp2, (2, 3), 4, 5, 64, cnt2, wst2) for cb in range(8, 24)]
                    stg2[0][0](); stg2[1][0]()
                    NPE = 21
                    dg = [T(st, "dg%d" % i, [128, NPE, 128], BF16) for i in range(2)]
                    for ch in range(8):
                        acc = gcs[:, ch, :]; ak = ("gcs", ch)
                        d_ = dg[ch % 2]; dk = "dg%d" % (ch % 2)
                        tt("pool", d_[:, :, :], identb[:, :].unsqueeze(1).to_broadcast([128, NPE, 128]),
                           cst[:, CAW + ch * 31:CAW + ch * 31 + NPE].unsqueeze(2).to_broadcast([128, NPE, 128]), ALU.mult, ["identb", "cst"], [dk])
                        pbs = []
                        cbanks = [(ps[0], "ps0"), (ps[1], "ps1"), (pt[0][:, :].bitcast(F32), "pt0")]
                        for gix, (a0, n) in enumerate(((0, 512), (512, 512), (1024, 162))):
                            pb, pk = cbanks[gix]
                            for k in range(NPE):
                                mm(pb[:, 0:n], d_[:, k, :], cinT[:, ch, k + a0:k + a0 + n], k == 0, k == NPE - 1, [dk, "cinT"], [pk])
                            pbs.append((pb, pk, a0, n))
                        ts("dve", acc, cinT[:, ch, NPE:NPE + 1186], cst[:, CAW + ch * 31 + NPE:CAW + ch * 31 + NPE + 1], cst[:, CAB + ch:CAB + ch + 1], ALU.mult, ALU.add, ["cinT", "cst"], [ak])
                        for k in range(NPE + 1, 31):
                            stt(acc, cinT[:, ch, k:k + 1186], cst[:, CAW + ch * 31 + k:CAW + ch * 31 + k + 1], acc, ALU.mult, ALU.add, ["cinT", "cst", ak], [ak])
                        for (pb, pk, a0, n) in pbs:
                            tt("dve", gcs[:, ch, a0:a0 + n], pb[:, 0:n], gcs[:, ch, a0:a0 + n], ALU.add, [pk, ak], [ak])
                        for i_ in (2 * ch, 2 * ch + 1):
                            stg2[i_][1]()
                            if i_ >= 1:
                                stg2[i_ - 1][2]()
                            if i_ + 2 < 16:
                                stg2[i_ + 2][0]()
                    stg2[15][2]()
                    cp("dve", modT[:, 32:96, :], ps[5][:, 0:128].rearrange("p (k t) -> p k t", t=2), ["ps5"], ["modT"])
                    for grp in range(2):
                        ts("dve", A2T[:, :, grp], modT[:, 64:80, grp], 1.0, None, ALU.add, None, ["modT"], ["A2T"])
                        tt("dve", A2T[:, :, grp], A2T[:, :, grp], gnT[:, :, 1], ALU.mult, ["A2T", "gnT"], ["A2T"])
                    for (a0, n, m0) in ((0, 512, 0), (512, 512, 512), (1024, 128, 1024), (1182, 4, 1152)):
                        for ch in range(8):
                            mm(ps[0][:, 0:n], onesf[:, :], gcs[:, ch, a0:a0 + n], ch == 0, ch == 7, ["onesf", ("gcs", ch)], ["ps0"])
                        for ch in range(8):
                            sq_ = sqc[ch % 2]; sk_ = "sqc%d" % (ch % 2)
                            act(sq_[:, 0:n], gcs[:, ch, a0:a0 + n], AF.Square, [("gcs", ch)], [sk_])
                            mm(ps[1][:, 0:n], onesf[:, :], sq_[:, 0:n], ch == 0, ch == 7, ["onesf", sk_], ["ps1"])
                        ts("dve", mean[:, 0:n], ps[0][:, 0:n], 1.0 / 1024, None, ALU.mult, None, ["ps0"], ["mean"])
                        tt("dve", m2[:, 0:n], mean[:, 0:n], mean[:, 0:n], ALU.mult, ["mean"], ["m2"])
                        stt(var[:, 0:n], ps[1][:, 0:n], 1.0 / 1024, m2[:, 0:n], ALU.mult, ALU.subtract, ["ps1", "m2"], ["var"])
                        ts("dve", var[:, 0:n], var[:, 0:n], EPS, None, ALU.add, None, ["var"], ["var"])
                        act(var[:, 0:n], var[:, 0:n], AF.Sqrt, ["var"], ["var"])
                        S.op("dve", lambda e, n=n: e.reciprocal(out=var[:, 0:n], in_=var[:, 0:n]), reads=["var"], writes=["var"])
                        for ch in range(8):
                            tl_ = tln[ch % 2]; tk_ = "tln%d" % (ch % 2)
                            tt("dve", tl_[:, 0:n], gcs[:, ch, a0:a0 + n], mean[:, 0:n], ALU.subtract, [("gcs", ch), "mean"], [tk_])
                            tt("pool", tl_[:, 0:n], tl_[:, 0:n], var[:, 0:n], ALU.mult, [tk_, "var"], [tk_])
                            act(aT[:, ch, m0:m0 + n], tl_[:, 0:n], AF.Silu, [tk_, "cst"], ["mixT"], scale=cst[:, LNG + ch:LNG + ch + 1], bias=cst[:, LNB + ch:LNB + ch + 1])
                    S.emit(nc, esem, dsem)
                    chk(2)

            oT = T(stA, "oT", [128, 8, 1156], BF16)
            with ExitStack() as st:
                KTh = [T(st, "KTh%d" % i, [128, 3200], BF16) for i in range(2)]
                V1 = [T(st, "V1_%d" % i, [128, 10, 192], BF16) for i in range(2)]
                V4 = [T(st, "V4_%d" % i, [128, 4, 4, 192], BF16) for i in range(2)]
                V16 = [T(st, "V16_%d" % i, [128, 2, 16, 192], BF16) for i in range(2)]
                QZ = [T(st, "QZ%d" % i, [128, 2, 1152], BF16) for i in range(2)]
                pT = [T(st, "pT%d" % i, [128, 512], BF16) for i in range(4)]
                rD = T(st, "rD", [128, 512], F32)
                Kc = T(st, "Kc", [128, 9, 1024], BF16); Vc = T(st, "Vc", [128, 9, 1024], BF16)
                KcT = T(st, "KcT", [128, 8, 9, 128], BF16)
                pN = T(st, "pN", [4, 64], BF16)

                def load_hp(hp_):
                    bs = hp_ % 2; key = "hp%d" % bs; c0 = hp_ * 192
                    dma("sp", KTh[bs][:, :], KT_s[hp_, :, :], reads=["KT_s"], writes=[key])
                    dma("sp", V1[bs][:, :, :], V_s[15 * 128:25 * 128, c0:c0 + 192].rearrange("(s p) f -> p s f", p=128), reads=["V_s"], writes=[key])
                    for G in range(3, 6):
                        dma("sp", V4[bs][:, G - 3, :, :], V_s[512 * G:512 * G + 512, c0:c0 + 192].rearrange("(m r) f -> m r f", r=4), reads=["V_s"], writes=[key])
                    dma("sp", V4[bs][0:32, 3, :, :], V_s[3072:3200, c0:c0 + 192].rearrange("(m r) f -> m r f", r=4), reads=["V_s"], writes=[key])
                    dma("sp", V16[bs][:, 0, :, :], V_s[0:2048, c0:c0 + 192].rearrange("(m r) f -> m r f", r=16), reads=["V_s"], writes=[key])
                    dma("sp", V16[bs][0:72, 1, :, :], V_s[2048:3200, c0:c0 + 192].rearrange("(m r) f -> m r f", r=16), reads=["V_s"], writes=[key])
                    cp("pool", QZ[bs][0:64, 0, :], QT[0:64, hp_, 0:1152], ["QT"], ["qz%d" % bs])
                    cp("pool", QZ[bs][64:128, 1, :], QT[64:128, hp_, 0:1152], ["QT"], ["qz%d" % bs])

                for bs in range(2):
                    memset("pool", QZ[bs][64:128, 0, :], 0.0, ["qz%d" % bs])
                    memset("pool", QZ[bs][0:64, 1, :], 0.0, ["qz%d" % bs])
                for (dst, srcd, key) in ((Kc, ck, "Kc"), (Vc, cv, "Vc")):
                    dma("pool", dst[:, 0, :], srcd[1920:2048, :], writes=[key])
                    dma("pool", dst[:, 1:5, :], srcd[1536:2048, :].rearrange("(m t) f -> m t f", t=4), writes=[key])
                    dma("pool", dst[:, 5:9, :], srcd.rearrange("(m t) f -> m t f", t=16)[:, 0:4, :], writes=[key])
                load_hp(0)
                units = []
                for hp in range(8):
                    bsel = hp % 2
                    kth = KTh[bsel]; v1 = V1[bsel]; v4 = V4[bsel]; v16 = V16[bsel]; hk = "hp%d" % bsel; qz = QZ[bsel]; qk_ = "qz%d" % bsel
                    for gi in range(3):
                        G = 4 + gi
                        nqt = 512 if gi < 2 else 128
                        qb = 512 * gi
                        batches = []
                        ntq = nqt // 128
                        for kind in ("prev", "cur"):
                            tl_ = []
                            for qt in range(ntq):
                                Tq = 16 + 4 * gi + qt
                                Tk = Tq - 1 if kind == "prev" else Tq
                                tl_.append(dict(kc=slice(Tk * 128, Tk * 128 + 128), qc=slice(qb + 128 * qt, qb + 128 * qt + 128), oc=slice(128 * qt, 128 * qt + 128), V=v1[:, Tk - 15, :]))
                            batches.append(dict(nk=128, nq=128, mask=band[:, 128:256] if kind == "prev" else band[:, 0:128], tiles=tl_))
                        nq2 = nqt // 4
                        for kind in ("prev", "cur"):
                            tl_ = []
                            nk = 128 if (kind == "prev" or gi < 2) else 32
                            for r in range(4):
                                Gk = G - 1 if kind == "prev" else G
                                tl_.append(dict(kc=slice(512 * Gk + r, 512 * Gk + 4 * nk, 4), qc=slice(qb + r, qb + nqt, 4), oc=slice(r, nqt, 4), V=v4[0:nk, Gk - 3, r, :]))
                            mk_ = band[0:nk, 128:128 + nq2] if kind == "prev" else band[0:nk, 0:nq2]
                            batches.append(dict(nk=nk, nq=nq2, mask=mk_, tiles=tl_))
                        nq3 = nqt // 16
                        m0 = 32 * gi
                        for kind in ("prev", "cur"):
                            tl_ = []
                            nk = 128 if kind == "prev" else (m0 + nq3)
                            for r in range(16):
                                if kind == "prev":
                                    kc = slice(r, 2048, 16); H = 0
                                else:
                                    kc = slice(2048 + r, 2048 + 16 * nk, 16); H = 1
                                tl_.append(dict(kc=kc, qc=slice(qb + r, qb + nqt, 16), oc=slice(r, nqt, 16), V=v16[0:nk, H, r, :]))
                            mk_ = band[0:nk, 128 + m0:128 + m0 + nq3] if kind == "prev" else band[0:nk, m0:m0 + nq3]
                            batches.append(dict(nk=nk, nq=nq3, mask=mk_, tiles=tl_))
                        ulist = []
                        for bt in batches:
                            per = max(1, 256 // bt["nq"])
                            for c_ in range(0, len(bt["tiles"]), per):
                                ulist.append(dict(nk=bt["nk"], nq=bt["nq"], mask=bt["mask"], tiles=bt["tiles"][c_:c_ + per]))
                        for ui, un in enumerate(ulist):
                            units.append(dict(un=un, first=(ui == 0), last=(ui == len(ulist) - 1), hp=hp, gi=gi, nqt=nqt, qb=qb, kth=kth, hk=hk, qz=qz, qk_=qk_,
                                              pre=(hp + 1 if (gi == 0 and ui == 5 and hp + 1 < 8) else None)))

                def mk3(u, d):
                    un = d["un"]; nk = un["nk"]; nq = un["nq"]; tiles_ = un["tiles"]; ntl = len(tiles_); tot = 2 * ntl * nq
                    sb = ps[u % 4]; sk = "ps%d" % (u % 4); p_ = pT[u % 4]; pk_ = "pT%d" % (u % 4)
                    kth = d["kth"]; hk = d["hk"]; qz = d["qz"]; qk_ = d["qk_"]; hp = d["hp"]; nqt = d["nqt"]; qb = d["qb"]

                    def s0():
                        if d["pre"] is not None:
                            load_hp(d["pre"])
                        for ti, tl in enumerate(tiles_):
                            mm(sb[0:nk, 2 * ti * nq:2 * (ti + 1) * nq].rearrange("p (h q) -> p h q", h=2), kth[:, tl["kc"]], qz[:, :, tl["qc"]], True, True, [hk, qk_], [sk])

                    def s1():
                        act(p_[0:nk, 0:tot], sb[0:nk, 0:tot], AF.Exp, [sk], [pk_], scale=0.125)
                        p3 = p_[0:nk, 0:tot].rearrange("p (t q) -> p t q", q=nq)
                        tt("dve" if u % 2 == 0 else "pool", p3, p3, un["mask"].unsqueeze(1).to_broadcast([nk, 2 * ntl, nq]), ALU.mult, [pk_, "band"], [pk_])

                    def s2():
                        for ti, tl in enumerate(tiles_):
                            st_ = bool(d["first"] and ti == 0)
                            mm(ps[4][:, tl["oc"]], tl["V"][:, 0:128], p_[0:nk, 2 * ti * nq:(2 * ti + 1) * nq], st_, False, [hk, pk_], ["ps4"], skip=True)
                            mm(ps[5][:, tl["oc"]], tl["V"][:, 64:192], p_[0:nk, (2 * ti + 1) * nq:(2 * ti + 2) * nq], st_, False, [hk, pk_], ["ps5"], skip=True)
                        if d["last"]:
                            S.op("dve", lambda e: e.reciprocal(out=rD[64:128, 0:nqt], in_=ps[4][64:128, 0:nqt]), reads=["ps4"], writes=["rDa"])
                            tt("dve", oT[0:64, hp, qb:qb + nqt], ps[4][0:64, 0:nqt], rD[64:128, 0:nqt], ALU.mult, ["ps4", "rDa"], ["mixT"])
                            S.op("dve", lambda e: e.reciprocal(out=rD[0:64, 0:nqt], in_=ps[5][0:64, 0:nqt]), reads=["ps5"], writes=["rDb"])
                            tt("dve", oT[64:128, hp, qb:qb + nqt], ps[5][64:128, 0:nqt], rD[0:64, 0:nqt], ALU.mult, ["ps5", "rDb"], ["mixT"])

                    return [s0, s1, s2]

                pipeline([mk3(u, d) for u, d in enumerate(units)], [0, 2, 4])
                S.emit(nc, esem, dsem)
                chk(3)

                ntr = 0
                for ch in range(8):
                    for half in range(2):
                        pb = pt[ntr % 2]; pk = "pt%d" % (ntr % 2); ntr += 1
                        n_ = 5 if half == 0 else 4
                        for s_ in range(n_):
                            sl = half * 5 + s_
                            tr(pb[:, s_ * 128:(s_ + 1) * 128], Kc[:, sl, ch * 128:(ch + 1) * 128], identb[:, :], ["Kc", "identb"], [pk])
                        cp("dve" if ntr % 2 else "act", KcT[:, ch, half * 5:half * 5 + n_, :], pb[:, 0:n_ * 128].rearrange("p (s k) -> p s k", s=n_), [pk], ["KcT"])
                memset("dve", ps[4][:, :], 0.0, ["ps4"])
                memset("dve", ps[5][:, :], 0.0, ["ps5"])
                for hh in range(2):
                    hr = slice(64 * hh, 64 * hh + 64)
                    sb = ps[2 * hh]; sk = "ps%d" % (2 * hh); sn = ps[2 * hh + 1]; snk = "ps%d" % (2 * hh + 1)
                    p_ = pT[2 * hh]; pk_ = "pT%d" % (2 * hh)
                    for hp in range(8):
                        qs = QT[hr, hp, 1152:1156]
                        mm(sb[:, 12 * hp:12 * hp + 4], KcT[hr, hp, 0, :], qs, True, True, ["KcT", "QT"], [sk])
                        for t in range(4):
                            mm(sb[:, 12 * hp + 4 + t:12 * hp + 5 + t], KcT[hr, hp, 1 + t, :], QT[hr, hp, 1152 + t:1153 + t], True, True, ["KcT", "QT"], [sk])
                            mm(sb[:, 12 * hp + 8 + t:12 * hp + 9 + t], KcT[hr, hp, 5 + t, :], QT[hr, hp, 1152 + t:1153 + t], True, True, ["KcT", "QT"], [sk])
                        mm(sn[0:4, 4 * hp:4 * hp + 4], KTs[hr, hp, 0:4], qs, True, True, ["KTs", "QT"], [snk])
                    act(p_[:, 0:96], sb[:, 0:96], AF.Exp, [sk], [pk_], scale=0.125)
                    tt("dve", p_[:, 0:96], p_[:, 0:96], msk12[:, :], ALU.mult, [pk_, "msk12"], [pk_])
                    act(pN[0:4, 32 * hh:32 * hh + 32], sn[0:4, 0:32], AF.Exp, [snk], ["pN"], scale=0.125)
                    tt("dve", pN[0:4, 32 * hh:32 * hh + 32], pN[0:4, 32 * hh:32 * hh + 32], wm[0:4, :], ALU.mult, ["pN", "wm"], ["pN"])
                    for hp in range(8):
                        vc0 = hp * 128 + 64 * hh
                        oc = slice(4 * hp, 4 * hp + 4)
                        mm(ps[4][hr, oc], Vc[:, 0, vc0:vc0 + 64], p_[:, 12 * hp:12 * hp + 4], False, False, ["Vc", pk_], ["ps4"], skip=True)
                        mm(ps[5][hr, oc], ones64[:, :], p_[:, 12 * hp:12 * hp + 4], False, False, ["ones64", pk_], ["ps5"], skip=True)
                        for t in range(4):
                            for (sl, cc) in ((1 + t, 12 * hp + 4 + t), (5 + t, 12 * hp + 8 + t)):
                                mm(ps[4][hr, 4 * hp + t:4 * hp + t + 1], Vc[:, sl, vc0:vc0 + 64], p_[:, cc:cc + 1], False, False, ["Vc", pk_], ["ps4"], skip=True)
                                mm(ps[5][hr, 4 * hp + t:4 * hp + t + 1], ones64[:, :], p_[:, cc:cc + 1], False, False, ["ones64", pk_], ["ps5"], skip=True)
                        pn = pN[0:4, 32 * hh + 4 * hp:32 * hh + 4 * hp + 4]
                        mm(ps[4][hr, oc], Vsm[0:4, vc0:vc0 + 64], pn, False, False, ["Vsm", "pN"], ["ps4"], skip=True)
                        mm(ps[5][hr, oc], ones64[0:4, :], pn, False, False, ["ones64", "pN"], ["ps5"], skip=True)
                S.op("dve", lambda e: e.reciprocal(out=rD[:, 0:32], in_=ps[5][:, 0:32]), reads=["ps5"], writes=["rD"])
                tt("dve", oT[:, :, 1152:1156], ps[4][:, 0:32].rearrange("p (h t) -> p h t", h=8), rD[:, 0:32].rearrange("p (h t) -> p h t", h=8), ALU.mult, ["ps4", "rD"], ["mixT"])
                S.emit(nc, esem, dsem)
                chk(4)

            with ExitStack() as st:
                wb = [T(st, "wo%d" % i, [128, 16, 512], BF16) for i in range(2)]
                G1 = [T(st, "G1_%d" % i, [128, 2048], F32) for i in range(2)]
                xin = [T(st, "xin%d" % i, [128, 512], F32) for i in range(3)]
                tq = [T(st, "tq%d" % i, [128, 512], F32) for i in range(2)]
                dma("sp", G1[0][:, :], GS[0, :, :], reads=["GS"], writes=["G1"])
                dma("sp", G1[1][:, :], GS[1, :, :], reads=["GS"], writes=["G1"])
                tiles4 = [(xm[m * 128:(m + 1) * 128, :], 128, m * 128, m * 128, 0) for m in range(9)] + [(xs, 4, 1152, 1280, 1)]
                nb = 0
                def ld4(cb_):
                    load_w(wb[cb_ % 2], "wo%d" % (cb_ % 2), lambda r0, r1, c0_, c1_: w_out[r0:r1, c0_:c1_], [(0, 512, 512 * cb_)])
                ld4(0)
                seq4 = [(cb_, tl_) for cb_ in range(4) for tl_ in tiles4]

                def ldx(n_):
                    cb_, (src_, P_, _c, _r, _g) = seq4[n_]
                    dma("sp", xin[n_ % 3][0:P_, :], src_[:, 512 * cb_:512 * cb_ + 512], writes=["xin%d" % (n_ % 3)])
                ldx(0); ldx(1)
                for cb in range(4):
                    w = wb[cb % 2]; wk = "wo%d" % (cb % 2)
                    if cb + 1 < 4:
                        ld4(cb + 1)
                    for (src, P, col0, row0, grp) in tiles4:
                        pb = ps[nb % 4]; pk = "ps%d" % (nb % 4)
                        xi = xin[nb % 3]; xk = "xin%d" % (nb % 3); tq_ = tq[nb % 2]; tk_ = "tq%d" % (nb % 2)
                        if nb + 2 < len(seq4):
                            ldx(nb + 2)
                        nb += 1
                        for k in range(16):
                            mm(pb[0:P, :], (aT if k < 8 else oT)[:, k % 8, col0:col0 + P], w[:, k, :], k == 0, k == 15, ["mixT", wk], [pk])
                        tt("dve", tq_[0:P, :], pb[0:P, :], G1[grp][0:P, 512 * cb:512 * cb + 512], ALU.mult, [pk, "G1"], [tk_])
                        tt("pool", tq_[0:P, :], tq_[0:P, :], xi[0:P, :], ALU.add, [tk_, xk], [tk_])
                        dma("pool", XM_s[row0:row0 + P, 512 * cb:512 * cb + 512], tq_[0:P, :], reads=[tk_], writes=["XM_s"])
                S.emit(nc, esem, dsem)
                chk(5)

        with ExitStack() as stF:
            actT = T(stF, "actT", [128, 44, 1030], BF16)
            with ExitStack() as stH:
                h2T = T(stH, "h2T", [128, 16, 1030], BF16)
                with ExitStack() as st:
                    xts = [T(st, "xu%d" % i, [128, 2048], F32) for i in range(2)]
                    sqts = [T(st, "squ%d" % i, [128, 2048], BF16) for i in range(2)]; xbs = [T(st, "xbu%d" % i, [128, 2048], BF16) for i in range(2)]
                    specs = [dict(src=XM_s[0:128, :], P=128, grp=0, col0=0, hkey="h2T", c0=126, n=2, src_reads=["XM_s"])]
                    for m in range(1, 9):
                        specs.append(dict(src=XM_s[m * 128:(m + 1) * 128, :], P=128, grp=0, col0=2 + 128 * (m - 1), hkey="h2T", src_reads=["XM_s"]))
                    specs.append(dict(src=XM_s[1280:1284, :], P=4, grp=1, col0=1026, hkey="h2T", src_reads=["XM_s"]))
                    norm_loop((xts, sqts, xbs), specs, A2T, 48, h2T)
                    S.emit(nc, esem, dsem)
                    chk(6)
                with ExitStack() as st:
                    wb = [T(st, "wu%d" % i, [128, 16, 512], BF16) for i in range(2)]
                    upg = [T(st, "upg%d" % i, [128, 1032], F32) for i in range(2)]
                    upv = [T(st, "upv%d" % i, [128, 1032], F32) for i in range(2)]
                    ug_l = [T(st, "ug%d" % i, [128, 1030], F32) for i in range(2)]; uv_l = [T(st, "uv%d" % i, [128, 1030], F32) for i in range(2)]
                    upsel = T(st, "upsel", [128, 88, 4], F32)
                    stft = T(st, "stft", [2, 512], F32); stfT = T(st, "stfT", [128, 88, 2], F32)
                    fco = [T(st, "fco%d" % i, [4, 512], F32) for i in range(1)]
                    for c4 in range(0, 88, 4):
                        dma("sp", stft[:, :], stf[:, c4 * 128:(c4 + 4) * 128], writes=["stft"])
                        for ch in range(c4, c4 + 4):
                            tr(ps[0][:, ch * 2:ch * 2 + 2], stft[0:2, (ch - c4) * 128:(ch - c4 + 1) * 128], identf[0:2, 0:2], ["stft", "identf"], ["ps0"])
                    cp("dve", stfT[:, :, :], ps[0][:, 0:176].rearrange("p (c t) -> p c t", t=2), ["ps0"], ["stfT"])
                    dma("sp", cas[0:26, :], sta[4:30, :])
                    groups = ((0, 512), (512, 512), (1024, 6))

                    wus = [T(st, "wus%d" % i, [128, 2, 512], F32) for i in range(2)]
                    nld5 = [0]

                    def ld5_dma(jb_, k2):
                        sb_ = wus[k2 % 2]; sk_ = "wus%d" % (k2 % 2)
                        for (d0, s0_) in ((0, 256 * jb_), (256, 5632 + 256 * jb_)):
                            dma("sp", sb_[:, :, d0:d0 + 256], w_up[256 * k2:256 * k2 + 256, s0_:s0_ + 256].rearrange("(k p) n -> p k n", p=128), writes=[sk_])

                    def ld5_cast(jb_, k2):
                        sb_ = wus[k2 % 2]; sk_ = "wus%d" % (k2 % 2)
                        cp("act", wb[jb_ % 2][:, 2 * k2:2 * k2 + 2, :], sb_[:, :, :], [sk_], ["wu%d" % (jb_ % 2)])

                    def ld5(jb_):
                        for k2 in range(8):
                            ld5_dma(jb_, k2)
                            ld5_cast(jb_, k2)

                    def mk5(u, jb, sub, isv):
                        w = wb[jb % 2]; wk = "wu%d" % (jb % 2)
                        up = (upv if isv else upg)[sub]; uk = ("upv" if isv else "upg") + str(sub)
                        ch = (44 if isv else 0) + 2 * jb + sub
                        wc0 = isv * 256 + sub * 128
                        banks = [(3 * u + g_) % 6 for g_ in range(3)]
                        ucv = uv_l[sub] if isv else ug_l[sub]; k_ = ("uv%d" if isv else "ug%d") % sub

                        q_ = 2 * sub + isv

                        def s0():
                            if jb + 1 < 22:
                                ld5_dma(jb + 1, 2 * q_)
                                ld5_dma(jb + 1, 2 * q_ + 1)
                            for gi_, (g0, gn_) in enumerate(groups):
                                pb = ps[banks[gi_]]; pk = "ps%d" % banks[gi_]
                                for k in range(16):
                                    mm(pb[:, 0:gn_], w[:, k, wc0:wc0 + 128], h2T[:, k, g0:g0 + gn_], k == 0, k == 15, [wk, "h2T"], [pk])

                        def s1():
                            if jb + 1 < 22:
                                ld5_cast(jb + 1, 2 * q_)
                                ld5_cast(jb + 1, 2 * q_ + 1)
                            for gi_, (g0, gn_) in enumerate(groups):
                                pb = ps[banks[gi_]]; pk = "ps%d" % banks[gi_]
                                if gn_ == 512:
                                    cp("act", up[:, g0:g0 + 512], pb[:, 0:512], [pk], [uk])
                                else:
                                    cp("act", up[:, 1024:1026], pb[:, 0:2], [pk], [uk])
                                    cp("act", up[:, 1028:1032], pb[:, 2:6], [pk], [uk])
                            tt("pool", up[:, 0:2], up[:, 0:2], cst[:, VROW:VROW + 2], ALU.mult, [uk, "cst"], [uk])
                            cp("pool", up[:, 1026:1028], stfT[:, ch, :], ["stfT"], [uk])
                            cp("pool", upsel[:, ch, :].rearrange("p (a b) -> p a b", a=2), up[:, 1024:1032].rearrange("p (a b) -> p a b", b=2)[:, 0:4:3, :], [uk], ["upsel"])
                            fw = FCW + ch * 3
                            ts("dve", ucv[:, :], up[:, 0:1030], cst[:, fw:fw + 1], cst[:, FCB + ch:FCB + ch + 1], ALU.mult, ALU.add, [uk, "cst"], [k_])
                            stt(ucv[:, :], up[:, 1:1031], cst[:, fw + 1:fw + 2], ucv[:, :], ALU.mult, ALU.add, [uk, "cst", k_], [k_])
                            stt(ucv[:, :], up[:, 2:1032], cst[:, fw + 2:fw + 3], ucv[:, :], ALU.mult, ALU.add, [uk, "cst", k_], [k_])

                        def s2():
                            act(ug_l[sub][:, :], ug_l[sub][:, :], AF.Silu, ["ug%d" % sub], ["ug%d" % sub])
                            tt("pool", actT[:, 2 * jb + sub, :], ug_l[sub][:, :], uv_l[sub][:, :], ALU.mult, ["ug%d" % sub, "uv%d" % sub], ["actT"])

                        return [s0, s1, s2 if isv else None]

                    ld5(0)
                    items5 = []
                    for jb in range(22):
                        for sub in range(2):
                            for isv in range(2):
                                items5.append(mk5(len(items5), jb, sub, isv))
                    pipeline(items5, [0, 1, 2])
                    for r4 in range(22):
                        pb = ps[4 + r4 % 2]; pk = "ps%d" % (4 + r4 % 2)
                        for i4 in range(4):
                            ch = r4 * 4 + i4
                            tr(pb[0:4, i4 * 128:(i4 + 1) * 128], upsel[:, ch, :], identf[:, :], ["upsel", "identf"], [pk])
                        f_ = fco[0]; fk = "fco0"
                        cp("act", f_[:, :], pb[0:4, :], [pk], [fk])
                        dma("sp", fcp[:, r4 * 512:(r4 + 1) * 512], f_[0:2, :], reads=[fk])
                        dma("sp", fcs[:, r4 * 512:(r4 + 1) * 512], f_[2:4, :], reads=[fk])
                    S.emit(nc, esem, dsem)
                    chk(7)
            with ExitStack() as st:
                wd = [T(st, "wd%d" % i, [128, 44, 256], BF16) for i in range(2)]
                G2 = [T(st, "G2_%d" % i, [128, 2048], F32) for i in range(2)]
                xin = [T(st, "xmi%d" % i, [128, 256], F32) for i in range(2)]
                tq = [T(st, "ty%d" % i, [128, 256], F32) for i in range(2)]
                dma("sp", G2[0][:, :], GS[2, :, :], reads=["GS"], writes=["G2"])
                dma("sp", G2[1][:, :], GS[3, :, :], reads=["GS"], writes=["G2"])
                tiles6 = [(128, 128 * m, 128 * (m + 1), yp[m * 128:(m + 1) * 128, :], 0) for m in range(8)] + [(4, 1026, 1280, yso, 1)]
                nb = 0
                wds = [T(st, "wds%d" % i, [128, 11, 256], F32) for i in range(2)]
                nld6 = [0]

                def ld6(cb_):
                    for q4 in range(4):
                        sb_ = wds[nld6[0] % 2]; sk_ = "wds%d" % (nld6[0] % 2); nld6[0] += 1
                        dma("act", sb_[:, :, :], w_down[1408 * q4:1408 * q4 + 1408, 256 * cb_:256 * cb_ + 256].rearrange("(k p) n -> p k n", p=128), writes=[sk_])
                        cp("act", wd[cb_ % 2][:, 11 * q4:11 * q4 + 11, :], sb_[:, :, :], [sk_], ["wd%d" % (cb_ % 2)])
                ld6(0)
                for cb in range(8):
                    w = wd[cb % 2]; wk = "wd%d" % (cb % 2)
                    if cb + 1 < 8:
                        ld6(cb + 1)
                    for (P, acol, xrow, ydst, grp) in tiles6:
                        pb = ps[nb % 4]; pk = "ps%d" % (nb % 4)
                        xi = xin[nb % 2]; xk = "xmi%d" % (nb % 2); tq_ = tq[nb % 2]; tk_ = "ty%d" % (nb % 2); nb += 1
                        dma("sp", xi[0:P, :], XM_s[xrow:xrow + P, 256 * cb:256 * cb + 256], reads=["XM_s"], writes=[xk])
                        for k in range(44):
                            mm(pb[0:P, 0:256], actT[:, k, acol:acol + P], w[:, k, :], k == 0, k == 43, ["actT", wk], [pk])
                        tt("dve", tq_[0:P, :], pb[0:P, 0:256], G2[grp][0:P, 256 * cb:256 * cb + 256], ALU.mult, [pk, "G2"], [tk_])
                        tt("pool", tq_[0:P, :], tq_[0:P, :], xi[0:P, :], ALU.add, [tk_, xk], [tk_])
                        dma("sp", ydst[:, 256 * cb:256 * cb + 256], tq_[0:P, :], reads=[tk_])
                S.finish()
                S.emit(nc, esem, dsem)
    return nc


_NC = None
STOP_AFTER = None
DEBUG_CORES = None


class _Stop(Exception):
    pass


def _host_consts():
    half = 32
    inv = (np.float32(10000.0) ** (-(np.arange(half, dtype=np.float32) / np.float32(half)))).astype(np.float32)
    band = np.zeros((128, 256), np.float32)
    ki = np.arange(128)[:, None]; c = np.arange(256)[None, :]
    band[(c >= ki) & (c <= ki + 128)] = 1.0
    m12 = np.ones((128, 12), np.float32)
    m12[:, 0:4] = (np.arange(128)[:, None] >= np.arange(4)[None, :]).astype(np.float32)
    msk12 = np.tile(m12, (1, 8))
    tq = np.arange(4)
    wmat = (tq[:, None] <= tq[None, :]).astype(np.float32) + 2.0 * (tq[:, None] == tq[None, :]).astype(np.float32)
    wm = np.tile(wmat, (1, 8))
    sel = np.zeros((2, 256), np.float32); sel[0, 0:128] = 1.0; sel[1, 128:256] = 1.0
    return inv, band.astype(ml_dtypes.bfloat16), msk12.astype(ml_dtypes.bfloat16), wm.astype(ml_dtypes.bfloat16), sel


def kernel(x_prompt, x_sample, cache_win_k, cache_win_v, state_conv_a, state_ffn_conv, c_prompt, c_sample,
           norm_mix_g, norm_ffn_g, w_ada, b_ada, w_in, conv_a_w, conv_a_b, ln_a_g, ln_a_b, q_norm_g, k_norm_g,
           w_out, w_up, ffn_conv_w, ffn_conv_b, w_down):
    global _NC
    f = lambda a: np.ascontiguousarray(np.asarray(a, dtype=np.float32))
    x_prompt = f(x_prompt); x_sample = f(x_sample)
    inv, band, msk12, wm, sel = _host_consts()
    if _NC is None:
        _NC = build_program()
    nc = _NC
    caw = f(conv_a_w)[0]; fcw = f(ffn_conv_w)[0]
    shared = dict(
        gn=np.stack([f(norm_mix_g)[0], f(norm_ffn_g)[0]]), bada=np.stack([f(b_ada)[0], f(b_ada)[0]]),
        band=band, msk12=msk12, wm=wm, sel=sel,
        identb=np.eye(128, dtype=np.float32).astype(ml_dtypes.bfloat16), identf=np.eye(128, dtype=np.float32),
        w_ada=f(w_ada)[0], w_in=f(w_in)[0], w_out=f(w_out)[0], w_up=f(w_up)[0], w_down=f(w_down)[0])
    cst0 = np.zeros((128, NCST), np.float32)
    cst0[:, CAW:CAW + 248] = caw.T.reshape(8, 128, 31).transpose(1, 0, 2).reshape(128, 248)
    cst0[:, CAB:CAB + 8] = f(conv_a_b)[0].reshape(8, 128).T
    cst0[:, LNG:LNG + 8] = f(ln_a_g)[0].reshape(8, 128).T
    cst0[:, LNB:LNB + 8] = f(ln_a_b)[0].reshape(8, 128).T
    cst0[:, FCW:FCW + 264] = fcw.T.reshape(88, 128, 3).transpose(1, 0, 2).reshape(128, 264)
    cst0[:, FCB:FCB + 88] = f(ffn_conv_b)[0].reshape(88, 128).T
    cst0[:, QG:QG + 64] = f(q_norm_g)[0][None, :]
    cst0[:, KG:KG + 64] = f(k_norm_g)[0][None, :]
    cst0[:, NHALF] = -0.5
    in_maps = []
    for i in range(8):
        b, c = i // 4, i % 4
        s = 1024 * c
        P0 = s - 128 - 2048
        pos = P0 + np.arange(3200)
        xall = np.zeros((3200, 2048), np.float32)
        ok = pos >= 0
        xall[ok] = x_prompt[b, pos[ok]]
        cst = cst0.copy()
        val = np.ones((26, 128), np.float32)
        val[0:25] = np.where(ok.reshape(25, 128), np.float32(1.0), np.float32(1e-30))
        cst[:, VAL:VAL + 26] = val.T
        cst[:, VROW:VROW + 2] = 1.0 if c > 0 else 0.0
        posf = np.concatenate([np.maximum(pos, 0), 16384 + np.arange(4), np.zeros(124, np.int64)]).astype(np.float32)
        ang = posf[:, None] * inv[None, :]
        ropec = np.cos(ang).astype(np.float32).reshape(26, 128, 32).transpose(1, 0, 2)
        ropes = np.sin(ang).astype(np.float32).reshape(26, 128, 32).transpose(1, 0, 2)
        m = dict(shared)
        m.update(xh=np.ascontiguousarray(xall[0:2048]), xm=np.ascontiguousarray(xall[2048:3200]), xs=x_sample[i],
                 cvec=np.stack([f(c_prompt)[b], f(c_sample)[i]]), cst=cst, ropec=np.ascontiguousarray(ropec), ropes=np.ascontiguousarray(ropes),
                 ck=f(cache_win_k)[0, i].reshape(2048, 1024), cv=f(cache_win_v)[0, i].reshape(2048, 1024),
                 sta=f(state_conv_a)[0, i], stf=f(state_ffn_conv)[0, i])
        in_maps.append(m)
    if DEBUG_CORES is not None:
        res = run_bass_kernel_spmd(nc, [in_maps[c] for c in DEBUG_CORES], core_ids=list(range(len(DEBUG_CORES))))
        return res.results, in_maps
    res = run_bass_kernel_spmd(nc, in_maps, core_ids=list(range(8)))
    R = res.results
    y_p = np.zeros((2, 4096, 2048), np.float32); y_s = np.zeros((8, 4, 2048), np.float32)
    wkp = np.zeros((1, 2, 2048, 16, 64), np.float32); wvp = np.zeros_like(wkp)
    cap = np.zeros((1, 2, 30, 1024), np.float32); fcp = np.zeros((1, 2, 2, 11264), np.float32)
    wks = np.zeros((1, 8, 2048, 16, 64), np.float32); wvs = np.zeros_like(wks)
    cas = np.zeros((1, 8, 30, 1024), np.float32); fcs = np.zeros((1, 8, 2, 11264), np.float32)
    for i in range(8):
        b, c = i // 4, i % 4
        r = R[i]
        y_p[b, 1024 * c:1024 * c + 1024] = r["yp"]
        y_s[i] = r["yso"]
        if c >= 2:
            wkp[0, b, 1024 * (c - 2):1024 * (c - 1)] = r["kp"].reshape(1024, 16, 64)
            wvp[0, b, 1024 * (c - 2):1024 * (c - 1)] = r["vp"].reshape(1024, 16, 64)
        if c == 3:
            cap[0, b] = r["cap"]; fcp[0, b] = r["fcp"]
        wks[0, i] = r["kso"].reshape(2048, 16, 64); wvs[0, i] = r["vso"].reshape(2048, 16, 64)
        cas[0, i] = r["cas"]; fcs[0, i] = r["fcs"]
    return (y_p, y_s, wkp, wvp, cap, fcp, wks, wvs, cas, fcs)
```

```python
import numpy as np
import ml_dtypes
from contextlib import ExitStack
import concourse.bass as bass
import concourse.mybir as mybir
from concourse.bass_utils import run_bass_kernel_spmd

F32 = mybir.dt.float32
BF16 = mybir.dt.bfloat16
AF = mybir.ActivationFunctionType
ALU = mybir.AluOpType
AX = mybir.AxisListType
ENGS = ("sp", "act", "dve", "pool", "pe")
ND = 44
DPOOL = {"sp": (0, 28), "pool": (28, 36), "act": (36, 44)}
EPS = 1e-6

CAW, CAB, LNG, LNB, FCW, FCB, QG, KG, VAL, NHALF, VROW, NCST = 0, 248, 256, 264, 272, 536, 624, 688, 752, 778, 780, 782


class Sched:
    def __init__(self):
        self.ops = {e: [] for e in ENGS}
        self.cnt = {e: 0 for e in ENGS}
        self.seen = {e: {} for e in ENGS}
        self.lastw = {}
        self.readers = {}
        self.dma_uses = [0] * ND
        self.dma_rr = {}

    def _need(self, eng, tok, waits):
        if tok is None:
            return
        k, v = tok
        if k == eng and eng == "pe":
            return
        if self.seen[eng].get(k, 0) >= v:
            return
        if waits.get(k, 0) < v:
            waits[k] = v

    def op(self, eng, fn, reads=(), writes=(), dma=False):
        excl = [r for r in reads if isinstance(r, str) and len(r) == 3 and r[:2] in ("ps", "pt")]
        if excl:
            reads = [r for r in reads if r not in excl]
            writes = list(writes) + excl
        waits = {}
        for r in reads:
            self._need(eng, self.lastw.get(r), waits)
        for w in writes:
            self._need(eng, self.lastw.get(w), waits)
            for k, v in self.readers.get(w, {}).items():
                self._need(eng, (k, v), waits)
        if dma:
            lo, hi = DPOOL[eng]
            j = self.dma_rr.get(eng, lo)
            self.dma_rr[eng] = lo + (j + 1 - lo) % (hi - lo)
            if self.dma_uses[j] > 0:
                self._need(eng, (("d", j), 16 * self.dma_uses[j]), waits)
            self.dma_uses[j] += 1
            tok = (("d", j), 16 * self.dma_uses[j])
        else:
            self.cnt[eng] += 1
            tok = (eng, self.cnt[eng])
        for k, v in waits.items():
            self.seen[eng][k] = v
        self.ops[eng].append((list(waits.items()), fn, tok, dma))
        for r in reads:
            d = self.readers.setdefault(r, {})
            if d.get(tok[0], 0) < tok[1]:
                d[tok[0]] = tok[1]
        for w in writes:
            self.lastw[w] = tok
            self.readers[w] = {}
        return tok

    def finish(self):
        waits = {}
        for j in range(ND):
            if self.dma_uses[j] > 0:
                self._need("sp", (("d", j), 16 * self.dma_uses[j]), waits)
        for e in ENGS:
            if e != "sp" and self.cnt[e] > 0:
                self._need("sp", (e, self.cnt[e]), waits)
        for k, v in waits.items():
            self.seen["sp"][k] = v
        self.ops["sp"].append((list(waits.items()), None, None, False))

    def emit(self, nc, esem, dsem):
        def semof(k):
            return esem[k] if isinstance(k, str) else dsem[k[1]]

        def run(e, name):
            for waits, fn, tok, dma in self.ops[name]:
                for k, v in waits:
                    e.wait_ge(semof(k), v)
                if fn is None:
                    continue
                ins = fn(e)
                ins.then_inc(semof(tok[0]), 16 if dma else 1)
            self.ops[name] = []

        with nc.Block() as block:
            @block.sync
            def _(e):
                run(e, "sp")

            @block.scalar
            def _(e):
                run(e, "act")

            @block.vector
            def _(e):
                run(e, "dve")

            @block.gpsimd
            def _(e):
                run(e, "pool")

            @block.tensor
            def _(e):
                run(e, "pe")


def build_program():
    try:
        return _build()
    except _Stop as ex:
        return ex.nc


def _build():
    nc = bass.Bass("TRN2", target_bir_lowering=False)
    S = Sched()

    def din(name, shape, dt=F32):
        return nc.dram_tensor(name, list(shape), dt, kind="ExternalInput").ap()

    def dout(name, shape):
        return nc.dram_tensor(name, list(shape), F32, kind="ExternalOutput").ap()

    def dscr(name, shape, dt):
        return nc.dram_tensor(name, list(shape), dt, kind="Internal").ap()

    xh = din("xh", [2048, 2048]); xm = din("xm", [1152, 2048]); xs = din("xs", [4, 2048])
    cvec = din("cvec", [2, 2048]); gn = din("gn", [2, 2048]); bada = din("bada", [2, 12288])
    cst_d = din("cst", [128, NCST]); ropec_d = din("ropec", [128, 26, 32]); ropes_d = din("ropes", [128, 26, 32])
    band_d = din("band", [128, 256], BF16); msk12_d = din("msk12", [128, 96], BF16); wm_d = din("wm", [4, 32], BF16)
    identb_d = din("identb", [128, 128], BF16); identf_d = din("identf", [128, 128]); sel_d = din("sel", [2, 256])
    ck = din("ck", [2048, 1024]); cv = din("cv", [2048, 1024]); sta = din("sta", [30, 1024]); stf = din("stf", [2, 11264])
    w_ada = din("w_ada", [2048, 12288]); w_in = din("w_in", [2048, 5120]); w_out = din("w_out", [2048, 2048])
    w_up = din("w_up", [2048, 11264]); w_down = din("w_down", [5632, 2048])

    yp = dout("yp", [1024, 2048]); yso = dout("yso", [4, 2048]); kp = dout("kp", [1024, 1024]); vp = dout("vp", [1024, 1024])
    cap = dout("cap", [30, 1024]); fcp = dout("fcp", [2, 11264]); kso = dout("kso", [2048, 1024]); vso = dout("vso", [2048, 1024])
    cas = dout("cas", [30, 1024]); fcs = dout("fcs", [2, 11264])

    KT_s = dscr("KT_s", [8, 128, 3200], BF16)
    V_s = dscr("V_s", [3200, 1536], BF16)
    GS = dscr("GS", [4, 128, 2048], F32)
    XM_s = dscr("XM_s", [1284, 2048], F32)

    def chk(n):
        if STOP_AFTER is not None and n == STOP_AFTER:
            S.finish()
            S.emit(nc, esem, dsem)
            ex = _Stop()
            ex.nc = nc
            raise ex

    with ExitStack() as st0:
        def T(stk, name, shape, dt):
            return stk.enter_context(nc.sbuf_tensor("sb_" + name, list(shape), dt))

        esem = {e: st0.enter_context(nc.semaphore("s_" + e)) for e in ENGS}
        dsem = [st0.enter_context(nc.semaphore("d%d" % j)) for j in range(ND)]
        ps = [st0.enter_context(nc.psum_tensor("ps%d" % i, [128, 512], F32)) for i in range(6)]
        pt = [st0.enter_context(nc.psum_tensor("pt%d" % i, [128, 1024], BF16)) for i in range(2)]

        identb = T(st0, "identb", [128, 128], BF16); identf = T(st0, "identf", [128, 128], F32)
        cst = T(st0, "cst", [128, NCST], F32); band = T(st0, "band", [128, 256], BF16)
        msk12 = T(st0, "msk12", [128, 96], BF16); wm = T(st0, "wm", [4, 32], BF16)
        sel = T(st0, "sel", [2, 256], F32); onesf = T(st0, "onesf", [128, 128], F32); ones64 = T(st0, "ones64", [128, 64], BF16)
        modT = T(st0, "modT", [128, 96, 2], F32); gnT = T(st0, "gnT", [128, 16, 2], F32)
        A1T = T(st0, "A1T", [128, 16, 2], F32); A2T = T(st0, "A2T", [128, 16, 2], F32)
        ss = T(st0, "ss", [128, 8], F32); rstd = T(st0, "rstd", [128, 8], F32)

        def dma(eng, out, in_, reads=(), writes=()):
            S.op(eng, lambda e: e.dma_start(out=out, in_=in_), reads=reads, writes=writes, dma=True)

        def mm(out, lhsT, rhs, start, stop, reads, writes, skip=False):
            if skip:
                S.op("pe", lambda e: e.matmul(out=out, lhsT=lhsT, rhs=rhs, start=start, stop=stop, skip_group_check=True), reads=reads, writes=writes)
            else:
                S.op("pe", lambda e: e.matmul(out=out, lhsT=lhsT, rhs=rhs, start=start, stop=stop), reads=reads, writes=writes)

        def tr(out, in_, ident, reads, writes):
            S.op("pe", lambda e: e.transpose(out=out, in_=in_, identity=ident), reads=reads, writes=writes)

        def act(out, in_, func, reads, writes, scale=None, bias=None, accum_out=None):
            kw = {}
            if accum_out is not None:
                kw["accum_out"] = accum_out
            if scale is not None:
                kw["scale"] = scale
            if bias is not None:
                kw["bias"] = bias
            S.op("act", lambda e: e.activation(out=out, in_=in_, func=func, **kw), reads=reads, writes=writes)

        def tt(eng, out, in0, in1, op, reads, writes):
            S.op(eng, lambda e: e.tensor_tensor(out=out, in0=in0, in1=in1, op=op), reads=reads, writes=writes)

        def ts(eng, out, in0, s1, s2, op0, op1, reads, writes):
            if s2 is None:
                S.op(eng, lambda e: e.tensor_scalar(out=out, in0=in0, scalar1=s1, scalar2=None, op0=op0), reads=reads, writes=writes)
            else:
                S.op(eng, lambda e: e.tensor_scalar(out=out, in0=in0, scalar1=s1, scalar2=s2, op0=op0, op1=op1), reads=reads, writes=writes)

        def stt(out, in0, scalar, in1, op0, op1, reads, writes):
            S.op("dve", lambda e: e.scalar_tensor_tensor(out=out, in0=in0, scalar=scalar, in1=in1, op0=op0, op1=op1), reads=reads, writes=writes)

        def cp(eng, out, in_, reads, writes):
            if eng == "act":
                act(out, in_, AF.Copy, reads, writes)
            else:
                S.op(eng, lambda e: e.tensor_copy(out=out, in_=in_), reads=reads, writes=writes)

        def memset(eng, ap, val, writes):
            S.op(eng, lambda e: e.memset(ap, val), writes=writes)

        def rsq(out, in_, P, n, reads, writes):
            tt("pool", out, in_, cst[0:P, NHALF:NHALF + 1].to_broadcast([P, n]), ALU.pow, list(reads) + ["cst"], writes)

        for t_, d_, k_ in ((identb, identb_d, "identb"), (identf, identf_d, "identf"), (cst, cst_d, "cst"), (band, band_d, "band"),
                           (msk12, msk12_d, "msk12"), (wm, wm_d, "wm"), (sel, sel_d, "sel")):
            dma("sp", t_[:], d_, writes=[k_])
        memset("dve", onesf[:], 1.0, ["onesf"])
        memset("dve", ones64[:], 1.0, ["ones64"])

        sT = T(st0, "sT", [128, 16, 2], BF16)
        stfT = T(st0, "stfT", [128, 88, 2], F32)

        def ada_stages(cb, wb, tmp, gtl, badp, pmm, psel, ptr_, tr_base, cnt, wst):
            w = wb[cb % 2]; wk = "wb%d" % (cb % 2)
            pb = ps[pmm[cb % 2]]; pk = "ps%d" % pmm[cb % 2]
            tp = tmp[cb % 2]; tk = "tmp%d" % (cb % 2)
            bp = badp[cb % 4]; bk = "badp%d" % (cb % 4)

            def L():
                for q4 in range(4):
                    sb_ = wst[q4 % 2]; sk_ = "wst%d" % (q4 % 2)
                    dma("sp", sb_[:, :, :], w_ada[512 * q4:512 * q4 + 512, cb * 512:(cb + 1) * 512].rearrange("(k p) n -> p k n", p=128), writes=[sk_])
                    cp("act", w[:, 4 * q4:4 * q4 + 4, :], sb_[:, :, :], [sk_], [wk])
                dma("sp", bp[:, :], bada[:, cb * 512:(cb + 1) * 512], writes=[bk])

            def M():
                for k in range(16):
                    mm(pb[0:2, :], sT[:, k, :], w[:, k, :], k == 0, k == 15, ["sT", wk], [pk])

            def E():
                tt("dve", tp[:], pb[0:2, :], bp[:, :], ALU.add, [pk, bk], [tk])
                for j in range(4):
                    c0 = (cb * 4 + j) * 2 - tr_base
                    tr(ps[ptr_][:, c0:c0 + 2], tp[0:2, j * 128:(j + 1) * 128], identf[0:2, 0:2], [tk, "identf"], ["ps%d" % ptr_])
                gi_ = cb // 4
                if gi_ in (2, 5):
                    for grp in range(2):
                        mm(ps[psel][:], sel[0:2, grp * 128:(grp + 1) * 128], tp[0:2, :], True, True, ["sel", tk], ["ps%d" % psel])
                        g = gtl[cnt[0] % 2]; gk = "gtl%d" % (cnt[0] % 2); cnt[0] += 1
                        cp("act", g[:], ps[psel][:], ["ps%d" % psel], [gk])
                        which = (0 if gi_ == 2 else 2) + grp
                        cc = (cb % 4) * 512
                        dma("sp", GS[which, :, cc:cc + 512], g[:], reads=[gk], writes=["GS"])

            return L, M, E

        with ExitStack() as st:
            cvt = T(st, "cvt", [2, 2048], F32); cvb = T(st, "cvb", [2, 2048], BF16)
            gnt = T(st, "gnt", [2, 2048], F32)
            wb = [T(st, "wb%d" % i, [128, 16, 512], BF16) for i in range(2)]
            tmp = [T(st, "tmp%d" % i, [2, 512], F32) for i in range(2)]
            gtl = [T(st, "gtl%d" % i, [128, 512], F32) for i in range(2)]
            badp = [T(st, "badp%d" % i, [2, 512], F32) for i in range(4)]
            dma("sp", cvt[:], cvec, writes=["cvt"]); dma("sp", gnt[:], gn, writes=["gnt"])
            act(cvb[:], cvt[:], AF.Silu, ["cvt"], ["cvb"])
            for k in range(16):
                tr(pt[0][:, 2 * k:2 * k + 2], cvb[0:2, k * 128:(k + 1) * 128], identb[0:2, 0:2], ["cvb", "identb"], ["pt0"])
            cp("dve", sT[:, :, :], pt[0][:, 0:32].rearrange("p (k t) -> p k t", t=2), ["pt0"], ["sT"])
            for k in range(16):
                tr(ps[3][:, 192 + 2 * k:194 + 2 * k], gnt[0:2, k * 128:(k + 1) * 128], identf[0:2, 0:2], ["gnt", "identf"], ["ps3"])
            stft = [T(st, "stft%d" % i, [2, 2816], F32) for i in range(2)]
            for c4 in range(4):
                sf = stft[c4 % 2]; sfk = "stft%d" % (c4 % 2)
                dma("sp", sf[:, :], stf[:, c4 * 2816:(c4 + 1) * 2816], writes=[sfk])
                for c_ in range(22):
                    ch = c4 * 22 + c_
                    tr(ps[4][:, ch * 2:ch * 2 + 2], sf[0:2, c_ * 128:(c_ + 1) * 128], identf[0:2, 0:2], [sfk, "identf"], ["ps4"])
            cp("dve", stfT[:, :, :], ps[4][:, 0:176].rearrange("p (c t) -> p c t", t=2), ["ps4"], ["stfT"])
            cnt0 = [0]
            wst0 = [T(st, "wst%d" % i, [128, 4, 512], F32) for i in range(2)]
            stg = [ada_stages(cb, wb, tmp, gtl, badp, (0, 1), 2, 3, 0, cnt0, wst0) for cb in range(8)]
            stg[0][0]()
            for cb in range(8):
                if cb + 1 < 8:
                    stg[cb + 1][0]()
                stg[cb][1]()
                stg[cb][2]()
            cp("dve", modT[:, 0:32, :], ps[3][:, 0:64].rearrange("p (k t) -> p k t", t=2), ["ps3"], ["modT"])
            cp("dve", gnT[:, :, :], ps[3][:, 192:224].rearrange("p (k t) -> p k t", t=2), ["ps3"], ["gnT"])
            for grp in range(2):
                ts("dve", A1T[:, :, grp], modT[:, 16:32, grp], 1.0, None, ALU.add, None, ["modT"], ["A1T"])
                tt("dve", A1T[:, :, grp], A1T[:, :, grp], gnT[:, :, 0], ALU.mult, ["A1T", "gnT"], ["A1T"])
            S.emit(nc, esem, dsem)
            chk(0)

        def pipe_steps(items, lags, rev=True):
            n = len(items); L = max(lags)
            steps = []
            for t_ in range(n + L):
                def step(t_=t_):
                    for st_i, lg in (reversed(list(enumerate(lags))) if rev else list(enumerate(lags))):
                        ii = t_ - lg
                        if 0 <= ii < n and items[ii][st_i] is not None:
                            items[ii][st_i]()
                steps.append(step)
            return steps

        def pipeline(items, lags):
            for st_ in pipe_steps(items, lags):
                st_()

        def run_merged(stepsA, stepsB, off):
            for t_ in range(max(len(stepsA), len(stepsB) + off)):
                if t_ < len(stepsA):
                    stepsA[t_]()
                if 0 <= t_ - off < len(stepsB):
                    stepsB[t_ - off]()

        def norm_loop(stk_tiles, specs, AT, sh_lo, hT, defer=False):
            xts, sqts, xbs = stk_tiles

            def mk(i, sp_):
                b2 = i % 2
                xt = xts[b2]; sqt = sqts[b2]; xb = xbs[b2]
                xk = "xt%d" % (b2 if xts[0] is not xts[1] else 0); sk = "sqt%d" % b2; bk = "xb%d" % b2
                P = sp_["P"]; grp = sp_["grp"]; col0 = sp_["col0"]; c0 = sp_.get("c0", 0); n = sp_.get("n", P); hkey = sp_["hkey"]
                ssc = ss[0:P, b2:b2 + 1]; rsc = rstd[0:P, b2:b2 + 1]; ssk = "ss%d" % b2; rsk = "rstd%d" % b2

                def n0():
                    dma("sp", xt[0:P, :], sp_["src"], reads=sp_.get("src_reads", ()), writes=[xk])
                    act(sqt[0:P, :], xt[0:P, :], AF.Square, [xk], [sk, ssk], accum_out=ssc)
                    ts("dve", ssc, ssc, 1.0 / 2048, EPS, ALU.mult, ALU.add, [ssk], [ssk])
                    rsq(rsc, ssc, P, 1, [ssk], [rsk])
                    act(xb[0:P, :], xt[0:P, :], AF.Identity, [xk, rsk], [bk], scale=rsc)

                def n1():
                    for k in range(16):
                        b = k // 8
                        tr(pt[b][:, (k % 8) * 128:(k % 8) * 128 + P], xb[0:P, k * 128:(k + 1) * 128], identb[0:P, 0:P], [bk, "identb"], ["pt%d" % b])

                def n2():
                    for b in range(2):
                        src_ = pt[b][:, 0:1024].rearrange("p (k t) -> p k t", k=8)[:, :, c0:c0 + n]
                        dst = hT[:, 8 * b:8 * b + 8, col0:col0 + n]
                        tt("dve", dst, src_, AT[:, 8 * b:8 * b + 8, grp:grp + 1].to_broadcast([128, 8, n]), ALU.mult, ["pt%d" % b, "A1T", "A2T"], [hkey])
                        tt("pool", dst, dst, modT[:, sh_lo + 8 * b:sh_lo + 8 * b + 8, grp:grp + 1].to_broadcast([128, 8, n]), ALU.add, [hkey, "modT"], [hkey])

                return [n0, n1, n2]

            steps_ = pipe_steps([mk(i, sp_) for i, sp_ in enumerate(specs)], [0, 1, 2])
            if defer:
                return steps_
            for st_ in steps_:
                st_()

        def load_w(w, wk, src_rows_fn, ncolsets):
            for (d0, n, s0) in ncolsets:
                for q4 in range(4):
                    dma("pool", w[:, 4 * q4:4 * q4 + 4, d0:d0 + n], src_rows_fn(512 * q4, 512 * q4 + 512, s0, s0 + n).rearrange("(k p) n -> p k n", p=128), writes=[wk])

        def phase1(tiles, blocks, ntok, pname):
            with ExitStack() as st:
                hT = T(st, "hT" + pname, [128, 16, ntok], BF16)
                ropec = T(st, "ropec" + pname, [128, 26, 32], F32); ropes = T(st, "ropes" + pname, [128, 26, 32], F32)
                dma("sp", ropec[:], ropec_d, writes=["rope"]); dma("sp", ropes[:], ropes_d, writes=["rope2"])
                if pname == "a":
                    xts = [T(st, "xt%d" % i_ + pname, [128, 2048], F32) for i_ in range(2)]
                else:
                    xts = [T(st, "xt0" + pname, [128, 2048], F32)] * 2
                sqts = [T(st, "sqt%d" % i_ + pname, [128, 2048], BF16) for i_ in range(2)]; xbs = [T(st, "xb%d" % i_ + pname, [128, 2048], BF16) for i_ in range(2)]
                wb = [T(st, "wb%d" % i + pname, [128, 16, 512], BF16) for i in range(2)]
                sq5_l = [T(st, "sq5%d" % i + pname, [128, 512], F32) for i in range(2)]; qn_l = [T(st, "qn%d" % i + pname, [128, 512], F32) for i in range(2)]
                qr = [T(st, "qr%d" % i + pname, [128, 512], F32) for i in range(3)]
                t1_l = [T(st, "t1%d" % i + pname, [128, 256], F32) for i in range(2)]; t2_l = [T(st, "t2%d" % i + pname, [128, 256], F32) for i in range(2)]
                t3_l = [T(st, "t3%d" % i + pname, [128, 256], F32) for i in range(2)]; t4_l = [T(st, "t4%d" % i + pname, [128, 256], F32) for i in range(2)]
                kb_l = [T(st, "kb%d" % i + pname, [128, 512], BF16) for i in range(2)]
                ssq = T(st, "ssq" + pname, [128, 16], F32); rsq_ = T(st, "rsq_" + pname, [128, 24], F32)
                ktp = [T(st, "ktp%d" % i + pname, [128, 4, 128], BF16) for i in range(2)]
                vt = [T(st, "vt%d" % i + pname, [128, 512], F32) for i in range(2)]
                vb = [T(st, "vb%d" % i + pname, [128, 768], BF16) for i in range(2)]
                vv = T(st, "vv" + pname, [128, 64], BF16)
                sg_l = [T(st, "sg%d" % i + pname, [128, 256], F32) for i in range(2)]
                gt = [T(st, "gt%d" % i + pname, [128, 256], F32) for i in range(3)]
                gb_l = [T(st, "gb%d" % i + pname, [128, 256], BF16) for i in range(2)]
                nsteps = norm_loop((xts, sqts, xbs), [dict(src=tl["src"], P=tl["P"], grp=(1 if tl["kind"] == "samp" else 0), col0=tl["col0"], hkey=("hT", i)) for i, tl in enumerate(tiles)], A1T, 0, hT, defer=True)
                nb = 0

                wst = [T(st, "wsi%d" % i_ + pname, [128, 1, 512], F32) for i_ in range(2)]

                def piece_dma(bi_, pc):
                    kind_, j_ = blocks[bi_]
                    sb_ = wst[pc % 2]; sk_ = "wsi%d" % (pc % 2)
                    if kind_ == "glu":
                        colsets = [(0, 256, 256 * j_), (256, 256, 1024 + 256 * j_)]
                    else:
                        colsets = [(0, 512, {"q": 2048, "k": 3072, "v": 4096}[kind_] + 512 * j_)]
                    for (d0, n_, s0_) in colsets:
                        dma("sp", sb_[:, 0, d0:d0 + n_], w_in[128 * pc:128 * pc + 128, s0_:s0_ + n_], writes=[sk_])

                def piece_cast(bi_, pc):
                    cp("act", wb[bi_ % 2][:, pc, :], wst[pc % 2][:, 0, :], ["wsi%d" % (pc % 2)], ["wbuf%d" % (bi_ % 2)])

                def issue_load(bi_):
                    for pc in range(16):
                        piece_dma(bi_, pc)
                        piece_cast(bi_, pc)

                def mk_item(bi, blk, i, tl, nb, first, pos):
                    kind, j = blk
                    w = wb[bi % 2]; wk = "wbuf%d" % (bi % 2)
                    P = tl["P"]; t = tl["t"]; col0 = tl["col0"]
                    pb = ps[nb % 4]; pk = "ps%d" % (nb % 4); tb = nb % 2
                    sq5 = sq5_l[tb]; qn = qn_l[tb]; t1 = t1_l[tb]; t2 = t2_l[tb]; t3 = t3_l[tb]; t4 = t4_l[tb]; kb = kb_l[tb]; sg = sg_l[tb]; gb = gb_l[tb]
                    TB = str(tb)
                    ptb = ps[4 + tb][:, :].bitcast(BF16); ptk = "ps%d" % (4 + tb)
                    vcol = cst[0:P, VAL + t:VAL + t + 1]
                    g = gt[nb % 3]; gk = "gt%d" % (nb % 3)
                    o = qr[nb % 3]; ok_ = "qr%d" % (nb % 3)
                    r3 = nb % 3; RK = "rsq_%d" % r3
                    rs8 = rsq_[0:P, 8 * r3:8 * r3 + 8]

                    def stA():
                        if bi + 1 < len(blocks):
                            if 1 <= pos <= 8:
                                piece_cast(bi + 1, 2 * pos - 2)
                                piece_cast(bi + 1, 2 * pos - 1)
                            if pos < 8:
                                piece_dma(bi + 1, 2 * pos)
                                piece_dma(bi + 1, 2 * pos + 1)
                        for k in range(16):
                            mm(pb[0:P, :], hT[:, k, col0:col0 + P], w[:, k, :], k == 0, k == 15, [("hT", i), wk], [pk])

                    def stB():
                        if kind == "glu":
                            act(sg[0:P, :], pb[0:P, 256:512], AF.Sigmoid, [pk], ["sg" + TB])
                            stt(g[0:P, :], pb[0:P, 0:256], vcol, sg[0:P, :], ALU.mult, ALU.mult, [pk, "sg" + TB, "cst"], [gk])
                            if tl["kind"] == "samp":
                                dma("sp", cas[26:30, 256 * j:256 * j + 256], g[0:4, :], reads=[gk])
                            elif tl.get("own") == 7:
                                dma("sp", cap[:, 256 * j:256 * j + 256], g[98:128, :], reads=[gk])
                        elif kind in ("q", "k"):
                            gcol = QG if kind == "q" else KG
                            q3 = qn[0:P, :].rearrange("p (h d) -> p h d", h=8)
                            tt("dve", q3, pb[0:P, :].rearrange("p (h d) -> p h d", h=8), cst[0:P, gcol:gcol + 64].unsqueeze(1).to_broadcast([P, 8, 64]), ALU.mult, [pk, "cst"], ["qn" + TB])
                            act(sq5[0:P, :], pb[0:P, :], AF.Square, [pk], ["sq5" + TB])
                            S.op("dve", lambda e: e.tensor_reduce(out=ssq[0:P, 8 * tb:8 * tb + 8], in_=sq5[0:P, :].rearrange("p (h d) -> p h d", h=8), axis=AX.X, op=ALU.add), reads=["sq5" + TB], writes=["ssq" + TB])
                            ts("dve", ssq[0:P, 8 * tb:8 * tb + 8], ssq[0:P, 8 * tb:8 * tb + 8], 1.0 / 64, EPS, ALU.mult, ALU.add, ["ssq" + TB], ["ssq" + TB])
                            rsq(rs8, ssq[0:P, 8 * tb:8 * tb + 8], P, 8, ["ssq" + TB], [RK])
                            o3 = o[0:P, :].rearrange("p (h d) -> p h d", h=8)
                            C = ropec[0:P, t, :].unsqueeze(1).to_broadcast([P, 8, 32]); Sn = ropes[0:P, t, :].unsqueeze(1).to_broadcast([P, 8, 32])
                            x1 = q3[:, :, 0:32]; x2 = q3[:, :, 32:64]
                            v1 = t1[0:P, :].rearrange("p (h d) -> p h d", h=8); v2 = t2[0:P, :].rearrange("p (h d) -> p h d", h=8)
                            v3 = t3[0:P, :].rearrange("p (h d) -> p h d", h=8); v4 = t4[0:P, :].rearrange("p (h d) -> p h d", h=8)
                            tt("dve", v1, x1, C, ALU.mult, ["qn" + TB, "rope"], ["t1" + TB])
                            tt("dve", v2, x2, Sn, ALU.mult, ["qn" + TB, "rope2"], ["t2" + TB])
                            tt("dve", o3[:, :, 0:32], v1, v2, ALU.subtract, ["t1" + TB, "t2" + TB], [ok_])
                            tt("pool", v3, x2, C, ALU.mult, ["qn" + TB, "rope"], ["t3" + TB])
                            tt("pool", v4, x1, Sn, ALU.mult, ["qn" + TB, "rope2"], ["t4" + TB])
                            tt("pool", o3[:, :, 32:64], v3, v4, ALU.add, ["t3" + TB, "t4" + TB], [ok_])
                        else:
                            if tl["kind"] == "samp":
                                v_ = vt[tb]; vk_ = "vt%d" % tb
                                cp("act", v_[0:4, :], pb[0:4, :], [pk], [vk_])
                                dma("sp", vso[2044:2048, 512 * j:512 * j + 512], v_[0:4, :], reads=[vk_])
                                cp("dve", Vsm[0:4, 512 * j:512 * j + 512], pb[0:4, :], [pk], ["Vsm"])
                            else:
                                if "own" in tl:
                                    v_ = vt[tb]; vk_ = "vt%d" % tb
                                    cp("act", v_[:, :], pb[:, :], [pk], [vk_])
                                    dma("sp", vp[tl["own"] * 128:(tl["own"] + 1) * 128, 512 * j:512 * j + 512], v_[:, :], reads=[vk_])
                                b_ = vb[tb]; bk_ = "vb%d" % tb
                                b4 = b_[:, :].rearrange("p (h s c) -> p h s c", h=4, s=3, c=64)
                                ts("dve", b4[:, :, 0:3:2, :], pb[:, :].rearrange("p (h s c) -> p h s c", h=4, s=2, c=64), vcol, None, ALU.mult, None, [pk, "cst"], [bk_])
                                cp("pool", b4[:, :, 1, :], vcol.unsqueeze(2).to_broadcast([128, 4, 64]), ["cst"], [bk_])
                                dma("sp", V_s[t * 128:(t + 1) * 128, 768 * j:768 * j + 768], b_[:, :], reads=[bk_], writes=["V_s"])

                    def stC1():
                        if kind == "glu":
                            cp("act", gb[0:P, :], g[0:P, :], [gk], ["gb" + TB])
                            for i2 in range(2):
                                tr(ptb[:, i2 * 128:i2 * 128 + P], gb[0:P, i2 * 128:(i2 + 1) * 128], identb[0:P, 0:P], ["gb" + TB, "identb"], [ptk])
                        else:
                            o3 = o[0:P, :].rearrange("p (h d) -> p h d", h=8)
                            tt("dve", o3, o3, rs8.unsqueeze(2).to_broadcast([P, 8, 64]), ALU.mult, [ok_, RK], [ok_])
                            if kind == "k":
                                if tl["kind"] == "samp":
                                    dma("sp", kso[2044:2048, 512 * j:512 * j + 512], o[0:4, :], reads=[ok_])
                                elif "own" in tl:
                                    dma("sp", kp[tl["own"] * 128:(tl["own"] + 1) * 128, 512 * j:512 * j + 512], o[:, :], reads=[ok_])
                            cp("act", kb[0:P, :], o[0:P, :], [ok_], ["kb" + TB])
                            for i2 in range(4):
                                tr(ptb[:, i2 * 128:i2 * 128 + P], kb[0:P, i2 * 128:(i2 + 1) * 128], identb[0:P, 0:P], ["kb" + TB, "identb"], [ptk])

                    def stC2():
                        if kind == "glu":
                            cc = (30 + col0) if tl["kind"] != "samp" else 1212
                            cp("dve", cinT[:, 2 * j:2 * j + 2, cc:cc + P], ptb[:, 0:256].rearrange("p (c t) -> p c t", c=2)[:, :, 0:P], [ptk], ["cinT"])
                        else:
                            src3 = ptb[:, 0:512].rearrange("p (c t) -> p c t", c=4)[:, :, 0:P]
                            if kind == "q":
                                cp("dve", QT[:, 4 * j:4 * j + 4, col0:col0 + P], src3, [ptk], ["QT"])
                            elif tl["kind"] == "samp":
                                cp("dve", KTs[:, 4 * j:4 * j + 4, 0:4], src3, [ptk], ["KTs"])
                            else:
                                kt_ = ktp[tb]; kk = "ktp%d" % tb
                                cp("dve", kt_[:, :, :], src3, [ptk], [kk])
                                dma("sp", KT_s[4 * j:4 * j + 4, :, t * 128:(t + 1) * 128].rearrange("c p n -> p c n"), kt_[:, :, :], reads=[kk], writes=["KT_s"])

                    return [stA, stB, stC1 if kind != "v" else None, stC2 if kind != "v" else None]

                issue_load(0)
                if pname == "a":
                    for q8 in range(4):
                        dma("pool", kso[511 * q8:511 * q8 + 511, :], ck[4 + 511 * q8:4 + 511 * q8 + 511, :])
                        dma("pool", vso[511 * q8:511 * q8 + 511, :], cv[4 + 511 * q8:4 + 511 * q8 + 511, :])
                items = []
                for bi, blk in enumerate(blocks):
                    first = True
                    pos = 0
                    for i, tl in enumerate(tiles):
                        if blk[0] in ("glu", "q") and tl["kind"] == "halo":
                            continue
                        items.append(mk_item(bi, blk, i, tl, len(items), first, pos))
                        first = False
                        pos += 1
                run_merged(nsteps, pipe_steps(items, [0, 1, 3, 4], rev=False), 3)
                S.emit(nc, esem, dsem)
                chk(1 if pname == "a" else 15)

        halo_tiles = [dict(kind="halo", src=xh[t * 128:(t + 1) * 128, :], P=128, t=t, col0=t * 128) for t in range(16)]
        phase1(halo_tiles, [("k", 0), ("k", 1), ("v", 0), ("v", 1)], 2048, "a")
        with ExitStack() as stA:
            QT = T(stA, "QT", [128, 8, 1156], BF16)
            aT = T(stA, "aT", [128, 8, 1156], BF16)
            KTs = T(stA, "KTs", [128, 8, 4], BF16)
            Vsm = T(stA, "Vsm", [4, 1024], BF16)
            with ExitStack() as stB:
                cinT = T(stB, "cinT", [128, 8, 1216], BF16)
                memset("pool", cinT[:, :, 0:30], 0.0, ["cinT"])

                main_tiles = []
                for m in range(9):
                    d = dict(kind="main", src=xm[m * 128:(m + 1) * 128, :], P=128, t=16 + m, col0=m * 128)
                    if m >= 1:
                        d["own"] = m - 1
                    main_tiles.append(d)
                main_tiles.append(dict(kind="samp", src=xs, P=4, t=25, col0=1152))
                blocks_b = [("glu", 0), ("glu", 1), ("glu", 2), ("glu", 3), ("q", 0), ("q", 1), ("k", 0), ("k", 1), ("v", 0), ("v", 1)]
                import os as _os
                if _os.environ.get("DBG_KINDS"):
                    blocks_b = [b_ for b_ in blocks_b if b_[0] in _os.environ["DBG_KINDS"].split(",")]
                if _os.environ.get("DBG_NOSAMP"):
                    main_tiles = main_tiles[:-1]
                phase1(main_tiles, blocks_b, 1156, "b")

                with ExitStack() as st:
                    stt_ = T(st, "stt_", [30, 1024], F32)
                    gcs = T(st, "gcs", [128, 8, 1186], F32)
                    sqc = [T(st, "sqc%d" % i, [128, 512], F32) for i in range(2)]
                    mean = T(st, "mean", [128, 512], F32); var = T(st, "var", [128, 512], F32); m2 = T(st, "m2", [128, 512], F32)
                    tln = [T(st, "tln%d" % i, [128, 512], F32) for i in range(2)]
                    dma("sp", stt_[:, :], sta, writes=["stt_"])
                    for ch in range(8):
                        tr(ps[0][:, ch * 32:ch * 32 + 30], stt_[0:30, ch * 128:(ch + 1) * 128], identf[0:30, 0:30], ["stt_", "identf"], ["ps0"])
                    cp("dve", cinT[:, :, 1182:1212], ps[0][:, 0:256].rearrange("p (c t) -> p c t", c=8)[:, :, 0:30], ["ps0"], ["cinT"])
                    wb2 = [T(st, "wc%d" % i, [128, 16, 512], BF16) for i in range(2)]
                    tmp2 = [T(st, "tmc%d" % i, [2, 512], F32) for i in range(2)]
                    gtl2 = [T(st, "gtc%d" % i, [128, 512], F32) for i in range(2)]
                    badp2 = [T(st, "bdc%d" % i, [2, 512], F32) for i in range(4)]
                    cnt2 = [0]
                    wst2 = [T(st, "wsc%d" % i, [128, 4, 512], F32) for i in range(2)]
                    stg2 = [ada_stages(cb, wb2, tmp2, gtl2, badp2, (2, 3), 4, 5, 64, cnt2, wst2) for cb in range(8, 24)]
                    stg2[0][0](); stg2[1][0]()
                    NPE = 21
                    dg = [T(st, "dg%d" % i, [128, NPE, 128], BF16) for i in range(2)]
                    for ch in range(8):
                        acc = gcs[:, ch, :]; ak = ("gcs", ch)
                        d_ = dg[ch % 2]; dk = "dg%d" % (ch % 2)
                        tt("pool", d_[:, :, :], identb[:, :].unsqueeze(1).to_broadcast([128, NPE, 128]),
                           cst[:, CAW + ch * 31:CAW + ch * 31 + NPE].unsqueeze(2).to_broadcast([128, NPE, 128]), ALU.mult, ["identb", "cst"], [dk])
                        pbs = []
                        cbanks = [(ps[0], "ps0"), (ps[1], "ps1"), (pt[0][:, :].bitcast(F32), "pt0")]
                        for gix, (a0, n) in enumerate(((0, 512), (512, 512), (1024, 162))):
                            pb, pk = cbanks[gix]
                            for k in range(NPE):
                                mm(pb[:, 0:n], d_[:, k, :], cinT[:, ch, k + a0:k + a0 + n], k == 0, k == NPE - 1, [dk, "cinT"], [pk])
                            pbs.append((pb, pk, a0, n))
                        ts("dve", acc, cinT[:, ch, NPE:NPE + 1186], cst[:, CAW + ch * 31 + NPE:CAW + ch * 31 + NPE + 1], cst[:, CAB + ch:CAB + ch + 1], ALU.mult, ALU.add, ["cinT", "cst"], [ak])
                        for k in range(NPE + 1, 31):
                            stt(acc, cinT[:, ch, k:k + 1186], cst[:, CAW + ch * 31 + k:CAW + ch * 31 + k + 1], acc, ALU.mult, ALU.add, ["cinT", "cst", ak], [ak])
                        for (pb, pk, a0, n) in pbs:
                            tt("dve", gcs[:, ch, a0:a0 + n], pb[:, 0:n], gcs[:, ch, a0:a0 + n], ALU.add, [pk, ak], [ak])
                        for i_ in (2 * ch, 2 * ch + 1):
                            stg2[i_][1]()
                            if i_ >= 1:
                                stg2[i_ - 1][2]()
                            if i_ + 2 < 16:
                                stg2[i_ + 2][0]()
                    stg2[15][2]()
                    cp("dve", modT[:, 32:96, :], ps[5][:, 0:128].rearrange("p (k t) -> p k t", t=2), ["ps5"], ["modT"])
                    for grp in range(2):
                        ts("dve", A2T[:, :, grp], modT[:, 64:80, grp], 1.0, None, ALU.add, None, ["modT"], ["A2T"])
                        tt("dve", A2T[:, :, grp], A2T[:, :, grp], gnT[:, :, 1], ALU.mult, ["A2T", "gnT"], ["A2T"])
                    for (a0, n, m0) in ((0, 512, 0), (512, 512, 512), (1024, 128, 1024), (1182, 4, 1152)):
                        for ch in range(8):
                            mm(ps[0][:, 0:n], onesf[:, :], gcs[:, ch, a0:a0 + n], ch == 0, ch == 7, ["onesf", ("gcs", ch)], ["ps0"])
                        for ch in range(8):
                            sq_ = sqc[ch % 2]; sk_ = "sqc%d" % (ch % 2)
                            act(sq_[:, 0:n], gcs[:, ch, a0:a0 + n], AF.Square, [("gcs", ch)], [sk_])
                            mm(ps[1][:, 0:n], onesf[:, :], sq_[:, 0:n], ch == 0, ch == 7, ["onesf", sk_], ["ps1"])
                        ts("dve", mean[:, 0:n], ps[0][:, 0:n], 1.0 / 1024, None, ALU.mult, None, ["ps0"], ["mean"])
                        tt("dve", m2[:, 0:n], mean[:, 0:n], mean[:, 0:n], ALU.mult, ["mean"], ["m2"])
                        stt(var[:, 0:n], ps[1][:, 0:n], 1.0 / 1024, m2[:, 0:n], ALU.mult, ALU.subtract, ["ps1", "m2"], ["var"])
                        ts("dve", var[:, 0:n], var[:, 0:n], EPS, None, ALU.add, None, ["var"], ["var"])
                        act(var[:, 0:n], var[:, 0:n], AF.Sqrt, ["var"], ["var"])
                        S.op("dve", lambda e, n=n: e.reciprocal(out=var[:, 0:n], in_=var[:, 0:n]), reads=["var"], writes=["var"])
                        for ch in range(8):
                            tl_ = tln[ch % 2]; tk_ = "tln%d" % (ch % 2)
                            tt("dve", tl_[:, 0:n], gcs[:, ch, a0:a0 + n], mean[:, 0:n], ALU.subtract, [("gcs", ch), "mean"], [tk_])
                            tt("pool", tl_[:, 0:n], tl_[:, 0:n], var[:, 0:n], ALU.mult, [tk_, "var"], [tk_])
                            act(aT[:, ch, m0:m0 + n], tl_[:, 0:n], AF.Silu, [tk_, "cst"], ["mixT"], scale=cst[:, LNG + ch:LNG + ch + 1], bias=cst[:, LNB + ch:LNB + ch + 1])
                    S.emit(nc, esem, dsem)
                    chk(2)

            oT = T(stA, "oT", [128, 8, 1156], BF16)
            with ExitStack() as st:
                KTh = [T(st, "KTh%d" % i, [128, 3200], BF16) for i in range(2)]
                V1 = [T(st, "V1_%d" % i, [128, 10, 192], BF16) for i in range(2)]
                V4 = [T(st, "V4_%d" % i, [128, 4, 4, 192], BF16) for i in range(2)]
                V16 = [T(st, "V16_%d" % i, [128, 2, 16, 192], BF16) for i in range(2)]
                QZ = [T(st, "QZ%d" % i, [128, 2, 1152], BF16) for i in range(2)]
                pT = [T(st, "pT%d" % i, [128, 512], BF16) for i in range(4)]
                rD = T(st, "rD", [128, 512], F32)
                Kc = T(st, "Kc", [128, 9, 1024], BF16); Vc = T(st, "Vc", [128, 9, 1024], BF16)
                KcT = T(st, "KcT", [128, 8, 9, 128], BF16)
                pN = T(st, "pN", [4, 64], BF16)

                def load_hp(hp_):
                    bs = hp_ % 2; key = "hp%d" % bs; c0 = hp_ * 192
                    dma("sp", KTh[bs][:, :], KT_s[hp_, :, :], reads=["KT_s"], writes=[key])
                    dma("sp", V1[bs][:, :, :], V_s[15 * 128:25 * 128, c0:c0 + 192].rearrange("(s p) f -> p s f", p=128), reads=["V_s"], writes=[key])
                    for G in range(3, 6):
                        dma("sp", V4[bs][:, G - 3, :, :], V_s[512 * G:512 * G + 512, c0:c0 + 192].rearrange("(m r) f -> m r f", r=4), reads=["V_s"], writes=[key])
                    dma("sp", V4[bs][0:32, 3, :, :], V_s[3072:3200, c0:c0 + 192].rearrange("(m r) f -> m r f", r=4), reads=["V_s"], writes=[key])
                    dma("sp", V16[bs][:, 0, :, :], V_s[0:2048, c0:c0 + 192].rearrange("(m r) f -> m r f", r=16), reads=["V_s"], writes=[key])
                    dma("sp", V16[bs][0:72, 1, :, :], V_s[2048:3200, c0:c0 + 192].rearrange("(m r) f -> m r f", r=16), reads=["V_s"], writes=[key])
                    cp("pool", QZ[bs][0:64, 0, :], QT[0:64, hp_, 0:1152], ["QT"], ["qz%d" % bs])
                    cp("pool", QZ[bs][64:128, 1, :], QT[64:128, hp_, 0:1152], ["QT"], ["qz%d" % bs])

                for bs in range(2):
                    memset("pool", QZ[bs][64:128, 0, :], 0.0, ["qz%d" % bs])
                    memset("pool", QZ[bs][0:64, 1, :], 0.0, ["qz%d" % bs])
                for (dst, srcd, key) in ((Kc, ck, "Kc"), (Vc, cv, "Vc")):
                    dma("pool", dst[:, 0, :], srcd[1920:2048, :], writes=[key])
                    dma("pool", dst[:, 1:5, :], srcd[1536:2048, :].rearrange("(m t) f -> m t f", t=4), writes=[key])
                    dma("pool", dst[:, 5:9, :], srcd.rearrange("(m t) f -> m t f", t=16)[:, 0:4, :], writes=[key])
                load_hp(0)
                units = []
                for hp in range(8):
                    bsel = hp % 2
                    kth = KTh[bsel]; v1 = V1[bsel]; v4 = V4[bsel]; v16 = V16[bsel]; hk = "hp%d" % bsel; qz = QZ[bsel]; qk_ = "qz%d" % bsel
                    for gi in range(3):
                        G = 4 + gi
                        nqt = 512 if gi < 2 else 128
                        qb = 512 * gi
                        batches = []
                        ntq = nqt // 128
                        for kind in ("prev", "cur"):
                            tl_ = []
                            for qt in range(ntq):
                                Tq = 16 + 4 * gi + qt
                                Tk = Tq - 1 if kind == "prev" else Tq
                                tl_.append(dict(kc=slice(Tk * 128, Tk * 128 + 128), qc=slice(qb + 128 * qt, qb + 128 * qt + 128), oc=slice(128 * qt, 128 * qt + 128), V=v1[:, Tk - 15, :]))
                            batches.append(dict(nk=128, nq=128, mask=band[:, 128:256] if kind == "prev" else band[:, 0:128], tiles=tl_))
                        nq2 = nqt // 4
                        for kind in ("prev", "cur"):
                            tl_ = []
                            nk = 128 if (kind == "prev" or gi < 2) else 32
                            for r in range(4):
                                Gk = G - 1 if kind == "prev" else G
                                tl_.append(dict(kc=slice(512 * Gk + r, 512 * Gk + 4 * nk, 4), qc=slice(qb + r, qb + nqt, 4), oc=slice(r, nqt, 4), V=v4[0:nk, Gk - 3, r, :]))
                            mk_ = band[0:nk, 128:128 + nq2] if kind == "prev" else band[0:nk, 0:nq2]
                            batches.append(dict(nk=nk, nq=nq2, mask=mk_, tiles=tl_))
                        nq3 = nqt // 16
                        m0 = 32 * gi
                        for kind in ("prev", "cur"):
                            tl_ = []
                            nk = 128 if kind == "prev" else (m0 + nq3)
                            for r in range(16):
                                if kind == "prev":
                                    kc = slice(r, 2048, 16); H = 0
                                else:
                                    kc = slice(2048 + r, 2048 + 16 * nk, 16); H = 1
                                tl_.append(dict(kc=kc, qc=slice(qb + r, qb + nqt, 16), oc=slice(r, nqt, 16), V=v16[0:nk, H, r, :]))
                            mk_ = band[0:nk, 128 + m0:128 + m0 + nq3] if kind == "prev" else band[0:nk, m0:m0 + nq3]
                            batches.append(dict(nk=nk, nq=nq3, mask=mk_, tiles=tl_))
                        ulist = []
                        for bt in batches:
                            per = max(1, 256 // bt["nq"])
                            for c_ in range(0, len(bt["tiles"]), per):
                                ulist.append(dict(nk=bt["nk"], nq=bt["nq"], mask=bt["mask"], tiles=bt["tiles"][c_:c_ + per]))
                        for ui, un in enumerate(ulist):
                            units.append(dict(un=un, first=(ui == 0), last=(ui == len(ulist) - 1), hp=hp, gi=gi, nqt=nqt, qb=qb, kth=kth, hk=hk, qz=qz, qk_=qk_,
                                              pre=(hp + 1 if (gi == 0 and ui == 5 and hp + 1 < 8) else None)))

                def mk3(u, d):
                    un = d["un"]; nk = un["nk"]; nq = un["nq"]; tiles_ = un["tiles"]; ntl = len(tiles_); tot = 2 * ntl * nq
                    sb = ps[u % 4]; sk = "ps%d" % (u % 4); p_ = pT[u % 4]; pk_ = "pT%d" % (u % 4)
                    kth = d["kth"]; hk = d["hk"]; qz = d["qz"]; qk_ = d["qk_"]; hp = d["hp"]; nqt = d["nqt"]; qb = d["qb"]

                    def s0():
                        if d["pre"] is not None:
                            load_hp(d["pre"])
                        for ti, tl in enumerate(tiles_):
                            mm(sb[0:nk, 2 * ti * nq:2 * (ti + 1) * nq].rearrange("p (h q) -> p h q", h=2), kth[:, tl["kc"]], qz[:, :, tl["qc"]], True, True, [hk, qk_], [sk])

                    def s1():
                        act(p_[0:nk, 0:tot], sb[0:nk, 0:tot], AF.Exp, [sk], [pk_], scale=0.125)
                        p3 = p_[0:nk, 0:tot].rearrange("p (t q) -> p t q", q=nq)
                        tt("dve" if u % 2 == 0 else "pool", p3, p3, un["mask"].unsqueeze(1).to_broadcast([nk, 2 * ntl, nq]), ALU.mult, [pk_, "band"], [pk_])

                    def s2():
                        for ti, tl in enumerate(tiles_):
                            st_ = bool(d["first"] and ti == 0)
                            mm(ps[4][:, tl["oc"]], tl["V"][:, 0:128], p_[0:nk, 2 * ti * nq:(2 * ti + 1) * nq], st_, False, [hk, pk_], ["ps4"], skip=True)
                            mm(ps[5][:, tl["oc"]], tl["V"][:, 64:192], p_[0:nk, (2 * ti + 1) * nq:(2 * ti + 2) * nq], st_, False, [hk, pk_], ["ps5"], skip=True)
                        if d["last"]:
                            S.op("dve", lambda e: e.reciprocal(out=rD[64:128, 0:nqt], in_=ps[4][64:128, 0:nqt]), reads=["ps4"], writes=["rDa"])
                            tt("dve", oT[0:64, hp, qb:qb + nqt], ps[4][0:64, 0:nqt], rD[64:128, 0:nqt], ALU.mult, ["ps4", "rDa"], ["mixT"])
                            S.op("dve", lambda e: e.reciprocal(out=rD[0:64, 0:nqt], in_=ps[5][0:64, 0:nqt]), reads=["ps5"], writes=["rDb"])
                            tt("dve", oT[64:128, hp, qb:qb + nqt], ps[5][64:128, 0:nqt], rD[0:64, 0:nqt], ALU.mult, ["ps5", "rDb"], ["mixT"])

                    return [s0, s1, s2]

                pipeline([mk3(u, d) for u, d in enumerate(units)], [0, 2, 4])
                S.emit(nc, esem, dsem)
                chk(3)

                ntr = 0
                for ch in range(8):
                    for half in range(2):
                        pb = pt[ntr % 2]; pk = "pt%d" % (ntr % 2); ntr += 1
                        n_ = 5 if half == 0 else 4
                        for s_ in range(n_):
                            sl = half * 5 + s_
                            tr(pb[:, s_ * 128:(s_ + 1) * 128], Kc[:, sl, ch * 128:(ch + 1) * 128], identb[:, :], ["Kc", "identb"], [pk])
                        cp("dve" if ntr % 2 else "act", KcT[:, ch, half * 5:half * 5 + n_, :], pb[:, 0:n_ * 128].rearrange("p (s k) -> p s k", s=n_), [pk], ["KcT"])
                memset("dve", ps[4][:, :], 0.0, ["ps4"])
                memset("dve", ps[5][:, :], 0.0, ["ps5"])
                for hh in range(2):
                    hr = slice(64 * hh, 64 * hh + 64)
                    sb = ps[2 * hh]; sk = "ps%d" % (2 * hh); sn = ps[2 * hh + 1]; snk = "ps%d" % (2 * hh + 1)
                    p_ = pT[2 * hh]; pk_ = "pT%d" % (2 * hh)
                    for hp in range(8):
                        qs = QT[hr, hp, 1152:1156]
                        mm(sb[:, 12 * hp:12 * hp + 4], KcT[hr, hp, 0, :], qs, True, True, ["KcT", "QT"], [sk])
                        for t in range(4):
                            mm(sb[:, 12 * hp + 4 + t:12 * hp + 5 + t], KcT[hr, hp, 1 + t, :], QT[hr, hp, 1152 + t:1153 + t], True, True, ["KcT", "QT"], [sk])
                            mm(sb[:, 12 * hp + 8 + t:12 * hp + 9 + t], KcT[hr, hp, 5 + t, :], QT[hr, hp, 1152 + t:1153 + t], True, True, ["KcT", "QT"], [sk])
                        mm(sn[0:4, 4 * hp:4 * hp + 4], KTs[hr, hp, 0:4], qs, True, True, ["KTs", "QT"], [snk])
                    act(p_[:, 0:96], sb[:, 0:96], AF.Exp, [sk], [pk_], scale=0.125)
                    tt("dve", p_[:, 0:96], p_[:, 0:96], msk12[:, :], ALU.mult, [pk_, "msk12"], [pk_])
                    act(pN[0:4, 32 * hh:32 * hh + 32], sn[0:4, 0:32], AF.Exp, [snk], ["pN"], scale=0.125)
                    tt("dve", pN[0:4, 32 * hh:32 * hh + 32], pN[0:4, 32 * hh:32 * hh + 32], wm[0:4, :], ALU.mult, ["pN", "wm"], ["pN"])
                    for hp in range(8):
                        vc0 = hp * 128 + 64 * hh
                        oc = slice(4 * hp, 4 * hp + 4)
                        mm(ps[4][hr, oc], Vc[:, 0, vc0:vc0 + 64], p_[:, 12 * hp:12 * hp + 4], False, False, ["Vc", pk_], ["ps4"], skip=True)
                        mm(ps[5][hr, oc], ones64[:, :], p_[:, 12 * hp:12 * hp + 4], False, False, ["ones64", pk_], ["ps5"], skip=True)
                        for t in range(4):
                            for (sl, cc) in ((1 + t, 12 * hp + 4 + t), (5 + t, 12 * hp + 8 + t)):
                                mm(ps[4][hr, 4 * hp + t:4 * hp + t + 1], Vc[:, sl, vc0:vc0 + 64], p_[:, cc:cc + 1], False, False, ["Vc", pk_], ["ps4"], skip=True)
                                mm(ps[5][hr, 4 * hp + t:4 * hp + t + 1], ones64[:, :], p_[:, cc:cc + 1], False, False, ["ones64", pk_], ["ps5"], skip=True)
                        pn = pN[0:4, 32 * hh + 4 * hp:32 * hh + 4 * hp + 4]
                        mm(ps[4][hr, oc], Vsm[0:4, vc0:vc0 + 64], pn, False, False, ["Vsm", "pN"], ["ps4"], skip=True)
                        mm(ps[5][hr, oc], ones64[0:4, :], pn, False, False, ["ones64", "pN"], ["ps5"], skip=True)
                S.op("dve", lambda e: e.reciprocal(out=rD[:, 0:32], in_=ps[5][:, 0:32]), reads=["ps5"], writes=["rD"])
                tt("dve", oT[:, :, 1152:1156], ps[4][:, 0:32].rearrange("p (h t) -> p h t", h=8), rD[:, 0:32].rearrange("p (h t) -> p h t", h=8), ALU.mult, ["ps4", "rD"], ["mixT"])
                S.emit(nc, esem, dsem)
                chk(4)

            with ExitStack() as st:
                wb = [T(st, "wo%d" % i, [128, 16, 512], BF16) for i in range(2)]
                G1 = [T(st, "G1_%d" % i, [128, 2048], F32) for i in range(2)]
                xin = [T(st, "xin%d" % i, [128, 512], F32) for i in range(3)]
                tq = [T(st, "tq%d" % i, [128, 512], F32) for i in range(2)]
                dma("sp", G1[0][:, :], GS[0, :, :], reads=["GS"], writes=["G1"])
                dma("sp", G1[1][:, :], GS[1, :, :], reads=["GS"], writes=["G1"])
                tiles4 = [(xm[m * 128:(m + 1) * 128, :], 128, m * 128, m * 128, 0) for m in range(9)] + [(xs, 4, 1152, 1280, 1)]
                nb = 0
                def ld4(cb_):
                    load_w(wb[cb_ % 2], "wo%d" % (cb_ % 2), lambda r0, r1, c0_, c1_: w_out[r0:r1, c0_:c1_], [(0, 512, 512 * cb_)])
                ld4(0)
                seq4 = [(cb_, tl_) for cb_ in range(4) for tl_ in tiles4]

                def ldx(n_):
                    cb_, (src_, P_, _c, _r, _g) = seq4[n_]
                    dma("sp", xin[n_ % 3][0:P_, :], src_[:, 512 * cb_:512 * cb_ + 512], writes=["xin%d" % (n_ % 3)])
                ldx(0); ldx(1)
                for cb in range(4):
                    w = wb[cb % 2]; wk = "wo%d" % (cb % 2)
                    if cb + 1 < 4:
                        ld4(cb + 1)
                    for (src, P, col0, row0, grp) in tiles4:
                        pb = ps[nb % 4]; pk = "ps%d" % (nb % 4)
                        xi = xin[nb % 3]; xk = "xin%d" % (nb % 3); tq_ = tq[nb % 2]; tk_ = "tq%d" % (nb % 2)
                        if nb + 2 < len(seq4):
                            ldx(nb + 2)
                        nb += 1
                        for k in range(16):
                            mm(pb[0:P, :], (aT if k < 8 else oT)[:, k % 8, col0:col0 + P], w[:, k, :], k == 0, k == 15, ["mixT", wk], [pk])
                        tt("dve", tq_[0:P, :], pb[0:P, :], G1[grp][0:P, 512 * cb:512 * cb + 512], ALU.mult, [pk, "G1"], [tk_])
                        tt("pool", tq_[0:P, :], tq_[0:P, :], xi[0:P, :], ALU.add, [tk_, xk], [tk_])
                        dma("pool", XM_s[row0:row0 + P, 512 * cb:512 * cb + 512], tq_[0:P, :], reads=[tk_], writes=["XM_s"])
                S.emit(nc, esem, dsem)
                chk(5)

        with ExitStack() as stF:
            actT = T(stF, "actT", [128, 44, 1030], BF16)
            with ExitStack() as stH:
                h2T = T(stH, "h2T", [128, 16, 1030], BF16)
                with ExitStack() as st:
                    xts = [T(st, "xu%d" % i, [128, 2048], F32) for i in range(2)]
                    sqts = [T(st, "squ%d" % i, [128, 2048], BF16) for i in range(2)]; xbs = [T(st, "xbu%d" % i, [128, 2048], BF16) for i in range(2)]
                    specs = [dict(src=XM_s[0:128, :], P=128, grp=0, col0=0, hkey="h2T", c0=126, n=2, src_reads=["XM_s"])]
                    for m in range(1, 9):
                        specs.append(dict(src=XM_s[m * 128:(m + 1) * 128, :], P=128, grp=0, col0=2 + 128 * (m - 1), hkey="h2T", src_reads=["XM_s"]))
                    specs.append(dict(src=XM_s[1280:1284, :], P=4, grp=1, col0=1026, hkey="h2T", src_reads=["XM_s"]))
                    norm_loop((xts, sqts, xbs), specs, A2T, 48, h2T)
                    S.emit(nc, esem, dsem)
                    chk(6)
                with ExitStack() as st:
                    wb = [T(st, "wu%d" % i, [128, 16, 512], BF16) for i in range(2)]
                    upg = [T(st, "upg%d" % i, [128, 1032], F32) for i in range(2)]
                    upv = [T(st, "upv%d" % i, [128, 1032], F32) for i in range(2)]
                    ug_l = [T(st, "ug%d" % i, [128, 1030], F32) for i in range(2)]; uv_l = [T(st, "uv%d" % i, [128, 1030], F32) for i in range(2)]
                    upsel = T(st, "upsel", [128, 88, 4], F32)
                    fco = [T(st, "fco%d" % i, [4, 512], F32) for i in range(1)]
                    dma("sp", cas[0:26, :], sta[4:30, :])
                    groups = ((0, 512), (512, 512), (1024, 6))

                    wus = [T(st, "wus%d" % i, [128, 2, 512], F32) for i in range(2)]
                    nld5 = [0]

                    def ld5_dma(jb_, k2):
                        sb_ = wus[k2 % 2]; sk_ = "wus%d" % (k2 % 2)
                        for (d0, s0_) in ((0, 256 * jb_), (256, 5632 + 256 * jb_)):
                            dma("sp", sb_[:, :, d0:d0 + 256], w_up[256 * k2:256 * k2 + 256, s0_:s0_ + 256].rearrange("(k p) n -> p k n", p=128), writes=[sk_])

                    def ld5_cast(jb_, k2):
                        sb_ = wus[k2 % 2]; sk_ = "wus%d" % (k2 % 2)
                        cp("act", wb[jb_ % 2][:, 2 * k2:2 * k2 + 2, :], sb_[:, :, :], [sk_], ["wu%d" % (jb_ % 2)])

                    def ld5(jb_):
                        for k2 in range(8):
                            ld5_dma(jb_, k2)
                            ld5_cast(jb_, k2)

                    def mk5(u, jb, sub, isv):
                        w = wb[jb % 2]; wk = "wu%d" % (jb % 2)
                        up = (upv if isv else upg)[sub]; uk = ("upv" if isv else "upg") + str(sub)
                        ch = (44 if isv else 0) + 2 * jb + sub
                        wc0 = isv * 256 + sub * 128
                        banks = [(3 * u + g_) % 6 for g_ in range(3)]
                        ucv = uv_l[sub] if isv else ug_l[sub]; k_ = ("uv%d" if isv else "ug%d") % sub

                        q_ = 2 * sub + isv

                        def s0():
                            if jb + 1 < 22:
                                ld5_dma(jb + 1, 2 * q_)
                                ld5_dma(jb + 1, 2 * q_ + 1)
                            for gi_, (g0, gn_) in enumerate(groups):
                                pb = ps[banks[gi_]]; pk = "ps%d" % banks[gi_]
                                for k in range(16):
                                    mm(pb[:, 0:gn_], w[:, k, wc0:wc0 + 128], h2T[:, k, g0:g0 + gn_], k == 0, k == 15, [wk, "h2T"], [pk])

                        def s1():
                            if jb + 1 < 22:
                                ld5_cast(jb + 1, 2 * q_)
                                ld5_cast(jb + 1, 2 * q_ + 1)
                            for gi_, (g0, gn_) in enumerate(groups):
                                pb = ps[banks[gi_]]; pk = "ps%d" % banks[gi_]
                                if gn_ == 512:
                                    cp("act", up[:, g0:g0 + 512], pb[:, 0:512], [pk], [uk])
                                else:
                                    cp("act", up[:, 1024:1026], pb[:, 0:2], [pk], [uk])
                                    cp("act", up[:, 1028:1032], pb[:, 2:6], [pk], [uk])
                            tt("pool", up[:, 0:2], up[:, 0:2], cst[:, VROW:VROW + 2], ALU.mult, [uk, "cst"], [uk])
                            cp("pool", up[:, 1026:1028], stfT[:, ch, :], ["stfT"], [uk])
                            cp("pool", upsel[:, ch, :].rearrange("p (a b) -> p a b", a=2), up[:, 1024:1032].rearrange("p (a b) -> p a b", b=2)[:, 0:4:3, :], [uk], ["upsel"])
                            fw = FCW + ch * 3
                            ts("dve", ucv[:, :], up[:, 0:1030], cst[:, fw:fw + 1], cst[:, FCB + ch:FCB + ch + 1], ALU.mult, ALU.add, [uk, "cst"], [k_])
                            stt(ucv[:, :], up[:, 1:1031], cst[:, fw + 1:fw + 2], ucv[:, :], ALU.mult, ALU.add, [uk, "cst", k_], [k_])
                            stt(ucv[:, :], up[:, 2:1032], cst[:, fw + 2:fw + 3], ucv[:, :], ALU.mult, ALU.add, [uk, "cst", k_], [k_])

                        def s2():
                            act(ug_l[sub][:, :], ug_l[sub][:, :], AF.Silu, ["ug%d" % sub], ["ug%d" % sub])
                            tt("pool", actT[:, 2 * jb + sub, :], ug_l[sub][:, :], uv_l[sub][:, :], ALU.mult, ["ug%d" % sub, "uv%d" % sub], ["actT"])

                        return [s0, s1, s2 if isv else None]

                    ld5(0)
                    items5 = []
                    for jb in range(22):
                        for sub in range(2):
                            for isv in range(2):
                                items5.append(mk5(len(items5), jb, sub, isv))
                    pipeline(items5, [0, 1, 2])
                    for r4 in range(22):
                        pb = ps[4 + r4 % 2]; pk = "ps%d" % (4 + r4 % 2)
                        for i4 in range(4):
                            ch = r4 * 4 + i4
                            tr(pb[0:4, i4 * 128:(i4 + 1) * 128], upsel[:, ch, :], identf[:, :], ["upsel", "identf"], [pk])
                        f_ = fco[0]; fk = "fco0"
                        cp("act", f_[:, :], pb[0:4, :], [pk], [fk])
                        dma("sp", fcp[:, r4 * 512:(r4 + 1) * 512], f_[0:2, :], reads=[fk])
                        dma("sp", fcs[:, r4 * 512:(r4 + 1) * 512], f_[2:4, :], reads=[fk])
                    S.emit(nc, esem, dsem)
                    chk(7)
            with ExitStack() as st:
                wd = [T(st, "wd%d" % i, [128, 44, 256], BF16) for i in range(2)]
                G2 = [T(st, "G2_%d" % i, [128, 2048], F32) for i in range(2)]
                xin = [T(st, "xmi%d" % i, [128, 256], F32) for i in range(2)]
                tq = [T(st, "ty%d" % i, [128, 256], F32) for i in range(2)]
                dma("sp", G2[0][:, :], GS[2, :, :], reads=["GS"], writes=["G2"])
                dma("sp", G2[1][:, :], GS[3, :, :], reads=["GS"], writes=["G2"])
                tiles6 = [(128, 128 * m, 128 * (m + 1), yp[m * 128:(m + 1) * 128, :], 0) for m in range(8)] + [(4, 1026, 1280, yso, 1)]
                nb = 0
                wds = [T(st, "wds%d" % i, [128, 11, 256], F32) for i in range(2)]
                nld6 = [0]

                def ld6(cb_):
                    for q4 in range(4):
                        sb_ = wds[nld6[0] % 2]; sk_ = "wds%d" % (nld6[0] % 2); nld6[0] += 1
                        dma("act", sb_[:, :, :], w_down[1408 * q4:1408 * q4 + 1408, 256 * cb_:256 * cb_ + 256].rearrange("(k p) n -> p k n", p=128), writes=[sk_])
                        cp("act", wd[cb_ % 2][:, 11 * q4:11 * q4 + 11, :], sb_[:, :, :], [sk_], ["wd%d" % (cb_ % 2)])
                ld6(0)
                for cb in range(8):
                    w = wd[cb % 2]; wk = "wd%d" % (cb % 2)
                    if cb + 1 < 8:
                        ld6(cb + 1)
                    for (P, acol, xrow, ydst, grp) in tiles6:
                        pb = ps[nb % 4]; pk = "ps%d" % (nb % 4)
                        xi = xin[nb % 2]; xk = "xmi%d" % (nb % 2); tq_ = tq[nb % 2]; tk_ = "ty%d" % (nb % 2); nb += 1
                        dma("sp", xi[0:P, :], XM_s[xrow:xrow + P, 256 * cb:256 * cb + 256], reads=["XM_s"], writes=[xk])
                        for k in range(44):
                            mm(pb[0:P, 0:256], actT[:, k, acol:acol + P], w[:, k, :], k == 0, k == 43, ["actT", wk], [pk])
                        tt("dve", tq_[0:P, :], pb[0:P, 0:256], G2[grp][0:P, 256 * cb:256 * cb + 256], ALU.mult, [pk, "G2"], [tk_])
                        tt("pool", tq_[0:P, :], tq_[0:P, :], xi[0:P, :], ALU.add, [tk_, xk], [tk_])
                        dma("sp", ydst[:, 256 * cb:256 * cb + 256], tq_[0:P, :], reads=[tk_])
                S.finish()
                S.emit(nc, esem, dsem)
    return nc


_NC = None
STOP_AFTER = None
DEBUG_CORES = None


class _Stop(Exception):
    pass


def _host_consts():
    half = 32
    inv = (np.float32(10000.0) ** (-(np.arange(half, dtype=np.float32) / np.float32(half)))).astype(np.float32)
    band = np.zeros((128, 256), np.float32)
    ki = np.arange(128)[:, None]; c = np.arange(256)[None, :]
    band[(c >= ki) & (c <= ki + 128)] = 1.0
    m12 = np.ones((128, 12), np.float32)
    m12[:, 0:4] = (np.arange(128)[:, None] >= np.arange(4)[None, :]).astype(np.float32)
    msk12 = np.tile(m12, (1, 8))
    tq = np.arange(4)
    wmat = (tq[:, None] <= tq[None, :]).astype(np.float32) + 2.0 * (tq[:, None] == tq[None, :]).astype(np.float32)
    wm = np.tile(wmat, (1, 8))
    sel = np.zeros((2, 256), np.float32); sel[0, 0:128] = 1.0; sel[1, 128:256] = 1.0
    return inv, band.astype(ml_dtypes.bfloat16), msk12.astype(ml_dtypes.bfloat16), wm.astype(ml_dtypes.bfloat16), sel


def kernel(x_prompt, x_sample, cache_win_k, cache_win_v, state_conv_a, state_ffn_conv, c_prompt, c_sample,
           norm_mix_g, norm_ffn_g, w_ada, b_ada, w_in, conv_a_w, conv_a_b, ln_a_g, ln_a_b, q_norm_g, k_norm_g,
           w_out, w_up, ffn_conv_w, ffn_conv_b, w_down):
    global _NC
    f = lambda a: np.ascontiguousarray(np.asarray(a, dtype=np.float32))
    x_prompt = f(x_prompt); x_sample = f(x_sample)
    inv, band, msk12, wm, sel = _host_consts()
    if _NC is None:
        _NC = build_program()
    nc = _NC
    caw = f(conv_a_w)[0]; fcw = f(ffn_conv_w)[0]
    shared = dict(
        gn=np.stack([f(norm_mix_g)[0], f(norm_ffn_g)[0]]), bada=np.stack([f(b_ada)[0], f(b_ada)[0]]),
        band=band, msk12=msk12, wm=wm, sel=sel,
        identb=np.eye(128, dtype=np.float32).astype(ml_dtypes.bfloat16), identf=np.eye(128, dtype=np.float32),
        w_ada=f(w_ada)[0], w_in=f(w_in)[0], w_out=f(w_out)[0], w_up=f(w_up)[0], w_down=f(w_down)[0])
    cst0 = np.zeros((128, NCST), np.float32)
    cst0[:, CAW:CAW + 248] = caw.T.reshape(8, 128, 31).transpose(1, 0, 2).reshape(128, 248)
    cst0[:, CAB:CAB + 8] = f(conv_a_b)[0].reshape(8, 128).T
    cst0[:, LNG:LNG + 8] = f(ln_a_g)[0].reshape(8, 128).T
    cst0[:, LNB:LNB + 8] = f(ln_a_b)[0].reshape(8, 128).T
    cst0[:, FCW:FCW + 264] = fcw.T.reshape(88, 128, 3).transpose(1, 0, 2).reshape(128, 264)
    cst0[:, FCB:FCB + 88] = f(ffn_conv_b)[0].reshape(88, 128).T
    cst0[:, QG:QG + 64] = f(q_norm_g)[0][None, :]
    cst0[:, KG:KG + 64] = f(k_norm_g)[0][None, :]
    cst0[:, NHALF] = -0.5
    in_maps = []
    for i in range(8):
        b, c = i // 4, i % 4
        s = 1024 * c
        P0 = s - 128 - 2048
        pos = P0 + np.arange(3200)
        xall = np.zeros((3200, 2048), np.float32)
        ok = pos >= 0
        xall[ok] = x_prompt[b, pos[ok]]
        cst = cst0.copy()
        val = np.ones((26, 128), np.float32)
        val[0:25] = np.where(ok.reshape(25, 128), np.float32(1.0), np.float32(1e-30))
        cst[:, VAL:VAL + 26] = val.T
        cst[:, VROW:VROW + 2] = 1.0 if c > 0 else 0.0
        posf = np.concatenate([np.maximum(pos, 0), 16384 + np.arange(4), np.zeros(124, np.int64)]).astype(np.float32)
        ang = posf[:, None] * inv[None, :]
        ropec = np.cos(ang).astype(np.float32).reshape(26, 128, 32).transpose(1, 0, 2)
        ropes = np.sin(ang).astype(np.float32).reshape(26, 128, 32).transpose(1, 0, 2)
        m = dict(shared)
        m.update(xh=np.ascontiguousarray(xall[0:2048]), xm=np.ascontiguousarray(xall[2048:3200]), xs=x_sample[i],
                 cvec=np.stack([f(c_prompt)[b], f(c_sample)[i]]), cst=cst, ropec=np.ascontiguousarray(ropec), ropes=np.ascontiguousarray(ropes),
                 ck=f(cache_win_k)[0, i].reshape(2048, 1024), cv=f(cache_win_v)[0, i].reshape(2048, 1024),
                 sta=f(state_conv_a)[0, i], stf=f(state_ffn_conv)[0, i])
        in_maps.append(m)
    if DEBUG_CORES is not None:
        res = run_bass_kernel_spmd(nc, [in_maps[c] for c in DEBUG_CORES], core_ids=list(range(len(DEBUG_CORES))))
        return res.results, in_maps
    res = run_bass_kernel_spmd(nc, in_maps, core_ids=list(range(8)))
    R = res.results
    y_p = np.zeros((2, 4096, 2048), np.float32); y_s = np.zeros((8, 4, 2048), np.float32)
    wkp = np.zeros((1, 2, 2048, 16, 64), np.float32); wvp = np.zeros_like(wkp)
    cap = np.zeros((1, 2, 30, 1024), np.float32); fcp = np.zeros((1, 2, 2, 11264), np.float32)
    wks = np.zeros((1, 8, 2048, 16, 64), np.float32); wvs = np.zeros_like(wks)
    cas = np.zeros((1, 8, 30, 1024), np.float32); fcs = np.zeros((1, 8, 2, 11264), np.float32)
    for i in range(8):
        b, c = i // 4, i % 4
        r = R[i]
        y_p[b, 1024 * c:1024 * c + 1024] = r["yp"]
        y_s[i] = r["yso"]
        if c >= 2:
            wkp[0, b, 1024 * (c - 2):1024 * (c - 1)] = r["kp"].reshape(1024, 16, 64)
            wvp[0, b, 1024 * (c - 2):1024 * (c - 1)] = r["vp"].reshape(1024, 16, 64)
        if c == 3:
            cap[0, b] = r["cap"]; fcp[0, b] = r["fcp"]
        wks[0, i] = r["kso"].reshape(2048, 16, 64); wvs[0, i] = r["vso"].reshape(2048, 16, 64)
        cas[0, i] = r["cas"]; fcs[0, i] = r["fcs"]
    return (y_p, y_s, wkp, wvp, cap, fcp, wks, wvs, cas, fcs)
```
